# Optimizing a Trainium2 kernel written in Bass

```python
import jax, jax.numpy as jnp
from jax import lax
import numpy as np

D_MODEL = 1024
BATCH = 4
SEQ = 8192
DEPTH = 1

GRID_W = 64
CTX_LEN = 256
A_WIDTH = 1024
A_HEAD_DIM = 128
A_HEADS = A_WIDTH // A_HEAD_DIM
A_CHUNK = 64
B_WIDTH = 1024
B_BLOCKS = 8
B_BLOCK_DIM = B_WIDTH // B_BLOCKS
B_CONV = 4
RG_C = 8.0
N_BRANCH = 2
IN_COLS = 5 * A_WIDTH + 2 * B_WIDTH + N_BRANCH * D_MODEL
DEEPNORM_ALPHA = (2 * DEPTH) ** 0.25
DEEPNORM_BETA = (8 * DEPTH) ** -0.25
LN_EPS = 1e-5
RMS_EPS = 1e-6

kernel_name = "hgrn2_rglru_gated_hybrid_dit"


def _in_split_points():
    sizes = [A_WIDTH] * 5 + [B_WIDTH] * 2 + [D_MODEL] * N_BRANCH
    return [int(v) for v in np.cumsum(sizes)[:-1]]


def layer_norm(t, g, b):
    tf = t.astype(jnp.float32)
    mu = jnp.mean(tf, axis=-1, keepdims=True)
    var = jnp.mean(jnp.square(tf - mu), axis=-1, keepdims=True)
    return ((tf - mu) * lax.rsqrt(var + LN_EPS) * g.astype(jnp.float32) + b.astype(jnp.float32)).astype(t.dtype)


def rms_norm(t, g):
    tf = t.astype(jnp.float32)
    return tf * lax.rsqrt(jnp.mean(jnp.square(tf), axis=-1, keepdims=True) + RMS_EPS) * g.astype(jnp.float32)


def to_heads(t):
    b, l, _ = t.shape
    return t.reshape(b, l, A_HEADS, A_HEAD_DIM).transpose(0, 2, 1, 3)


def from_heads(t):
    b, h, l, d = t.shape
    return t.transpose(0, 2, 1, 3).reshape(b, l, h * d)


def grid_to_colmajor(t, rows):
    b, _, ch = t.shape
    return t.reshape(b, rows, GRID_W, ch).transpose(0, 2, 1, 3)


def colmajor_to_grid(t):
    b, w, r, ch = t.shape
    return t.transpose(0, 2, 1, 3).reshape(b, r * w, ch)


def gla_chunkwise(q, k, v, logf, s0):
    b, h, t, dk = q.shape
    dv = v.shape[-1]
    n = t // A_CHUNK
    q = q.reshape(b, h, n, A_CHUNK, dk)
    k = k.reshape(b, h, n, A_CHUNK, dk)
    v = v.reshape(b, h, n, A_CHUNK, dv)
    g = jnp.cumsum(logf.reshape(b, h, n, A_CHUNK, dk), axis=-2)
    g_last = g[..., -1:, :]
    q_dec = q * jnp.exp(g)
    k_inv = k * jnp.exp(-g)
    k_end = k * jnp.exp(g_last - g)
    mask = jnp.tril(jnp.ones((A_CHUNK, A_CHUNK), dtype=bool))
    scores = jnp.where(mask, jnp.einsum('bhnck,bhnsk->bhncs', q_dec, k_inv), 0.0)
    o_intra = jnp.einsum('bhncs,bhnsv->bhncv', scores, v)
    u = jnp.einsum('bhnsk,bhnsv->bhnkv', k_end, v)
    decay = jnp.exp(g_last[..., 0, :])

    def step(s, inp):
        d_n, u_n = inp
        return d_n[..., None] * s + u_n, s

    s_fin, s_start = lax.scan(step, s0, (jnp.moveaxis(decay, 2, 0), jnp.moveaxis(u, 2, 0)))
    s_start = jnp.moveaxis(s_start, 0, 2)
    o_inter = jnp.einsum('bhnck,bhnkv->bhncv', q_dec, s_start)
    return (o_intra + o_inter).reshape(b, h, t, dv), s_fin


def gla_prefixed(ctx_in, lat_in, reverse):
    if reverse:
        ctx_in = tuple(jnp.flip(a, axis=2) for a in ctx_in)
        lat_in = tuple(jnp.flip(a, axis=2) for a in lat_in)
    b, h, _, dk = ctx_in[0].shape
    dv = ctx_in[2].shape[-1]
    s0 = jnp.zeros((b, h, dk, dv), jnp.float32)
    o_c, s_c = gla_chunkwise(*ctx_in, s0)
    o_x, _ = gla_chunkwise(*lat_in, s_c)
    if reverse:
        o_c, o_x = jnp.flip(o_c, axis=2), jnp.flip(o_x, axis=2)
    return o_c, o_x


def hgrn2_features(z, lb):
    f32 = jnp.float32
    q = to_heads(jax.nn.silu(z[0].astype(f32)) * (A_HEAD_DIM ** -0.5))
    v = to_heads(z[3].astype(f32))
    f_fwd = lb[0] + (1.0 - lb[0]) * jax.nn.sigmoid(z[1].astype(f32))
    f_bwd = lb[1] + (1.0 - lb[1]) * jax.nn.sigmoid(z[2].astype(f32))
    fwd = (q, to_heads(1.0 - f_fwd), v, to_heads(jnp.log(f_fwd)))
    bwd = (q, to_heads(1.0 - f_bwd), v, to_heads(jnp.log(f_bwd)))
    return fwd, bwd


def centred_dwconv(t, w, bias):
    lo = (B_CONV - 1) // 2
    hi = B_CONV - 1 - lo
    n = t.shape[-2]
    tp = jnp.pad(t, [(0, 0)] * (t.ndim - 2) + [(lo, hi), (0, 0)])
    out = bias + tp[..., 0:n, :] * w[0]
    for kk in range(1, B_CONV):
        out = out + tp[..., kk:kk + n, :] * w[kk]
    return out


def rglru_gates(xc, w_r, b_r, w_i, b_i, lam):
    b, l, ch = xc.shape
    xb = xc.reshape(b, l, B_BLOCKS, B_BLOCK_DIM)
    r = jax.nn.sigmoid(jnp.einsum('blgi,gij->blgj', xb, w_r).reshape(b, l, ch) + b_r)
    i = jax.nn.sigmoid(jnp.einsum('blgi,gij->blgj', xb, w_i).reshape(b, l, ch) + b_i)
    log_a = -RG_C * r * jax.nn.softplus(-lam)
    a = jnp.exp(log_a)
    mult = jnp.sqrt(-jnp.expm1(2.0 * log_a))
    return a, mult * (i * xc)


def linear_scan(a, bterm, h0):
    bterm = bterm.at[:, 0].add(a[:, 0] * h0)

    def comb(left, right):
        al, bl = left
        ar, br = right
        return al * ar, ar * bl + br

    _, h = lax.associative_scan(comb, (a, bterm), axis=1)
    return h


def rglru_prefixed(xc_c, xc_x, params, reverse):
    if reverse:
        xc_c, xc_x = jnp.flip(xc_c, axis=1), jnp.flip(xc_x, axis=1)
    a_c, b_c = rglru_gates(xc_c, *params)
    h_c = linear_scan(a_c, b_c, jnp.zeros((xc_c.shape[0], B_WIDTH), jnp.float32))
    a_x, b_x = rglru_gates(xc_x, *params)
    h_x = linear_scan(a_x, b_x, h_c[:, -1])
    if reverse:
        h_c, h_x = jnp.flip(h_c, axis=1), jnp.flip(h_x, axis=1)
    return h_c, h_x


def hybrid_layer(x, ctx, c, c_ctx, w_mod, b_mod, w_in, b_in, lb, norm_a_g, conv_w, conv_b,
                 w_r, b_r, w_i, b_i, lam, p_a, p_b, w_out, ln_g, ln_b, last):
    f32 = jnp.float32
    bsz, t, _ = x.shape
    rows = t // GRID_W
    mod_x = jax.nn.silu(c) @ w_mod + b_mod
    mod_c = jax.nn.silu(c_ctx) @ w_mod + b_mod
    sh_x, sc_x, gt_x = jnp.split(mod_x[:, None, :], 3, axis=-1)
    sh_c, sc_c, gt_c = jnp.split(mod_c, 3, axis=-1)
    u_x = x * (1.0 + sc_x) + sh_x
    u_c = ctx * (1.0 + sc_c) + sh_c
    splits = _in_split_points()
    z_x = jnp.split(u_x @ w_in + b_in, splits, axis=-1)
    z_c = jnp.split(u_c @ w_in + b_in, splits, axis=-1)

    fx, bx = hgrn2_features(z_x, lb)
    fc, bc = hgrn2_features(z_c, lb)
    oc_f, ox_f = gla_prefixed(fc, fx, False)
    oc_b, ox_b = gla_prefixed(bc, bx, True)

    xc_x = centred_dwconv(grid_to_colmajor(z_x[5], rows), conv_w, conv_b).astype(f32).reshape(bsz, t, B_WIDTH)
    xc_c = centred_dwconv(z_c[5], conv_w, conv_b).astype(f32)
    hc_f, hx_f = rglru_prefixed(xc_c, xc_x, (w_r[0], b_r[0], w_i[0], b_i[0], lam[0]), False)
    hc_b, hx_b = rglru_prefixed(xc_c, xc_x, (w_r[1], b_r[1], w_i[1], b_i[1], lam[1]), True)
    hx = colmajor_to_grid((hx_f + hx_b).reshape(bsz, GRID_W, rows, B_WIDTH))

    def merge(z, o_a, h_b):
        o_a = from_heads(rms_norm(o_a, norm_a_g)) * jax.nn.silu(z[4].astype(f32))
        o_b = h_b * jax.nn.silu(z[6].astype(f32))
        y = jax.nn.sigmoid(z[7]) * (o_a @ p_a) + jax.nn.sigmoid(z[8]) * (o_b @ p_b)
        return y @ w_out

    x_new = layer_norm(DEEPNORM_ALPHA * x + gt_x * merge(z_x, ox_f + ox_b, hx), ln_g, ln_b)
    if last:
        return x_new, ctx
    ctx_new = layer_norm(DEEPNORM_ALPHA * ctx + gt_c * merge(z_c, oc_f + oc_b, hc_f + hc_b), ln_g, ln_b)
    return x_new, ctx_new


def setup_inputs(seed: int = 0) -> dict:
    key = jax.random.key(seed)
    ks = jax.random.split(key, 24)
    f32 = jnp.float32
    n = lambda k, s, sc: jax.random.normal(k, s, f32) * sc
    u_a = jax.random.uniform(ks[17], (DEPTH, 2, B_WIDTH), f32, 0.9, 0.999)
    s_a = u_a ** (1.0 / RG_C)
    return {
        "x": n(ks[0], (BATCH, SEQ, D_MODEL), 1.0),
        "c": n(ks[1], (BATCH, D_MODEL), 1.0),
        "ctx": n(ks[2], (BATCH, CTX_LEN, D_MODEL), 1.0),
        "c_ctx": n(ks[3], (D_MODEL,), 1.0),
        "w_mod": n(ks[4], (DEPTH, D_MODEL, 3 * D_MODEL), 0.5 * D_MODEL ** -0.5),
        "b_mod": n(ks[5], (DEPTH, 3 * D_MODEL), 0.01),
        "w_in": n(ks[6], (DEPTH, D_MODEL, IN_COLS), D_MODEL ** -0.5),
        "b_in": n(ks[7], (DEPTH, IN_COLS), 0.01),
        "lb_logits": n(ks[8], (DEPTH + 1, 2, A_WIDTH), 0.1),
        "norm_a_g": 1.0 + n(ks[9], (DEPTH, A_HEAD_DIM), 0.01),
        "conv_w": n(ks[10], (DEPTH, B_CONV, B_WIDTH), B_CONV ** -0.5),
        "conv_b": n(ks[11], (DEPTH, B_WIDTH), 0.01),
        "w_r": n(ks[12], (DEPTH, 2, B_BLOCKS, B_BLOCK_DIM, B_BLOCK_DIM), B_BLOCK_DIM ** -0.5),
        "b_r": n(ks[13], (DEPTH, 2, B_WIDTH), 0.01),
        "w_i": n(ks[14], (DEPTH, 2, B_BLOCKS, B_BLOCK_DIM, B_BLOCK_DIM), B_BLOCK_DIM ** -0.5),
        "b_i": n(ks[15], (DEPTH, 2, B_WIDTH), 0.01),
        "lam": jnp.log(s_a) - jnp.log1p(-s_a),
        "p_a": n(ks[18], (DEPTH, A_WIDTH, D_MODEL), DEEPNORM_BETA * A_WIDTH ** -0.5),
        "p_b": n(ks[19], (DEPTH, B_WIDTH, D_MODEL), DEEPNORM_BETA * B_WIDTH ** -0.5),
        "w_out": n(ks[20], (DEPTH, D_MODEL, D_MODEL), DEEPNORM_BETA * D_MODEL ** -0.5),
        "ln_g": 1.0 + n(ks[21], (DEPTH, D_MODEL), 0.01),
        "ln_b": n(ks[22], (DEPTH, D_MODEL), 0.01),
    }


def reference(x, c, ctx, c_ctx, w_mod, b_mod, w_in, b_in, lb_logits, norm_a_g, conv_w, conv_b,
              w_r, b_r, w_i, b_i, lam, p_a, p_b, w_out, ln_g, ln_b):
    lb_all = jnp.cumsum(jax.nn.softmax(lb_logits.astype(jnp.float32), axis=0), axis=0)
    for layer in range(DEPTH):
        x, ctx = hybrid_layer(
            x, ctx, c, c_ctx, w_mod[layer], b_mod[layer], w_in[layer], b_in[layer], lb_all[layer],
            norm_a_g[layer], conv_w[layer], conv_b[layer], w_r[layer], b_r[layer], w_i[layer], b_i[layer],
            lam[layer], p_a[layer], p_b[layer], w_out[layer], ln_g[layer], ln_b[layer],
            last=(layer == DEPTH - 1))
    return x
```

```python
import os
import numpy as np
from contextlib import ExitStack
import concourse.bass as bass
import concourse.mybir as mybir
from concourse.bass_utils import run_bass_kernel_spmd

F32 = mybir.dt.float32
BF16 = mybir.dt.bfloat16
ALU = mybir.AluOpType
AF = mybir.ActivationFunctionType

D = 1024
T = 8192
NT = 512
NST = T // NT
T_OWN = T // 2
NST_OWN = T_OWN // NT
NCH = NT // 128
CTXL = 256
GW = 64
ROWS = T // GW
QSCALE = 128 ** -0.5
ALPHA = 2.0 ** 0.25
LN_EPS = 1e-5
RMS_EPS = 1e-6


class Buf:
    __slots__ = ("name", "last_w", "readers", "dma_sem", "dma_cnt")

    def __init__(self, name):
        self.name = name
        self.last_w = None
        self.readers = {}
        self.dma_sem = None
        self.dma_cnt = 0


class Eng:
    def __init__(self, name, h, sem):
        self.name, self.h, self.sem, self.n = name, h, sem, 0
        self.seen = {}


class K:
    SAME_ENG_SYNC = True

    def __init__(self, nc, stack):
        self.nc = nc
        self.stack = stack
        self.engs = {}
        for name, h in (("pe", nc.tensor), ("act", nc.scalar), ("dve", nc.vector),
                        ("pool", nc.gpsimd), ("sp", nc.sync)):
            sem = stack.enter_context(nc.semaphore("s_" + name))
            self.engs[name] = Eng(name, h, sem)
        self.bufs = []
        self.nwaits = 0
        self.free_sems = []

    def buf(self, name):
        b = Buf(name)
        self.bufs.append(b)
        return b

    def _wait(self, E, deps):
        need = {}
        for sem, val in deps:
            if val <= 0:
                continue
            if need.get(id(sem), (None, 0))[1] < val:
                need[id(sem)] = (sem, val)
        for sem, val in need.values():
            if sem is E.sem:
                if E.name == "pe" or not self.SAME_ENG_SYNC:
                    continue
            if E.seen.get(id(sem), 0) >= val:
                continue
            E.h.wait_ge(sem, val)
            self.nwaits += 1
            E.seen[id(sem)] = val

    def _deps(self, reads, writes):
        deps = []
        for b in reads:
            if b.last_w:
                deps.append(b.last_w)
        for b in writes:
            if b.last_w:
                deps.append(b.last_w)
            deps.extend(b.readers.values())
        return deps

    def _record(self, ev, reads, writes):
        sem, val = ev
        for b in reads:
            if b.readers.get(id(sem), (None, 0))[1] < val:
                b.readers[id(sem)] = ev
        for b in writes:
            b.last_w = ev
            b.readers = {}

    def op(self, e, emit, reads=(), writes=()):
        E = self.engs[e]
        self._wait(E, self._deps(reads, writes))
        ins = emit(E.h)
        E.n += 1
        ins.then_inc(E.sem, 1)
        self._record((E.sem, E.n), reads, writes)
        return ins

    def dma(self, q, out, in_, reads=(), writes=(), key=None, **kw):
        E = self.engs[q]
        kb = key if key is not None else (writes[0] if writes else reads[0])
        if kb.dma_sem is None:
            kb.dma_sem = self.stack.enter_context(self.nc.semaphore("d_" + kb.name))
        deps = self._deps(reads, writes)
        if kb.dma_cnt:
            deps.append((kb.dma_sem, kb.dma_cnt))
        self._wait(E, deps)
        ins = E.h.dma_start(out=out, in_=in_, **kw)
        kb.dma_cnt += 16
        ins.then_inc(kb.dma_sem, 16)
        self._record((kb.dma_sem, kb.dma_cnt), reads, writes)
        return ins

    def barrier(self, skip=()):
        sp = self.engs["sp"]
        deps = [(E.sem, E.n) for E in self.engs.values() if E is not sp]
        deps += [(b.dma_sem, b.dma_cnt) for b in self.bufs if b.dma_sem is not None and b not in skip]
        self._wait(sp, deps)
        ins = sp.h.nop()
        sp.n += 1
        ins.then_inc(sp.sem, 1)
        for E in self.engs.values():
            if E is not sp:
                self._wait(E, [(sp.sem, sp.n)])
        for b in self.bufs:
            b.last_w = None
            b.readers = {}


class Tl:
    __slots__ = ("t", "b")

    def __init__(self, t, b):
        self.t, self.b = t, b


class Ring:
    def __init__(self, tiles):
        self.tiles, self.i = tiles, 0

    def next(self):
        t = self.tiles[self.i % len(self.tiles)]
        self.i += 1
        return t


def fap(t, off, dims):
    base = t[:]
    return bass.AP(tensor=base.tensor, offset=base.offset + off,
                   ap=[list(base.ap[0])] + [list(d) for d in dims])


def build_program(debug=False):
    nc = bass.Bass("TRN2", target_bir_lowering=False)

    def inp(name, shape, dt=F32):
        return nc.dram_tensor(name, shape, dt, kind="ExternalInput").ap()

    def scratch(name, shape, dt):
        return nc.dram_tensor(name, shape, dt, kind=("ExternalOutput" if debug else "Internal")).ap()

    x_d = inp("x", [T, D])
    ctx_d = inp("ctx", [CTXL, D])
    cvec_d = inp("cvec", [128, 16])
    wmod_d = inp("w_mod", [D, 3 * D])
    bmodT_d = inp("bmodT", [128, 24])
    bmodr_d = inp("bmod_row", [1, 3 * D])
    w4_d = inp("w4", [72, 128, 8, 128])
    binT_d = inp("binT", [128, 72])
    bv_d = inp("bv_row", [1, D])
    lbl_d = inp("lbl", [128, 32])
    nag_d = inp("nag", [128, 1])
    convw_d = inp("convw", [128, 40])
    convb_d = inp("convb", [128, 8])
    wr_d = inp("wr", [128, 2048])
    wi_d = inp("wi", [128, 2048])
    br_d = inp("br", [128, 16])
    bi_d = inp("bi", [128, 16])
    lam_d = inp("lam", [128, 16])
    pa_d = inp("pa", [128, 8, D])
    pb_d = inp("pb", [128, 8, D])
    wo_d = inp("wo", [128, 8, D])
    lng_d = inp("lng", [128, D])
    lnb_d = inp("lnb", [128, D])
    ident_d = inp("ident", [128, 128])
    maskf_d = inp("maskf", [128, 128])
    maskb_d = inp("maskb", [128, 128])
    out_d = nc.dram_tensor("out", [T_OWN, D], F32, kind="ExternalOutput").ap()

    WB = scratch("wb_s", [72, 128, 8, 128], BF16)
    PAB = scratch("pab_s", [128, 8, D], BF16)
    PBB = scratch("pbb_s", [128, 8, D], BF16)
    WOB = scratch("wob_s", [128, 8, D], BF16)
    XT = scratch("xt_s", [8, 128, T], F32)
    OP = scratch("op_s", [8, 128, T], F32)
    QDB = scratch("qdb_s", [8, 128, T], BF16)
    UB = scratch("ub_s", [8, 128, T // 128, 128], F32)
    HS = scratch("h_s", [8, 128, T], F32)
    if debug:
        DBG = nc.dram_tensor("dbg", [128, 4096], F32, kind="ExternalOutput").ap()

    with ExitStack() as pst:
        k = K(nc, pst)
        cnt = [0]

        def sb(stack, name, shape, dt):
            cnt[0] += 1
            nm = f"{name}_{cnt[0]}"
            return Tl(stack.enter_context(nc.sbuf_tensor(nm, shape, dt)), k.buf(nm))

        def ring(stack, name, shape, dt, n):
            return Ring([sb(stack, name, shape, dt) for _ in range(n)])

        ps = []
        for i in range(7):
            ps.append(Tl(pst.enter_context(nc.psum_tensor(f"ps{i}", [128, 512], F32)), k.buf(f"ps{i}")))
        trb = Tl(pst.enter_context(nc.psum_tensor("trb", [128, 1024], BF16)), k.buf("trb"))
        pj = Ring([ps[0], ps[1]])

        def act(out, in_, func, reads, writes, **kw):
            k.op("act", lambda e: e.activation(out=out, in_=in_, func=func, **kw), reads, writes)

        def tt(eng, out, in0, in1, op, reads, writes):
            k.op(eng, lambda e: e.tensor_tensor(out=out, in0=in0, in1=in1, op=op), reads, writes)

        def ts(eng, out, in0, s1, s2, op0, op1, reads, writes):
            if s2 is None:
                k.op(eng, lambda e: e.tensor_scalar(out=out, in0=in0, scalar1=s1, scalar2=None, op0=op0), reads, writes)
            else:
                k.op(eng, lambda e: e.tensor_scalar(out=out, in0=in0, scalar1=s1, scalar2=s2, op0=op0, op1=op1), reads, writes)

        def stt(out, in0, scalar, in1, op0, op1, reads, writes):
            k.op("dve", lambda e: e.scalar_tensor_tensor(out=out, in0=in0, scalar=scalar, in1=in1, op0=op0, op1=op1),
                 reads, writes)

        def cp(eng, out, in_, reads, writes):
            if eng == "act":
                k.op("act", lambda e: e.activation(out=out, in_=in_, func=AF.Identity), reads, writes)
            else:
                k.op(eng, lambda e: e.tensor_copy(out=out, in_=in_), reads, writes)

        def mm(out, lhsT, rhs, start, stop, reads, writes):
            k.op("pe", lambda e: e.matmul(out, lhsT=lhsT, rhs=rhs, start=start, stop=stop), reads, writes)

        def tr(out, in_, ident, reads, writes):
            k.op("pe", lambda e: e.transpose(out, in_, ident), reads, writes)

        ident32 = sb(pst, "ident32", [128, 128], F32)
        identb = sb(pst, "identb", [128, 128], BF16)
        ones32 = sb(pst, "ones32", [128, 128], F32)
        zeros32 = sb(pst, "zeros32", [128, 128], F32)
        onesb = sb(pst, "onesb", [1, 128], BF16)
        modx = sb(pst, "modx", [128, 24], F32)
        modc = sb(pst, "modc", [128, 24], F32)
        scp1x = sb(pst, "scp1x", [128, 8], F32)
        scp1c = sb(pst, "scp1c", [128, 8], F32)
        gt_bc = sb(pst, "gt_bc", [128, D], F32)
        lb = sb(pst, "lb", [128, 16], F32)
        oml = sb(pst, "oml", [128, 16], F32)
        binT = sb(pst, "binT", [128, 72], F32)
        bv_bf = sb(pst, "bv_bf", [1, D], BF16)
        nag = sb(pst, "nag", [128, 1], F32)
        convw = sb(pst, "convw", [128, 40], F32)
        convb = sb(pst, "convb", [128, 8], F32)
        br = sb(pst, "br", [128, 16], F32)
        bi = sb(pst, "bi", [128, 16], F32)
        cA = sb(pst, "cA", [128, 16], F32)
        hcA = sb(pst, "hcA", [128, 16], F32)
        hbr = sb(pst, "hbr", [128, 16], F32)
        hbi = sb(pst, "hbi", [128, 16], F32)
        fc0 = sb(pst, "fc0", [128, 16], F32)
        fc1 = sb(pst, "fc1", [128, 16], F32)
        hbinT = sb(pst, "hbinT", [128, 72], F32)
        Sb = [sb(pst, f"Sb{h}", [128, 128], F32) for h in range(8)]
        dec_b = sb(pst, "dec_b", [128, 8, T // 128], F32)
        hcf = sb(pst, "hcf", [128, 8], F32)
        hcb = sb(pst, "hcb", [128, 8], F32)
        dcast = k.buf("dcast")

        with ExitStack() as st:
            k.dma("sp", ident32.t[:], ident_d[:, :], writes=[ident32.b])
            k.op("dve", lambda e: e.memset(ones32.t[:], 1.0), writes=[ones32.b])
            k.op("dve", lambda e: e.memset(zeros32.t[:], 0.0), writes=[zeros32.b])
            k.op("dve", lambda e: e.memset(onesb.t[:], 1.0), writes=[onesb.b])
            cp("dve", identb.t[:], ident32.t[:], [ident32.b], [identb.b])

            cvec = sb(st, "cvec", [128, 16], F32)
            cs = sb(st, "cs", [128, 16], F32)
            lbl = sb(st, "lbl", [128, 32], F32)
            lam = sb(st, "lam", [128, 16], F32)
            bmodT = sb(st, "bmodT", [128, 24], F32)
            bmodr = sb(st, "bmodr", [1, 3 * D], F32)
            gt_row = sb(st, "gt_row", [1, D], F32)
            bv32 = sb(st, "bv32", [1, D], F32)
            wmod = sb(st, "wmod", [128, 8, 3 * D], F32)
            tmp16 = sb(st, "tmp16", [128, 16], F32)
            tmp16b = sb(st, "tmp16b", [128, 16], F32)
            for tl, src in ((cvec, cvec_d), (lbl, lbl_d), (lam, lam_d), (bmodT, bmodT_d), (bmodr, bmodr_d),
                            (binT, binT_d), (nag, nag_d), (convw, convw_d), (convb, convb_d), (br, br_d),
                            (bi, bi_d), (bv32, bv_d)):
                k.dma("sp", tl.t[:], src[:, :], writes=[tl.b])
            for kc in range(8):
                k.dma("sp" if kc % 2 == 0 else "act", wmod.t[:, kc, :], wmod_d[kc * 128:(kc + 1) * 128, :], writes=[wmod.b])
            late = []
            for g in (1, 2, 3, 5, 0, 4, 6, 7, 8):
                kb = k.buf(f"dcast{g}")
                k.dma("pool", WB[g * 8:(g + 1) * 8], w4_d[g * 8:(g + 1) * 8], key=kb, reads=[wmod.b])
                if g in (4, 6, 7, 8):
                    late.append(kb)
            for nm_, dst, src in (("dcpa", PAB, pa_d), ("dcpb", PBB, pb_d), ("dcwo", WOB, wo_d)):
                kb = k.buf(nm_)
                k.dma("pool", dst[:, :, :], src[:, :, :], key=kb, reads=[wmod.b])
                late.append(kb)
            act(cs.t[:], cvec.t[:], AF.Silu, [cvec.b], [cs.b])
            for oc in range(24):
                for kc in range(8):
                    mm(ps[0].t[:, 2 * oc:2 * oc + 2], wmod.t[:, kc, oc * 128:(oc + 1) * 128],
                       fap(cs.t, kc, [[8, 2]]), kc == 0, kc == 7, [wmod.b, cs.b], [ps[0].b])
            tt("dve", modx.t[:], fap(ps[0].t, 0, [[2, 24]]), bmodT.t[:], ALU.add, [ps[0].b, bmodT.b], [modx.b])
            tt("dve", modc.t[:], fap(ps[0].t, 1, [[2, 24]]), bmodT.t[:], ALU.add, [ps[0].b, bmodT.b], [modc.b])
            ts("dve", scp1x.t[:], modx.t[:, 8:16], 1.0, None, ALU.add, None, [modx.b], [scp1x.b])
            ts("dve", scp1c.t[:], modc.t[:, 8:16], 1.0, None, ALU.add, None, [modc.b], [scp1c.b])
            for half in range(2):
                pr = ps[1 + half]
                for kc in range(8):
                    mm(pr.t[0:1, :], cs.t[:, kc:kc + 1], wmod.t[:, kc, 2048 + half * 512:2048 + (half + 1) * 512],
                       kc == 0, kc == 7, [wmod.b, cs.b], [pr.b])
                tt("dve", gt_row.t[0:1, half * 512:(half + 1) * 512], pr.t[0:1, :],
                   bmodr.t[0:1, 2048 + half * 512:2048 + (half + 1) * 512], ALU.add, [pr.b, bmodr.b], [gt_row.b])
            for half in range(2):
                pr = ps[3 + half]
                mm(pr.t[:, :], ones32.t[0:1, :], gt_row.t[0:1, half * 512:(half + 1) * 512], True, True,
                   [ones32.b, gt_row.b], [pr.b])
                act(gt_bc.t[:, half * 512:(half + 1) * 512], pr.t[:, :], AF.Identity, [pr.b], [gt_bc.b], scale=0.5)
            tt("dve", tmp16.t[:], lbl.t[:, 0:16], lbl.t[:, 16:32], ALU.subtract, [lbl.b], [tmp16.b])
            act(lb.t[:], tmp16.t[:], AF.Sigmoid, [tmp16.b], [lb.b])
            ts("dve", oml.t[:], lb.t[:], -1.0, 1.0, ALU.mult, ALU.add, [lb.b], [oml.b])
            act(tmp16.t[:], lam.t[:], AF.Exp, [lam.b], [tmp16.b], scale=-1.0)
            ts("dve", tmp16b.t[:], tmp16.t[:], 1.0 / 3.0, -0.5, ALU.mult, ALU.add, [tmp16.b], [tmp16b.b])
            tt("dve", tmp16b.t[:], tmp16b.t[:], tmp16.t[:], ALU.mult, [tmp16.b, tmp16b.b], [tmp16b.b])
            ts("dve", tmp16b.t[:], tmp16b.t[:], 1.0, None, ALU.add, None, [tmp16b.b], [tmp16b.b])
            tt("dve", tmp16b.t[:], tmp16b.t[:], tmp16.t[:], ALU.mult, [tmp16.b, tmp16b.b], [tmp16b.b])
            ts("dve", cA.t[:], tmp16b.t[:], -8.0, None, ALU.mult, None, [tmp16b.b], [cA.b])
            ts("dve", hcA.t[:], cA.t[:], 0.5, None, ALU.mult, None, [cA.b], [hcA.b])
            ts("dve", hbr.t[:], br.t[:], 0.5, None, ALU.mult, None, [br.b], [hbr.b])
            ts("dve", hbi.t[:], bi.t[:], 0.5, None, ALU.mult, None, [bi.b], [hbi.b])
            ts("dve", hbinT.t[:], binT.t[:], 0.5, None, ALU.mult, None, [binT.b], [hbinT.b])
            ts("dve", fc1.t[:], oml.t[:], 0.5, None, ALU.mult, None, [oml.b], [fc1.b])
            tt("dve", fc0.t[:], fc1.t[:], lb.t[:], ALU.add, [fc1.b, lb.b], [fc0.b])
            cp("dve", bv_bf.t[:], bv32.t[:], [bv32.b], [bv_bf.b])
            for h in range(8):
                k.op("dve", lambda e: e.memset(Sb[h].t[:], 0.0), writes=[Sb[h].b])
            k.barrier(skip=late)

        def make_uT(xtiles, uT, sh_t, scp1_t, nb_reads):
            n = len(xtiles) * 128
            for j in range(8):
                bank = pj.next()
                for i, xt in enumerate(xtiles):
                    tr(bank.t[:, i * 128:(i + 1) * 128], xt.t[:, j * 128:(j + 1) * 128], ident32.t[:],
                       [xt.b, ident32.b], [bank.b])
                act(uT.t[:, j, 0:n], bank.t[:, 0:n], AF.Identity, [bank.b] + nb_reads, [uT.b],
                    scale=scp1_t[:, j:j + 1], bias=sh_t[:, j:j + 1])

        def proj_fm(cb, uT, n, wring, evac):
            w = wring.next()
            k.dma("sp", w.t[:], WB[cb], writes=[w.b])
            bank = pj.next()
            for kc in range(8):
                mm(bank.t[:, 0:n], w.t[:, kc, :], uT.t[:, kc, 0:n], kc == 0, kc == 7, [w.b, uT.b], [bank.b])
            evac(bank)

        def v_proj(uT, nch, wv, vtok):
            for c in range(nch):
                for half in range(2):
                    bank = pj.next()
                    for kc in range(8):
                        mm(bank.t[:, :], uT.t[:, kc, c * 128:(c + 1) * 128],
                           fap(wv.t, half * 4 * 1024 + kc * 128, [[1024, 4], [1, 128]]), kc == 0, False,
                           [uT.b, wv.b], [bank.b])
                    mm(bank.t[:, :], onesb.t[0:1, :], bv_bf.t[0:1, half * 512:(half + 1) * 512], False, True,
                       [onesb.b, bv_bf.b], [bank.b])
                    cp("act", vtok.t[:, c, half * 512:(half + 1) * 512], bank.t[:, :], [bank.b], [vtok.b])

        def gla_local(f, P, rP, kin32, d1, dirn, nch, n):
            pos = 0 if dirn == 0 else 127
            cp("dve", fap(d1.t, pos, [[128, nch]]), fap(f.t, pos, [[128, nch]]), [f.b], [d1.b])
            if dirn == 0:
                o_ap, f_ap, d_ap = P.t[:, 0:n], f.t[:, 0:n], d1.t[:, 0:n]
            else:
                o_ap, f_ap, d_ap = (fap(P.t, n - 1, [[-1, n]]), fap(f.t, n - 1, [[-1, n]]), fap(d1.t, n - 1, [[-1, n]]))
            k.op("dve", lambda e: e.tensor_tensor_scan(out=o_ap, data0=f_ap, data1=d_ap, initial=1.0,
                                                       op0=ALU.mult, op1=ALU.max), [f.b, d1.b], [P.b])
            k.op("dve", lambda e: e.reciprocal(out=rP.t[:, 0:n], in_=P.t[:, 0:n]), [P.b], [rP.b])
            ts("pool", f.t[:, 0:n], f.t[:, 0:n], -1.0, 1.0, ALU.mult, ALU.add, [f.b], [f.b])
            tt("pool", kin32.t[:, 0:n], f.t[:, 0:n], rP.t[:, 0:n], ALU.mult, [f.b, rP.b], [kin32.b])

        def plast_bc(P, dirn, nch):
            return fap(P.t, 127 if dirn == 0 else 0, [[128, nch], [0, 128]])

        def plast(P, dirn, nch):
            return fap(P.t, 127 if dirn == 0 else 0, [[128, nch]])

        def f_evac(ftile, n, idx, cbidx, eng2):
            def ev(bank):
                act(ftile.t[:, 0:n], bank.t[:, 0:n], AF.Tanh, [bank.b, hbinT.b], [ftile.b],
                    scale=0.5, bias=hbinT.t[:, cbidx:cbidx + 1])
                ts(eng2, ftile.t[:, 0:n], ftile.t[:, 0:n], fc1.t[:, idx:idx + 1], fc0.t[:, idx:idx + 1],
                   ALU.mult, ALU.add, [ftile.b, fc1.b, fc0.b], [ftile.b])
            return ev

        def mixer_bufs(name, nsub):
            return {kk: [k.buf(f"{name}_{kk}{i}") for i in range(nsub)] for kk in ("xc", "xb", "A", "B0", "B1")} | {"raw": [k.buf(f"{name}_raw{i}") for i in range(4)]}

        def mixer_b(j, raw, xc, xcb, A, B, Tn, W, sub, tmp, diag, mb, h0f, h0b, fin, wr_bf, wi_bf, G):
            rows = Tn // W
            nr = sub // W
            nsub = Tn // sub

            def pm(t, s0):
                return fap(t, s0 // W, [[1, nr], [rows, W]])

            def rv(t, s0):
                return fap(t, s0, [[W, nr], [1, W]])

            for i in range(5):
                ts("pool", diag.t[:, i, :], ident32.t[:, :], convw.t[:, j * 5 + i:j * 5 + i + 1], None, ALU.mult, None,
                   [ident32.b, convw.b], [diag.b])
            csz = max(Tn // 4, 1)

            def rawb(lo_, hi_):
                return mb["raw"][max(lo_, 0) // csz:min((min(hi_, Tn) - 1) // csz, 3) + 1]

            def front(si):
                s0 = si * sub
                rb = rawb(s0 - 2 * W, s0 + sub + 2 * W)
                bank = pj.next()
                order = [2, 1, 3]
                for n_, i in enumerate(order):
                    off = (i - 2) * W
                    lo, hi = max(s0, -off), min(s0 + sub, Tn - off)
                    mm(bank.t[:, lo - s0:hi - s0], diag.t[:, i, :], raw.t[:, lo + off:hi + off], n_ == 0, n_ == 2,
                       [diag.b] + rb, [bank.b])
                act(xc.t[:, s0:s0 + sub], bank.t[:, 0:sub], AF.Identity, [bank.b, convb.b], [mb["xc"][si]],
                    bias=convb.t[:, j:j + 1])
                for i in (0, 4):
                    off = (i - 2) * W
                    lo, hi = max(s0, -off), min(s0 + sub, Tn - off)
                    stt(xc.t[:, lo:hi], raw.t[:, lo + off:hi + off], convw.t[:, j * 5 + i:j * 5 + i + 1], xc.t[:, lo:hi],
                        ALU.mult, ALU.add, rb + [convw.b, mb["xc"][si]], [mb["xc"][si]])
                cp("pool", xcb.t[:, s0:s0 + sub], xc.t[:, s0:s0 + sub], [mb["xc"][si]], [mb["xb"][si]])

            for dirn in range(2):
                Bd = B if dirn == 0 else raw
                BS = mb["B0"] if dirn == 0 else mb["B1"]
                AS = mb["A"]
                gi = dirn * 8 + j
                def igate(si):
                    s0 = si * sub
                    pi = ps[4 + si % 2]
                    mm(pi.t[:, 0:sub], wi_bf.t[:, gi * 128:(gi + 1) * 128], xcb.t[:, s0:s0 + sub], True, True,
                       [wi_bf.b, mb["xb"][si]], [pi.b])
                    act(pm(Bd.t, s0), rv(pi.t, 0), AF.Tanh, [pi.b, hbi.b], [BS[si]] + (mb["raw"] if dirn == 1 else []),
                        scale=0.5, bias=hbi.t[:, gi:gi + 1])

                if dirn == 1:
                    for si in range(nsub):
                        igate(si)
                for g0 in range(0, nsub, G):
                    grp = list(range(g0, min(nsub, g0 + G)))
                    tms = {}
                    if dirn == 0:
                        for si in grp:
                            front(si)
                    for si in grp:
                        s0 = si * sub
                        sl = slice(s0, s0 + sub)
                        pr = ps[2 + si % 2]
                        mm(pr.t[:, 0:sub], wr_bf.t[:, gi * 128:(gi + 1) * 128], xcb.t[:, sl], True, True,
                           [wr_bf.b, mb["xb"][si]], [pr.b])
                        act(pm(A.t, s0), rv(pr.t, 0), AF.Tanh, [pr.b, hbr.b], [AS[si]], scale=0.5, bias=hbr.t[:, gi:gi + 1])
                        if dirn == 0:
                            igate(si)
                    for si in grp:
                        s0 = si * sub
                        t_m = tmp.next()
                        tms[si] = t_m
                        act(pm(A.t, s0), pm(A.t, s0), AF.Exp, [AS[si], hcA.b], [AS[si]], scale=hcA.t[:, gi:gi + 1],
                            bias=hcA.t[:, gi:gi + 1])
                        stt(rv(t_m.t, 0), pm(A.t, s0), 1.0, pm(A.t, s0), ALU.mult, ALU.mult, [AS[si]], [t_m.b])
                    for si in grp:
                        t_m = tms[si]
                        act(rv(t_m.t, 0), rv(t_m.t, 0), AF.Sqrt, [t_m.b], [t_m.b], scale=-0.25, bias=0.25)
                    for si in grp:
                        s0 = si * sub
                        t_m = tms[si]
                        stt(pm(Bd.t, s0), pm(Bd.t, s0), 1.0, rv(xc.t, s0), ALU.add, ALU.mult, [BS[si], mb["xc"][si]], [BS[si]])
                        tt("pool" if si % 2 else "dve", pm(Bd.t, s0), pm(Bd.t, s0), rv(t_m.t, 0), ALU.mult,
                           [BS[si], t_m.b], [BS[si]])
                if dirn == 0:
                    a_ap, b_ap = A.t[:, 0:Tn], Bd.t[:, 0:Tn]
                    init = h0f
                else:
                    a_ap, b_ap = fap(A.t, Tn - 1, [[-1, Tn]]), fap(Bd.t, Tn - 1, [[-1, Tn]])
                    init = h0b
                k.op("dve", lambda e: e.tensor_tensor_scan(out=b_ap, data0=a_ap, data1=b_ap, initial=init,
                                                           op0=ALU.mult, op1=ALU.add),
                     AS + BS + [hcf.b, hcb.b], BS)
                fin(dirn, Bd, BS)

        with ExitStack() as st:
            xring = ring(st, "xt", [128, D], F32, 6)
            uT_r = ring(st, "uT", [128, 8, NT], BF16, 2)
            wring = ring(st, "w", [128, 8, 128], BF16, 6)
            wv = sb(st, "wv", [128, 8, 8, 128], BF16)
            vtok = sb(st, "vtok", [128, NCH, D], BF16)
            qraw_r = ring(st, "qraw", [128, NT], F32, 2)
            ff_r = ring(st, "ff", [128, NT], F32, 2)
            fb_r = ring(st, "fb", [128, NT], F32, 2)
            P_r = ring(st, "P", [128, NT], F32, 4)
            rP_r = ring(st, "rP", [128, NT], F32, 2)
            kin_r = ring(st, "kin", [128, NT], F32, 2)
            kinv_r = ring(st, "kinv", [128, NT], BF16, 4)
            qdec_r = ring(st, "qdec", [128, NT], BF16, 4)
            kend_r = ring(st, "kend", [128, NT], BF16, 4)
            kendT_r = ring(st, "kendT", [128, 2 * NCH, 128], BF16, 2)
            d1_r = [ring(st, "d1f", [128, NT], F32, 2), ring(st, "d1b", [128, NT], F32, 2)]
            t1_r = ring(st, "t1", [128, NT], F32, 2)
            t2_r = ring(st, "t2", [128, NT], F32, 2)
            scs_r = ring(st, "scs", [128, NT], BF16, 2)
            decf_r = ring(st, "decf", [128, NCH], F32, 2)
            sst_r = ring(st, "sst", [128, NCH, 128], BF16, 2)
            s32_r = ring(st, "s32", [128, NCH, 128], F32, 2)
            ost_r = ring(st, "ost", [128, NT], F32, 2)
            ubst_r = ring(st, "ubst", [128, NCH, 128], F32, 2)
            z5st_r = ring(st, "z5st", [128, NT], F32, 2)
            craw = sb(st, "craw", [128, 8, CTXL], F32)
            cxc = sb(st, "cxc", [128, CTXL], F32)
            cxcb = sb(st, "cxcb", [128, CTXL], BF16)
            cA_ = sb(st, "cA_", [128, CTXL], F32)
            cB_ = sb(st, "cB_", [128, CTXL], F32)
            crawj = sb(st, "crawj", [128, CTXL], F32)
            ctmp = ring(st, "ctmp", [128, CTXL], F32, 3)

            k.dma("sp", wv.t[:], WB[24:32].rearrange("cb p kc c -> p cb kc c"), writes=[wv.b])
            maskf4 = sb(st, "maskf4", [128, 4, 128], F32)
            maskb4 = sb(st, "maskb4", [128, 4, 128], F32)
            k.dma("sp", maskf4.t[:], bass.AP(tensor=maskf_d.tensor, offset=maskf_d.offset,
                                              ap=[[128, 128], [0, 4], [1, 128]]), writes=[maskf4.b])
            k.dma("sp", maskb4.t[:], bass.AP(tensor=maskb_d.tensor, offset=maskb_d.offset,
                                              ap=[[128, 128], [0, 4], [1, 128]]), writes=[maskb4.b])
            Sf = [sb(st, f"Sf{h}", [128, 128], F32) for h in range(8)]
            for h in range(8):
                k.op("pool", lambda e: e.memset(Sf[h].t[:], 0.0), writes=[Sf[h].b])
            wr_bf = sb(st, "wr_bf", [128, 2048], BF16)
            wi_bf = sb(st, "wi_bf", [128, 2048], BF16)
            k.dma("pool", wr_bf.t[:], wr_d[:, :], writes=[wr_bf.b])
            k.dma("pool", wi_bf.t[:], wi_d[:, :], writes=[wi_bf.b])
            for rg in d1_r:
                for tl in rg.tiles:
                    k.op("pool", lambda e: e.memset(tl.t[:], 0.0), writes=[tl.b])

            def startA(h, uT, n, want_q, dirs=(0, 1)):
                hd = {"h": h}
                parts = []
                if want_q:
                    qraw = qraw_r.next()
                    hd["qraw"] = qraw
                    parts.append(lambda: proj_fm(h, uT, n, wring, lambda bank: act(
                        qraw.t[:, 0:n], bank.t[:, 0:n], AF.Silu, [bank.b, binT.b], [qraw.b], bias=binT.t[:, h:h + 1])))
                hd["f"] = {}
                if 0 in dirs:
                    ff = ff_r.next()
                    hd["f"][0] = ff
                    parts.append(lambda: proj_fm(8 + h, uT, n, wring, f_evac(ff, n, h, 8 + h, "pool")))
                if 1 in dirs:
                    fbt = fb_r.next()
                    hd["f"][1] = fbt
                    parts.append(lambda: proj_fm(16 + h, uT, n, wring, f_evac(fbt, n, 8 + h, 16 + h, "pool")))
                hd["parts"] = parts
                return hd

            def stageB(hd, n, nch, want_out, dirs=(0, 1)):
                tl = {}
                for dirn in dirs:
                    f = hd["f"][dirn]
                    P, rP, kin32, d1 = P_r.next(), rP_r.next(), kin_r.next(), d1_r[dirn].next()
                    tl[dirn] = (f, P, rP, kin32, d1)
                    hd[("P", dirn)] = P
                    pos = 0 if dirn == 0 else 127
                    cp("pool", fap(d1.t, pos, [[128, nch]]), fap(f.t, pos, [[128, nch]]), [f.b], [d1.b])
                for dirn in dirs:
                    f, P, rP, kin32, d1 = tl[dirn]
                    if dirn == 0:
                        o_ap, f_ap, d_ap = P.t[:, 0:n], f.t[:, 0:n], d1.t[:, 0:n]
                    else:
                        o_ap, f_ap, d_ap = (fap(P.t, n - 1, [[-1, n]]), fap(f.t, n - 1, [[-1, n]]), fap(d1.t, n - 1, [[-1, n]]))
                    k.op("dve", lambda e: e.tensor_tensor_scan(out=o_ap, data0=f_ap, data1=d_ap, initial=1.0,
                                                               op0=ALU.mult, op1=ALU.max), [f.b, d1.b], [P.b])
                    ts("pool", f.t[:, 0:n], f.t[:, 0:n], -1.0, 1.0, ALU.mult, ALU.add, [f.b], [f.b])
                for dirn in dirs:
                    f, P, rP, kin32, d1 = tl[dirn]
                    k.op("dve", lambda e: e.reciprocal(out=rP.t[:, 0:n], in_=P.t[:, 0:n]), [P.b], [rP.b])
                    if want_out:
                        qdec = qdec_r.next()
                        qraw = hd["qraw"]
                        stt(qdec.t[:, 0:n], qraw.t[:, 0:n], QSCALE, P.t[:, 0:n], ALU.mult, ALU.mult,
                            [qraw.b, P.b], [qdec.b])
                        hd[("qdec", dirn)] = qdec
                for dirn in dirs:
                    f, P, rP, kin32, d1 = tl[dirn]
                    tt("pool", kin32.t[:, 0:n], f.t[:, 0:n], rP.t[:, 0:n], ALU.mult, [f.b, rP.b], [kin32.b])
                    if want_out:
                        kinv = kinv_r.next()
                        cp("act", kinv.t[:, 0:n], kin32.t[:, 0:n], [kin32.b], [kinv.b])
                        hd[("kinv", dirn)] = kinv
                    kend = kend_r.next()
                    tt("pool", fap(kend.t, 0, [[128, nch], [1, 128]]), fap(kin32.t, 0, [[128, nch], [1, 128]]),
                       plast_bc(P, dirn, nch), ALU.mult, [kin32.b, P.b], [kend.b])
                    hd[("kend", dirn)] = kend

            def pe1(hd, nch, want_out, dirs=(0, 1)):
                if want_out:
                    for dirn, bank in ((0, ps[2]), (1, ps[3])):
                        kinv, qdec = hd[("kinv", dirn)], hd[("qdec", dirn)]
                        for c in range(nch):
                            sl = slice(c * 128, (c + 1) * 128)
                            mm(bank.t[:, sl], kinv.t[:, sl], qdec.t[:, sl], True, True, [kinv.b, qdec.b], [bank.b])
                kendT = kendT_r.next()
                hd["kendT"] = kendT
                for dirn in dirs:
                    kend = hd[("kend", dirn)]
                    for c in range(nch):
                        tr(trb.t[:, (dirn * nch + c) * 128:(dirn * nch + c + 1) * 128], kend.t[:, c * 128:(c + 1) * 128],
                           identb.t[:], [kend.b, identb.b], [trb.b])
                lo, hi = min(dirs) * nch * 128, (max(dirs) + 1) * nch * 128
                cp("act", fap(kendT.t, lo, [[1, hi - lo]]), trb.t[:, lo:hi], [trb.b], [kendT.b])

            def pe2(hd, nch, dirs=(0, 1)):
                h, kendT = hd["h"], hd["kendT"]
                for dirn in dirs:
                    bank = ps[4 + dirn]
                    for c in range(nch):
                        mm(bank.t[:, c * 128:(c + 1) * 128], kendT.t[:, dirn * nch + c, :],
                           vtok.t[:, c, h * 128:(h + 1) * 128], True, True, [kendT.b, vtok.b], [bank.b])

            cx = [xring.next() for _ in range(2)]
            for i in range(2):
                k.dma("sp", cx[i].t[:], ctx_d[i * 128:(i + 1) * 128, :], writes=[cx[i].b])
            uT = uT_r.next()
            make_uT(cx, uT, modc.t, scp1c.t, [modc.b, scp1c.b])
            v_proj(uT, 2, wv, vtok)
            for h in range(8):
                hd = startA(h, uT, CTXL, False)
                for p in hd["parts"]:
                    p()
                stageB(hd, CTXL, 2, False)
                pe1(hd, 2, False)
                pe2(hd, 2)
                Pf, Pb = hd[("P", 0)], hd[("P", 1)]
                for c in range(2):
                    stt(Sf[h].t[:], Sf[h].t[:], fap(Pf.t, c * 128 + 127, [[1, 1]]), ps[4].t[:, c * 128:(c + 1) * 128],
                        ALU.mult, ALU.add, [Sf[h].b, Pf.b, ps[4].b], [Sf[h].b])
                for c in (1, 0):
                    stt(Sb[h].t[:], Sb[h].t[:], fap(Pb.t, c * 128, [[1, 1]]), ps[5].t[:, c * 128:(c + 1) * 128],
                        ALU.mult, ALU.add, [Sb[h].b, Pb.b, ps[5].b], [Sb[h].b])
            for j in range(8):
                proj_fm(40 + j, uT, CTXL, wring,
                        lambda bank: act(craw.t[:, j, :], bank.t[:, 0:CTXL], AF.Identity, [bank.b, binT.b], [craw.b],
                                         bias=binT.t[:, 40 + j:41 + j]))
            cmb = mixer_bufs("cmb", 1)
            cdiag = sb(st, "cdiag", [128, 5, 128], F32)
            for j in range(8):
                cp("pool", crawj.t[:], craw.t[:, j, :], [craw.b], cmb["raw"] + cmb["B1"])

                def fin_ctx(dirn, Bd, BS):
                    if dirn == 0:
                        cp("dve", hcf.t[:, j:j + 1], Bd.t[:, CTXL - 1:CTXL], BS, [hcf.b])
                    else:
                        cp("dve", hcb.t[:, j:j + 1], Bd.t[:, 0:1], BS, [hcb.b])
                mixer_b(j, crawj, cxc, cxcb, cA_, cB_, CTXL, 1, CTXL, ctmp, cdiag, cmb, 0.0, 0.0, fin_ctx, wr_bf, wi_bf, 1)

            def load_uT(stile):
                t0 = stile * NT
                xs = [xring.next() for _ in range(NCH)]
                for i in range(NCH):
                    k.dma("sp", xs[i].t[:], x_d[t0 + i * 128:t0 + (i + 1) * 128, :], writes=[xs[i].b])
                u = uT_r.next()
                make_uT(xs, u, modx.t, scp1x.t, [modx.b, scp1x.b])
                return u

            uT = load_uT(0)
            for stile in range(NST):
                t0 = stile * NT
                uT_next = None
                if stile >= NST_OWN:
                    hd = startA(0, uT, NT, False, (1,))
                    hd["parts"][0]()
                    v_proj(uT, NCH, wv, vtok)
                    for h in range(8):
                        nxt = startA(h + 1, uT, NT, False, (1,)) if h < 7 else None
                        stageB(hd, NT, NCH, False, (1,))
                        cp("dve", dec_b.t[:, h, stile * NCH:(stile + 1) * NCH], plast(hd[("P", 1)], 1, NCH),
                           [hd[("P", 1)].b], [dec_b.b])
                        if nxt:
                            nxt["parts"][0]()
                        pe1(hd, NCH, False, (1,))
                        z5 = z5st_r.next()
                        proj_fm(40 + h, uT, NT, wring,
                                lambda bank: act(z5.t[:, :], bank.t[:, :], AF.Identity, [bank.b, binT.b], [z5.b],
                                                 bias=binT.t[:, 40 + h:41 + h]))
                        k.dma("act", XT[h, :, t0:t0 + NT], z5.t[:], reads=[z5.b])
                        pe2(hd, NCH, (1,))
                        ubst = ubst_r.next()
                        cp("act", fap(ubst.t, 0, [[1, NT]]), ps[5].t[:, :], [ps[5].b], [ubst.b])
                        k.dma("act", UB[h, :, stile * NCH:(stile + 1) * NCH, :], ubst.t[:], reads=[ubst.b])
                        if h == 5 and stile + 1 < NST:
                            uT_next = load_uT(stile + 1)
                        hd = nxt
                    uT = uT_next
                    continue
                hd = startA(0, uT, NT, True)
                for p in hd["parts"]:
                    p()
                v_proj(uT, NCH, wv, vtok)
                for h in range(8):
                    nxt = startA(h + 1, uT, NT, True) if h < 7 else None
                    stageB(hd, NT, NCH, True)
                    Pf, Pb = hd[("P", 0)], hd[("P", 1)]
                    decf = decf_r.next()
                    cp("dve", decf.t[:, :], plast(Pf, 0, NCH), [Pf.b], [decf.b])
                    cp("dve", dec_b.t[:, h, stile * NCH:(stile + 1) * NCH], plast(Pb, 1, NCH), [Pb.b], [dec_b.b])
                    if nxt:
                        nxt["parts"][0]()
                        nxt["parts"][1]()
                    pe1(hd, NCH, True)
                    t1, t2, scs = t1_r.next(), t2_r.next(), scs_r.next()
                    tt("dve", t1.t[:, :], ps[2].t[:, :], fap(maskf4.t, 0, [[1, 512]]), ALU.mult, [ps[2].b, maskf4.b], [t1.b])
                    tt("dve", t2.t[:, :], ps[3].t[:, :], fap(maskb4.t, 0, [[1, 512]]), ALU.mult, [ps[3].b, maskb4.b], [t2.b])
                    tt("pool", scs.t[:, :], t1.t[:, :], t2.t[:, :], ALU.add, [t1.b, t2.b], [scs.b])
                    if nxt:
                        nxt["parts"][2]()
                    pe2(hd, NCH)
                    ubst = ubst_r.next()
                    cp("act", fap(ubst.t, 0, [[1, NT]]), ps[5].t[:, :], [ps[5].b], [ubst.b])
                    k.dma("act", UB[h, :, stile * NCH:(stile + 1) * NCH, :], ubst.t[:], reads=[ubst.b])
                    sst, s32 = sst_r.next(), s32_r.next()
                    cp("pool", sst.t[:, 0, :], Sf[h].t[:], [Sf[h].b], [sst.b])
                    for c in range(NCH):
                        src = Sf[h].t[:] if c == 0 else s32.t[:, c - 1, :]
                        dst = Sf[h].t[:] if c == NCH - 1 else s32.t[:, c, :]
                        stt(dst, src, decf.t[:, c:c + 1], ps[4].t[:, c * 128:(c + 1) * 128], ALU.mult, ALU.add,
                            [Sf[h].b, s32.b, decf.b, ps[4].b], [Sf[h].b] if c == NCH - 1 else [s32.b])
                    cp("pool", fap(sst.t, 128, [[1, (NCH - 1) * 128]]), fap(s32.t, 0, [[1, (NCH - 1) * 128]]), [s32.b], [sst.b])
                    z5 = z5st_r.next()
                    proj_fm(40 + h, uT, NT, wring,
                            lambda bank: act(z5.t[:, :], bank.t[:, :], AF.Identity, [bank.b, binT.b], [z5.b],
                                             bias=binT.t[:, 40 + h:41 + h]))
                    k.dma("act", XT[h, :, t0:t0 + NT], z5.t[:], reads=[z5.b])
                    if h == 5 and stile + 1 < NST:
                        uT_next = load_uT(stile + 1)
                    qdf = hd[("qdec", 0)]
                    for c in range(NCH):
                        sl = slice(c * 128, (c + 1) * 128)
                        mm(ps[6].t[:, sl], vtok.t[:, c, h * 128:(h + 1) * 128], scs.t[:, sl], True, False,
                           [vtok.b, scs.b], [ps[6].b])
                        mm(ps[6].t[:, sl], sst.t[:, c, :], qdf.t[:, sl], False, True, [sst.b, qdf.b], [ps[6].b])
                    ost = ost_r.next()
                    cp("act", ost.t[:, :], ps[6].t[:, :], [ps[6].b], [ost.b])
                    k.dma("act", OP[h, :, t0:t0 + NT], ost.t[:], reads=[ost.b])
                    qdb = hd[("qdec", 1)]
                    k.dma("sp", QDB[h, :, t0:t0 + NT], qdb.t[:], reads=[qdb.b])
                    hd = nxt
                uT = uT_next
            k.barrier()

        with ExitStack() as st:
            raw = sb(st, "raw", [128, T], F32)
            xc = sb(st, "xc", [128, T], F32)
            xcb = sb(st, "xcb", [128, T], BF16)
            A = sb(st, "A", [128, T], F32)
            B = sb(st, "B", [128, T], F32)
            tmp = ring(st, "mtmp", [128, 512], F32, 10)
            diag = sb(st, "diag", [128, 5, 128], F32)
            wr_bf = sb(st, "wr_bf", [128, 2048], BF16)
            wi_bf = sb(st, "wi_bf", [128, 2048], BF16)
            k.dma("pool", wr_bf.t[:], wr_d[:, :], writes=[wr_bf.b])
            k.dma("pool", wi_bf.t[:], wi_d[:, :], writes=[wi_bf.b])
            xmb = mixer_bufs("xmb", T // 512)
            for j in range(8):
                for q4 in range(4):
                    k.dma("sp" if q4 % 2 == 0 else "act", raw.t[:, q4 * 2048:(q4 + 1) * 2048], XT[j, :, q4 * 2048:(q4 + 1) * 2048],
                          writes=[xmb["raw"][q4]] + xmb["B1"])

                def fin_x(dirn, Bd, BS):
                    if dirn == 1:
                        for q4 in range(2):
                            r0 = q4 * (ROWS // 4)
                            cm = [[1, ROWS // 4], [ROWS, GW]]
                            tt("dve" if q4 == 0 else "pool", fap(xc.t, r0 * GW, [[GW, ROWS // 4], [1, GW]]), fap(B.t, r0, cm),
                               fap(Bd.t, r0, cm), ALU.add, xmb["B0"] + xmb["B1"], xmb["xc"][q4 * 4:(q4 + 1) * 4])
                        for q4 in range(2):
                            k.dma("sp", HS[j, :, q4 * 2048:(q4 + 1) * 2048], xc.t[:, q4 * 2048:(q4 + 1) * 2048],
                                  reads=xmb["xc"][q4 * 4:(q4 + 1) * 4], key=xc.b)
                mixer_b(j, raw, xc, xcb, A, B, T, GW, 512, tmp, diag, xmb, hcf.t[:, j:j + 1], hcb.t[:, j:j + 1], fin_x,
                        wr_bf, wi_bf, 8)
            k.barrier()

        with ExitStack() as st:
            xring = ring(st, "xt2", [128, D], F32, 7)
            uT = sb(st, "uT2", [128, 8, NT], BF16)
            wring = ring(st, "w2", [128, 8, 128], BF16, 4)
            pa_bf = sb(st, "pa_bf", [128, 8, D], BF16)
            pb_bf = sb(st, "pb_bf", [128, 8, D], BF16)
            wo_bf = sb(st, "wo_bf", [128, 8, D], BF16)
            lng = sb(st, "lng", [128, D], F32)
            lnb = sb(st, "lnb", [128, D], F32)
            op_r = ring(st, "opl", [128, NT], F32, 2)
            qdb_r = ring(st, "qdbl", [128, NT], BF16, 2)
            ub_r = ring(st, "ubl", [128, NCH, 128], F32, 2)
            sbs_r = ring(st, "sbs", [128, NCH, 128], BF16, 2)
            s32_r = ring(st, "s32b", [128, NCH, 128], F32, 2)
            o_r = ring(st, "o", [128, NT], F32, 2)
            sq_r = ring(st, "sq", [128, NT], F32, 2)
            rs_r = ring(st, "rs", [128, NT], F32, 2)
            g4_r = ring(st, "g4", [128, NT], F32, 2)
            oaT = sb(st, "oaT", [128, 8, NT], BF16)
            obT = sb(st, "obT", [128, 8, NT], BF16)
            yT = sb(st, "yT", [128, 8, NT], BF16)
            h_r = ring(st, "hl", [128, NT], F32, 3)
            g6_r = ring(st, "g6", [128, NT], F32, 2)
            s7_r = ring(st, "s7", [128, NT], F32, 2)
            s8_r = ring(st, "s8", [128, NT], F32, 2)
            ta_r = ring(st, "ta", [128, NT], F32, 2)
            tb_r = ring(st, "tb", [128, NT], F32, 2)
            r_r = ring(st, "r", [128, D], F32, 2)
            st6_r = ring(st, "st6", [128, 12], F32, 2)
            mv_r = ring(st, "mv", [128, 4], F32, 2)
            rms_eps_t = sb(st, "rms_eps", [128, 1], F32)
            ln_eps_t = sb(st, "ln_eps", [128, 1], F32)
            k.op("dve", lambda e: e.memset(rms_eps_t.t[:], RMS_EPS), writes=[rms_eps_t.b])
            k.op("dve", lambda e: e.memset(ln_eps_t.t[:], LN_EPS), writes=[ln_eps_t.b])
            k.dma("sp", pa_bf.t[:], PAB[:, :, :], writes=[pa_bf.b])
            k.dma("sp", pb_bf.t[:], PBB[:, :, :], writes=[pb_bf.b])
            k.dma("sp", wo_bf.t[:], WOB[:, :, :], writes=[wo_bf.b])
            k.dma("sp", lng.t[:], lng_d[:, :], writes=[lng.b])
            k.dma("sp", lnb.t[:], lnb_d[:, :], writes=[lnb.b])
            for stile in range(NST - 1, -1, -1):
                t0 = stile * NT
                if stile >= NST_OWN:
                    for h in range(8):
                        ubl = ub_r.next()
                        k.dma("sp", ubl.t[:], UB[h, :, stile * NCH:(stile + 1) * NCH, :], writes=[ubl.b])
                        for c in range(NCH - 1, -1, -1):
                            stt(Sb[h].t[:], Sb[h].t[:], dec_b.t[:, h, stile * NCH + c:stile * NCH + c + 1], ubl.t[:, c, :],
                                ALU.mult, ALU.add, [Sb[h].b, dec_b.b, ubl.b], [Sb[h].b])
                    continue
                xs = [xring.next() for _ in range(NCH)]
                for i in range(NCH):
                    k.dma("sp", xs[i].t[:], x_d[t0 + i * 128:t0 + (i + 1) * 128, :], writes=[xs[i].b])
                def loads_rec(h):
                    opl, qdbl, ubl = op_r.next(), qdb_r.next(), ub_r.next()
                    k.dma("sp", opl.t[:], OP[h, :, t0:t0 + NT], writes=[opl.b])
                    k.dma("sp", qdbl.t[:], QDB[h, :, t0:t0 + NT], writes=[qdbl.b])
                    k.dma("sp", ubl.t[:], UB[h, :, stile * NCH:(stile + 1) * NCH, :], writes=[ubl.b])
                    sbs, s32 = sbs_r.next(), s32_r.next()
                    cp("pool", sbs.t[:, NCH - 1, :], Sb[h].t[:], [Sb[h].b], [sbs.b])
                    for c in range(NCH - 1, -1, -1):
                        src = Sb[h].t[:] if c == NCH - 1 else s32.t[:, c, :]
                        dst = Sb[h].t[:] if c == 0 else s32.t[:, c - 1, :]
                        stt(dst, src, dec_b.t[:, h, stile * NCH + c:stile * NCH + c + 1], ubl.t[:, c, :], ALU.mult, ALU.add,
                            [Sb[h].b, s32.b, dec_b.b, ubl.b], [Sb[h].b] if c == 0 else [s32.b])
                    cp("pool", fap(sbs.t, 0, [[1, (NCH - 1) * 128]]), fap(s32.t, 0, [[1, (NCH - 1) * 128]]), [s32.b], [sbs.b])
                    return opl, qdbl, sbs

                cur = loads_rec(0)
                make_uT(xs, uT, modx.t, scp1x.t, [modx.b, scp1x.b])
                for h in range(8):
                    opl, qdbl, sbs = cur
                    for c in range(NCH):
                        sl = slice(c * 128, (c + 1) * 128)
                        mm(ps[2].t[:, sl], sbs.t[:, c, :], qdbl.t[:, sl], True, True, [sbs.b, qdbl.b], [ps[2].b])
                    o = o_r.next()
                    tt("dve", o.t[:, :], ps[2].t[:, :], opl.t[:, :], ALU.add, [ps[2].b, opl.b], [o.b])
                    sq = sq_r.next()
                    act(sq.t[:, :], o.t[:, :], AF.Square, [o.b], [sq.b])
                    if h < 7:
                        cur = loads_rec(h + 1)
                    g4 = g4_r.next()
                    proj_fm(32 + h, uT, NT, wring, lambda bank: act(g4.t[:, :], bank.t[:, :], AF.Silu, [bank.b, binT.b],
                                                                     [g4.b], bias=binT.t[:, 32 + h:33 + h]))
                    mm(ps[3].t[:, :], ones32.t[:, :], sq.t[:, :], True, True, [ones32.b, sq.b], [ps[3].b])
                    rs = rs_r.next()
                    act(rs.t[:, :], ps[3].t[:, :], AF.Ln, [ps[3].b], [rs.b], scale=1.0 / 128.0, bias=rms_eps_t.t[:, 0:1])
                    act(rs.t[:, :], rs.t[:, :], AF.Exp, [rs.b], [rs.b], scale=-0.5)
                    stt(o.t[:, :], o.t[:, :], nag.t[:, 0:1], rs.t[:, :], ALU.mult, ALU.mult, [o.b, nag.b, rs.b], [o.b])
                    tt("pool", oaT.t[:, h, :], o.t[:, :], g4.t[:, :], ALU.mult, [o.b, g4.b], [oaT.b])
                for j in range(8):
                    hl, g6 = h_r.next(), g6_r.next()
                    k.dma("sp", hl.t[:], HS[j, :, t0:t0 + NT], writes=[hl.b])
                    proj_fm(48 + j, uT, NT, wring, lambda bank: act(g6.t[:, :], bank.t[:, :], AF.Silu, [bank.b, binT.b],
                                                                     [g6.b], bias=binT.t[:, 48 + j:49 + j]))
                    tt("pool", obT.t[:, j, :], hl.t[:, :], g6.t[:, :], ALU.mult, [hl.b, g6.b], [obT.b])
                for j in range(8):
                    s7, s8 = s7_r.next(), s8_r.next()
                    proj_fm(56 + j, uT, NT, wring, lambda bank: act(s7.t[:, :], bank.t[:, :], AF.Tanh, [bank.b, hbinT.b],
                                                                     [s7.b], scale=0.5, bias=hbinT.t[:, 56 + j:57 + j]))
                    proj_fm(64 + j, uT, NT, wring, lambda bank: act(s8.t[:, :], bank.t[:, :], AF.Tanh, [bank.b, hbinT.b],
                                                                     [s8.b], scale=0.5, bias=hbinT.t[:, 64 + j:65 + j]))
                    ta, tb = ta_r.next(), tb_r.next()
                    for kc in range(8):
                        mm(ps[4].t[:, :], pa_bf.t[:, kc, j * 128:(j + 1) * 128], oaT.t[:, kc, :], kc == 0, kc == 7,
                           [pa_bf.b, oaT.b], [ps[4].b])
                    stt(ta.t[:, :], s7.t[:, :], 1.0, ps[4].t[:, :], ALU.add, ALU.mult, [ps[4].b, s7.b], [ta.b])
                    for kc in range(8):
                        mm(ps[5].t[:, :], pb_bf.t[:, kc, j * 128:(j + 1) * 128], obT.t[:, kc, :], kc == 0, kc == 7,
                           [pb_bf.b, obT.b], [ps[5].b])
                    stt(tb.t[:, :], s8.t[:, :], 1.0, ps[5].t[:, :], ALU.add, ALU.mult, [ps[5].b, s8.b], [tb.b])
                    tt("pool", yT.t[:, j, :], ta.t[:, :], tb.t[:, :], ALU.add, [ta.b, tb.b], [yT.b])
                for i in range(NCH):
                    r = r_r.next()
                    for half in range(2):
                        bank = ps[6] if half == 0 else ps[3]
                        hs = slice(half * 512, (half + 1) * 512)
                        for kc in range(8):
                            mm(bank.t[:, :], yT.t[:, kc, i * 128:(i + 1) * 128], wo_bf.t[:, kc, hs], kc == 0, kc == 7,
                               [yT.b, wo_bf.b], [bank.b])
                        tt("dve", r.t[:, hs], bank.t[:, :], gt_bc.t[:, hs], ALU.mult, [bank.b, gt_bc.b], [r.b])
                    stt(r.t[:, :], xs[i].t[:, :], ALPHA, r.t[:, :], ALU.mult, ALU.add, [xs[i].b, r.b], [r.b])
                    st6, mv = st6_r.next(), mv_r.next()
                    k.op("dve", lambda e: e.bn_stats(out=st6.t[:, 0:6], in_=r.t[:, 0:512]), [r.b], [st6.b])
                    k.op("dve", lambda e: e.bn_stats(out=st6.t[:, 6:12], in_=r.t[:, 512:1024]), [r.b], [st6.b])
                    k.op("dve", lambda e: e.bn_aggr(out=mv.t[:, 0:2], in_=st6.t[:, 0:12]), [st6.b], [mv.b])
                    act(mv.t[:, 2:3], mv.t[:, 1:2], AF.Ln, [mv.b], [mv.b], scale=1.0, bias=ln_eps_t.t[:, 0:1])
                    act(mv.t[:, 2:3], mv.t[:, 2:3], AF.Exp, [mv.b], [mv.b], scale=-0.5)
                    stt(mv.t[:, 3:4], mv.t[:, 0:1], -1.0, mv.t[:, 2:3], ALU.mult, ALU.mult, [mv.b], [mv.b])
                    act(r.t[:, :], r.t[:, :], AF.Identity, [r.b, mv.b], [r.b], scale=mv.t[:, 2:3], bias=mv.t[:, 3:4])
                    tt("dve", r.t[:, :], r.t[:, :], lng.t[:, :], ALU.mult, [r.b, lng.b], [r.b])
                    tt("pool", r.t[:, :], r.t[:, :], lnb.t[:, :], ALU.add, [r.b, lnb.b], [r.b])
                    k.dma("pool", out_d[t0 + i * 128:t0 + (i + 1) * 128, :], r.t[:, :], reads=[r.b])
            k.barrier()
        if debug:
            print("instr counts", {n: e.n for n, e in k.engs.items()}, "waits", k.nwaits)
    return nc


def _fm(v, n):
    return np.ascontiguousarray(np.asarray(v, np.float32).reshape(n, 128).T)


_CACHE = {}


def _shared_inputs(w_mod, b_mod, w_in, b_in, lb_logits, norm_a_g, conv_w, conv_b, w_r, b_r, w_i, b_i, lam,
                   p_a, p_b, w_out, ln_g, ln_b, flip):
    f = np.float32
    w_in0 = np.asarray(w_in, f)[0]
    w4 = w_in0.reshape(8, 128, 72, 128).transpose(2, 1, 0, 3)
    b_in0 = np.asarray(b_in, f)[0]
    binT = _fm(b_in0, 72)
    lbl = np.asarray(lb_logits, f).reshape(2, 2, 8, 128)
    wr = np.asarray(w_r, f)[0]
    wi = np.asarray(w_i, f)[0]
    br_ = np.asarray(b_r, f)[0]
    bi_ = np.asarray(b_i, f)[0]
    lam_ = np.asarray(lam, f)[0]
    cw = np.asarray(conv_w, f)[0]
    z = np.zeros_like(cw[0])
    if flip:
        order = list(range(0, 8)) + list(range(16, 24)) + list(range(8, 16)) + list(range(24, 72))
        w4 = w4[order]
        binT = binT[:, order]
        lbl = lbl[:, ::-1]
        wr, wi, br_, bi_, lam_ = wr[::-1], wi[::-1], br_[::-1], bi_[::-1], lam_[::-1]
        taps = np.stack([cw[3], cw[2], cw[1], cw[0], z], axis=0)
    else:
        taps = np.stack([z, cw[0], cw[1], cw[2], cw[3]], axis=0)
    return {
        "w_mod": np.ascontiguousarray(np.asarray(w_mod, f)[0]),
        "bmodT": _fm(np.asarray(b_mod, f)[0], 24),
        "bmod_row": np.ascontiguousarray(np.asarray(b_mod, f)[0][None, :]),
        "w4": np.ascontiguousarray(w4),
        "binT": np.ascontiguousarray(binT),
        "bv_row": np.ascontiguousarray(b_in0[None, 3072:4096]),
        "lbl": np.ascontiguousarray(lbl.transpose(3, 0, 1, 2).reshape(128, 32)),
        "nag": np.ascontiguousarray(np.asarray(norm_a_g, f)[0].reshape(128, 1)),
        "convw": np.ascontiguousarray(taps.reshape(5, 8, 128).transpose(2, 1, 0).reshape(128, 40)),
        "convb": _fm(np.asarray(conv_b, f)[0], 8),
        "wr": np.ascontiguousarray(wr.transpose(2, 0, 1, 3).reshape(128, 2048)),
        "wi": np.ascontiguousarray(wi.transpose(2, 0, 1, 3).reshape(128, 2048)),
        "br": _fm(np.ascontiguousarray(br_).reshape(-1), 16),
        "bi": _fm(np.ascontiguousarray(bi_).reshape(-1), 16),
        "lam": _fm(np.ascontiguousarray(lam_).reshape(-1), 16),
        "pa": np.ascontiguousarray(np.asarray(p_a, f)[0].reshape(8, 128, D).transpose(1, 0, 2)),
        "pb": np.ascontiguousarray(np.asarray(p_b, f)[0].reshape(8, 128, D).transpose(1, 0, 2)),
        "wo": np.ascontiguousarray(np.asarray(w_out, f)[0].reshape(8, 128, D).transpose(1, 0, 2)),
        "lng": np.ascontiguousarray(np.broadcast_to(np.asarray(ln_g, f)[0][None, :], (128, D))),
        "lnb": np.ascontiguousarray(np.broadcast_to(np.asarray(ln_b, f)[0][None, :], (128, D))),
        "ident": np.eye(128, dtype=f),
        "maskf": np.triu(np.ones((128, 128), f)),
        "maskb": np.tril(np.ones((128, 128), f)),
    }


def kernel(x, c, ctx, c_ctx, w_mod, b_mod, w_in, b_in, lb_logits, norm_a_g, conv_w, conv_b,
           w_r, b_r, w_i, b_i, lam, p_a, p_b, w_out, ln_g, ln_b):
    f = np.float32
    x = np.asarray(x, f); ctx = np.asarray(ctx, f); c = np.asarray(c, f); c_ctx = np.asarray(c_ctx, f)
    params = (w_mod, b_mod, w_in, b_in, lb_logits, norm_a_g, conv_w, conv_b, w_r, b_r, w_i, b_i, lam,
              p_a, p_b, w_out, ln_g, ln_b)
    shared = [_shared_inputs(*params, flip=False), _shared_inputs(*params, flip=True)]
    in_maps = []
    for core in range(8):
        b, half = core // 2, core % 2
        m = dict(shared[half])
        if half == 0:
            m["x"] = np.ascontiguousarray(x[b])
            m["ctx"] = np.ascontiguousarray(ctx[b])
        else:
            m["x"] = np.ascontiguousarray(x[b][::-1])
            m["ctx"] = np.ascontiguousarray(ctx[b][::-1])
        m["cvec"] = np.ascontiguousarray(np.concatenate([_fm(c[b], 8), _fm(c_ctx, 8)], axis=1))
        in_maps.append(m)
    debug = bool(os.environ.get("MK_DEBUG"))
    key = ("nc", debug)
    if key not in _CACHE:
        _CACHE[key] = build_program(debug)
    nc = _CACHE[key]
    res = run_bass_kernel_spmd(nc, in_maps, core_ids=list(range(8)))
    if debug:
        _CACHE["last"] = res
    out = np.empty((4, T, D), f)
    for b in range(4):
        out[b, :T_OWN] = np.asarray(res.results[2 * b]["out"], f)
        out[b, T_OWN:] = np.asarray(res.results[2 * b + 1]["out"], f)[::-1]
    return out
```

```python
import os
import numpy as np
from contextlib import ExitStack
import concourse.bass as bass
import concourse.mybir as mybir
from concourse.bass_utils import run_bass_kernel_spmd

F32 = mybir.dt.float32
BF16 = mybir.dt.bfloat16
ALU = mybir.AluOpType
AF = mybir.ActivationFunctionType

D = 1024
T = 8192
NT = 512
NST = T // NT
T_OWN = T // 2
NST_OWN = T_OWN // NT
NCH = NT // 128
CTXL = 256
GW = 64
ROWS = T // GW
QSCALE = 128 ** -0.5
ALPHA = 2.0 ** 0.25
LN_EPS = 1e-5
RMS_EPS = 1e-6


class Buf:
    __slots__ = ("name", "last_w", "readers", "dma_sem", "dma_cnt")

    def __init__(self, name):
        self.name = name
        self.last_w = None
        self.readers = {}
        self.dma_sem = None
        self.dma_cnt = 0


class Eng:
    def __init__(self, name, h, sem):
        self.name, self.h, self.sem, self.n = name, h, sem, 0
        self.seen = {}


class K:
    SAME_ENG_SYNC = True

    def __init__(self, nc, stack):
        self.nc = nc
        self.stack = stack
        self.engs = {}
        for name, h in (("pe", nc.tensor), ("act", nc.scalar), ("dve", nc.vector),
                        ("pool", nc.gpsimd), ("sp", nc.sync)):
            sem = stack.enter_context(nc.semaphore("s_" + name))
            self.engs[name] = Eng(name, h, sem)
        self.bufs = []
        self.nwaits = 0
        self.free_sems = []

    def buf(self, name):
        b = Buf(name)
        self.bufs.append(b)
        return b

    def _wait(self, E, deps):
        need = {}
        for sem, val in deps:
            if val <= 0:
                continue
            if need.get(id(sem), (None, 0))[1] < val:
                need[id(sem)] = (sem, val)
        for sem, val in need.values():
            if sem is E.sem:
                if E.name == "pe" or not self.SAME_ENG_SYNC:
                    continue
            if E.seen.get(id(sem), 0) >= val:
                continue
            E.h.wait_ge(sem, val)
            self.nwaits += 1
            E.seen[id(sem)] = val

    def _deps(self, reads, writes):
        deps = []
        for b in reads:
            if b.last_w:
                deps.append(b.last_w)
        for b in writes:
            if b.last_w:
                deps.append(b.last_w)
            deps.extend(b.readers.values())
        return deps

    def _record(self, ev, reads, writes):
        sem, val = ev
        for b in reads:
            if b.readers.get(id(sem), (None, 0))[1] < val:
                b.readers[id(sem)] = ev
        for b in writes:
            b.last_w = ev
            b.readers = {}

    def op(self, e, emit, reads=(), writes=()):
        E = self.engs[e]
        self._wait(E, self._deps(reads, writes))
        ins = emit(E.h)
        E.n += 1
        ins.then_inc(E.sem, 1)
        self._record((E.sem, E.n), reads, writes)
        return ins

    def dma(self, q, out, in_, reads=(), writes=(), key=None, **kw):
        E = self.engs[q]
        kb = key if key is not None else (writes[0] if writes else reads[0])
        if kb.dma_sem is None:
            kb.dma_sem = self.stack.enter_context(self.nc.semaphore("d_" + kb.name))
        deps = self._deps(reads, writes)
        if kb.dma_cnt:
            deps.append((kb.dma_sem, kb.dma_cnt))
        self._wait(E, deps)
        ins = E.h.dma_start(out=out, in_=in_, **kw)
        kb.dma_cnt += 16
        ins.then_inc(kb.dma_sem, 16)
        self._record((kb.dma_sem, kb.dma_cnt), reads, writes)
        return ins

    def barrier(self, skip=()):
        sp = self.engs["sp"]
        deps = [(E.sem, E.n) for E in self.engs.values() if E is not sp]
        deps += [(b.dma_sem, b.dma_cnt) for b in self.bufs if b.dma_sem is not None and b not in skip]
        self._wait(sp, deps)
        ins = sp.h.nop()
        sp.n += 1
        ins.then_inc(sp.sem, 1)
        for E in self.engs.values():
            if E is not sp:
                self._wait(E, [(sp.sem, sp.n)])
        for b in self.bufs:
            b.last_w = None
            b.readers = {}


class Tl:
    __slots__ = ("t", "b")

    def __init__(self, t, b):
        self.t, self.b = t, b


class Ring:
    def __init__(self, tiles):
        self.tiles, self.i = tiles, 0

    def next(self):
        t = self.tiles[self.i % len(self.tiles)]
        self.i += 1
        return t


def fap(t, off, dims):
    base = t[:]
    return bass.AP(tensor=base.tensor, offset=base.offset + off,
                   ap=[list(base.ap[0])] + [list(d) for d in dims])


def build_program(debug=False):
    nc = bass.Bass("TRN2", target_bir_lowering=False)

    def inp(name, shape, dt=F32):
        return nc.dram_tensor(name, shape, dt, kind="ExternalInput").ap()

    def scratch(name, shape, dt):
        return nc.dram_tensor(name, shape, dt, kind=("ExternalOutput" if debug else "Internal")).ap()

    x_d = inp("x", [T, D])
    ctx_d = inp("ctx", [CTXL, D])
    cvec_d = inp("cvec", [128, 16])
    wmod_d = inp("w_mod", [D, 3 * D])
    bmodT_d = inp("bmodT", [128, 24])
    bmodr_d = inp("bmod_row", [1, 3 * D])
    w4_d = inp("w4", [72, 128, 8, 128])
    binT_d = inp("binT", [128, 72])
    bv_d = inp("bv_row", [1, D])
    lbl_d = inp("lbl", [128, 32])
    nag_d = inp("nag", [128, 1])
    convw_d = inp("convw", [128, 40])
    convb_d = inp("convb", [128, 8])
    wr_d = inp("wr", [128, 2048])
    wi_d = inp("wi", [128, 2048])
    br_d = inp("br", [128, 16])
    bi_d = inp("bi", [128, 16])
    lam_d = inp("lam", [128, 16])
    pa_d = inp("pa", [128, 8, D])
    pb_d = inp("pb", [128, 8, D])
    wo_d = inp("wo", [128, 8, D])
    lng_d = inp("lng", [128, D])
    lnb_d = inp("lnb", [128, D])
    ident_d = inp("ident", [128, 128])
    maskf_d = inp("maskf", [128, 128])
    maskb_d = inp("maskb", [128, 128])
    out_d = nc.dram_tensor("out", [T_OWN, D], F32, kind="ExternalOutput").ap()

    WB = scratch("wb_s", [72, 128, 8, 128], BF16)
    PAB = scratch("pab_s", [128, 8, D], BF16)
    PBB = scratch("pbb_s", [128, 8, D], BF16)
    WOB = scratch("wob_s", [128, 8, D], BF16)
    XT = scratch("xt_s", [8, 128, T], F32)
    OP = scratch("op_s", [8, 128, T], F32)
    QDB = scratch("qdb_s", [8, 128, T], BF16)
    UB = scratch("ub_s", [8, 128, T // 128, 128], F32)
    HS = scratch("h_s", [8, 128, T], F32)
    if debug:
        DBG = nc.dram_tensor("dbg", [128, 4096], F32, kind="ExternalOutput").ap()

    with ExitStack() as pst:
        k = K(nc, pst)
        cnt = [0]

        def sb(stack, name, shape, dt):
            cnt[0] += 1
            nm = f"{name}_{cnt[0]}"
            return Tl(stack.enter_context(nc.sbuf_tensor(nm, shape, dt)), k.buf(nm))

        def ring(stack, name, shape, dt, n):
            return Ring([sb(stack, name, shape, dt) for _ in range(n)])

        ps = []
        for i in range(7):
            ps.append(Tl(pst.enter_context(nc.psum_tensor(f"ps{i}", [128, 512], F32)), k.buf(f"ps{i}")))
        trb = Tl(pst.enter_context(nc.psum_tensor("trb", [128, 1024], BF16)), k.buf("trb"))
        pj = Ring([ps[0], ps[1]])

        def act(out, in_, func, reads, writes, **kw):
            k.op("act", lambda e: e.activation(out=out, in_=in_, func=func, **kw), reads, writes)

        def tt(eng, out, in0, in1, op, reads, writes):
            k.op(eng, lambda e: e.tensor_tensor(out=out, in0=in0, in1=in1, op=op), reads, writes)

        def ts(eng, out, in0, s1, s2, op0, op1, reads, writes):
            if s2 is None:
                k.op(eng, lambda e: e.tensor_scalar(out=out, in0=in0, scalar1=s1, scalar2=None, op0=op0), reads, writes)
            else:
                k.op(eng, lambda e: e.tensor_scalar(out=out, in0=in0, scalar1=s1, scalar2=s2, op0=op0, op1=op1), reads, writes)

        def stt(out, in0, scalar, in1, op0, op1, reads, writes):
            k.op("dve", lambda e: e.scalar_tensor_tensor(out=out, in0=in0, scalar=scalar, in1=in1, op0=op0, op1=op1),
                 reads, writes)

        def cp(eng, out, in_, reads, writes):
            if eng == "act":
                k.op("act", lambda e: e.activation(out=out, in_=in_, func=AF.Identity), reads, writes)
            else:
                k.op(eng, lambda e: e.tensor_copy(out=out, in_=in_), reads, writes)

        def mm(out, lhsT, rhs, start, stop, reads, writes):
            k.op("pe", lambda e: e.matmul(out, lhsT=lhsT, rhs=rhs, start=start, stop=stop), reads, writes)

        def tr(out, in_, ident, reads, writes):
            k.op("pe", lambda e: e.transpose(out, in_, ident), reads, writes)

        ident32 = sb(pst, "ident32", [128, 128], F32)
        identb = sb(pst, "identb", [128, 128], BF16)
        ones32 = sb(pst, "ones32", [128, 128], F32)
        zeros32 = sb(pst, "zeros32", [128, 128], F32)
        onesb = sb(pst, "onesb", [1, 128], BF16)
        modx = sb(pst, "modx", [128, 24], F32)
        modc = sb(pst, "modc", [128, 24], F32)
        scp1x = sb(pst, "scp1x", [128, 8], F32)
        scp1c = sb(pst, "scp1c", [128, 8], F32)
        gt_bc = sb(pst, "gt_bc", [128, D], F32)
        lb = sb(pst, "lb", [128, 16], F32)
        oml = sb(pst, "oml", [128, 16], F32)
        binT = sb(pst, "binT", [128, 72], F32)
        bv_bf = sb(pst, "bv_bf", [1, D], BF16)
        nag = sb(pst, "nag", [128, 1], F32)
        convw = sb(pst, "convw", [128, 40], F32)
        convb = sb(pst, "convb", [128, 8], F32)
        br = sb(pst, "br", [128, 16], F32)
        bi = sb(pst, "bi", [128, 16], F32)
        cA = sb(pst, "cA", [128, 16], F32)
        hcA = sb(pst, "hcA", [128, 16], F32)
        hbr = sb(pst, "hbr", [128, 16], F32)
        hbi = sb(pst, "hbi", [128, 16], F32)
        fc0 = sb(pst, "fc0", [128, 16], F32)
        fc1 = sb(pst, "fc1", [128, 16], F32)
        hbinT = sb(pst, "hbinT", [128, 72], F32)
        Sb = [sb(pst, f"Sb{h}", [128, 128], F32) for h in range(8)]
        dec_b = sb(pst, "dec_b", [128, 8, T // 128], F32)
        hcf = sb(pst, "hcf", [128, 8], F32)
        hcb = sb(pst, "hcb", [128, 8], F32)
        dcast = k.buf("dcast")

        with ExitStack() as st:
            k.dma("sp", ident32.t[:], ident_d[:, :], writes=[ident32.b])
            k.op("dve", lambda e: e.memset(ones32.t[:], 1.0), writes=[ones32.b])
            k.op("dve", lambda e: e.memset(zeros32.t[:], 0.0), writes=[zeros32.b])
            k.op("dve", lambda e: e.memset(onesb.t[:], 1.0), writes=[onesb.b])
            cp("dve", identb.t[:], ident32.t[:], [ident32.b], [identb.b])

            cvec = sb(st, "cvec", [128, 16], F32)
            cs = sb(st, "cs", [128, 16], F32)
            lbl = sb(st, "lbl", [128, 32], F32)
            lam = sb(st, "lam", [128, 16], F32)
            bmodT = sb(st, "bmodT", [128, 24], F32)
            bmodr = sb(st, "bmodr", [1, 3 * D], F32)
            gt_row = sb(st, "gt_row", [1, D], F32)
            bv32 = sb(st, "bv32", [1, D], F32)
            wmod = sb(st, "wmod", [128, 8, 3 * D], F32)
            tmp16 = sb(st, "tmp16", [128, 16], F32)
            tmp16b = sb(st, "tmp16b", [128, 16], F32)
            for tl, src in ((cvec, cvec_d), (lbl, lbl_d), (lam, lam_d), (bmodT, bmodT_d), (bmodr, bmodr_d),
                            (binT, binT_d), (nag, nag_d), (convw, convw_d), (convb, convb_d), (br, br_d),
                            (bi, bi_d), (bv32, bv_d)):
                k.dma("sp", tl.t[:], src[:, :], writes=[tl.b])
            for kc in range(8):
                k.dma("sp" if kc % 2 == 0 else "act", wmod.t[:, kc, :], wmod_d[kc * 128:(kc + 1) * 128, :], writes=[wmod.b])
            late = []
            for g in (1, 2, 3, 5, 0, 4, 6, 7, 8):
                kb = k.buf(f"dcast{g}")
                k.dma("pool", WB[g * 8:(g + 1) * 8], w4_d[g * 8:(g + 1) * 8], key=kb, reads=[wmod.b])
                if g in (4, 6, 7, 8):
                    late.append(kb)
            for nm_, dst, src in (("dcpa", PAB, pa_d), ("dcpb", PBB, pb_d), ("dcwo", WOB, wo_d)):
                kb = k.buf(nm_)
                k.dma("pool", dst[:, :, :], src[:, :, :], key=kb, reads=[wmod.b])
                late.append(kb)
            act(cs.t[:], cvec.t[:], AF.Silu, [cvec.b], [cs.b])
            for oc in range(24):
                for kc in range(8):
                    mm(ps[0].t[:, 2 * oc:2 * oc + 2], wmod.t[:, kc, oc * 128:(oc + 1) * 128],
                       fap(cs.t, kc, [[8, 2]]), kc == 0, kc == 7, [wmod.b, cs.b], [ps[0].b])
            tt("dve", modx.t[:], fap(ps[0].t, 0, [[2, 24]]), bmodT.t[:], ALU.add, [ps[0].b, bmodT.b], [modx.b])
            tt("dve", modc.t[:], fap(ps[0].t, 1, [[2, 24]]), bmodT.t[:], ALU.add, [ps[0].b, bmodT.b], [modc.b])
            ts("dve", scp1x.t[:], modx.t[:, 8:16], 1.0, None, ALU.add, None, [modx.b], [scp1x.b])
            ts("dve", scp1c.t[:], modc.t[:, 8:16], 1.0, None, ALU.add, None, [modc.b], [scp1c.b])
            for half in range(2):
                pr = ps[1 + half]
                for kc in range(8):
                    mm(pr.t[0:1, :], cs.t[:, kc:kc + 1], wmod.t[:, kc, 2048 + half * 512:2048 + (half + 1) * 512],
                       kc == 0, kc == 7, [wmod.b, cs.b], [pr.b])
                tt("dve", gt_row.t[0:1, half * 512:(half + 1) * 512], pr.t[0:1, :],
                   bmodr.t[0:1, 2048 + half * 512:2048 + (half + 1) * 512], ALU.add, [pr.b, bmodr.b], [gt_row.b])
            for half in range(2):
                pr = ps[3 + half]
                mm(pr.t[:, :], ones32.t[0:1, :], gt_row.t[0:1, half * 512:(half + 1) * 512], True, True,
                   [ones32.b, gt_row.b], [pr.b])
                act(gt_bc.t[:, half * 512:(half + 1) * 512], pr.t[:, :], AF.Identity, [pr.b], [gt_bc.b], scale=0.5)
            tt("dve", tmp16.t[:], lbl.t[:, 0:16], lbl.t[:, 16:32], ALU.subtract, [lbl.b], [tmp16.b])
            act(lb.t[:], tmp16.t[:], AF.Sigmoid, [tmp16.b], [lb.b])
            ts("dve", oml.t[:], lb.t[:], -1.0, 1.0, ALU.mult, ALU.add, [lb.b], [oml.b])
            act(tmp16.t[:], lam.t[:], AF.Exp, [lam.b], [tmp16.b], scale=-1.0)
            ts("dve", tmp16b.t[:], tmp16.t[:], 1.0 / 3.0, -0.5, ALU.mult, ALU.add, [tmp16.b], [tmp16b.b])
            tt("dve", tmp16b.t[:], tmp16b.t[:], tmp16.t[:], ALU.mult, [tmp16.b, tmp16b.b], [tmp16b.b])
            ts("dve", tmp16b.t[:], tmp16b.t[:], 1.0, None, ALU.add, None, [tmp16b.b], [tmp16b.b])
            tt("dve", tmp16b.t[:], tmp16b.t[:], tmp16.t[:], ALU.mult, [tmp16.b, tmp16b.b], [tmp16b.b])
            ts("dve", cA.t[:], tmp16b.t[:], -8.0, None, ALU.mult, None, [tmp16b.b], [cA.b])
            ts("dve", hcA.t[:], cA.t[:], 0.5, None, ALU.mult, None, [cA.b], [hcA.b])
            ts("dve", hbr.t[:], br.t[:], 0.5, None, ALU.mult, None, [br.b], [hbr.b])
            ts("dve", hbi.t[:], bi.t[:], 0.5, None, ALU.mult, None, [bi.b], [hbi.b])
            ts("dve", hbinT.t[:], binT.t[:], 0.5, None, ALU.mult, None, [binT.b], [hbinT.b])
            ts("dve", fc1.t[:], oml.t[:], 0.5, None, ALU.mult, None, [oml.b], [fc1.b])
            tt("dve", fc0.t[:], fc1.t[:], lb.t[:], ALU.add, [fc1.b, lb.b], [fc0.b])
            cp("dve", bv_bf.t[:], bv32.t[:], [bv32.b], [bv_bf.b])
            for h in range(8):
                k.op("dve", lambda e: e.memset(Sb[h].t[:], 0.0), writes=[Sb[h].b])
            k.barrier(skip=late)

        def make_uT(xtiles, uT, sh_t, scp1_t, nb_reads):
            n = len(xtiles) * 128
            for j in range(8):
                bank = pj.next()
                for i, xt in enumerate(xtiles):
                    tr(bank.t[:, i * 128:(i + 1) * 128], xt.t[:, j * 128:(j + 1) * 128], ident32.t[:],
                       [xt.b, ident32.b], [bank.b])
                act(uT.t[:, j, 0:n], bank.t[:, 0:n], AF.Identity, [bank.b] + nb_reads, [uT.b],
                    scale=scp1_t[:, j:j + 1], bias=sh_t[:, j:j + 1])

        def proj_fm(cb, uT, n, wring, evac):
            w = wring.next()
            k.dma("sp", w.t[:], WB[cb], writes=[w.b])
            bank = pj.next()
            for kc in range(8):
                mm(bank.t[:, 0:n], w.t[:, kc, :], uT.t[:, kc, 0:n], kc == 0, kc == 7, [w.b, uT.b], [bank.b])
            evac(bank)

        def v_proj(uT, nch, wv, vtok):
            for c in range(nch):
                for half in range(2):
                    bank = pj.next()
                    for kc in range(8):
                        mm(bank.t[:, :], uT.t[:, kc, c * 128:(c + 1) * 128],
                           fap(wv.t, half * 4 * 1024 + kc * 128, [[1024, 4], [1, 128]]), kc == 0, False,
                           [uT.b, wv.b], [bank.b])
                    mm(bank.t[:, :], onesb.t[0:1, :], bv_bf.t[0:1, half * 512:(half + 1) * 512], False, True,
                       [onesb.b, bv_bf.b], [bank.b])
                    cp("act", vtok.t[:, c, half * 512:(half + 1) * 512], bank.t[:, :], [bank.b], [vtok.b])

        def gla_local(f, P, rP, kin32, d1, dirn, nch, n):
            pos = 0 if dirn == 0 else 127
            cp("dve", fap(d1.t, pos, [[128, nch]]), fap(f.t, pos, [[128, nch]]), [f.b], [d1.b])
            if dirn == 0:
                o_ap, f_ap, d_ap = P.t[:, 0:n], f.t[:, 0:n], d1.t[:, 0:n]
            else:
                o_ap, f_ap, d_ap = (fap(P.t, n - 1, [[-1, n]]), fap(f.t, n - 1, [[-1, n]]), fap(d1.t, n - 1, [[-1, n]]))
            k.op("dve", lambda e: e.tensor_tensor_scan(out=o_ap, data0=f_ap, data1=d_ap, initial=1.0,
                                                       op0=ALU.mult, op1=ALU.max), [f.b, d1.b], [P.b])
            k.op("dve", lambda e: e.reciprocal(out=rP.t[:, 0:n], in_=P.t[:, 0:n]), [P.b], [rP.b])
            ts("pool", f.t[:, 0:n], f.t[:, 0:n], -1.0, 1.0, ALU.mult, ALU.add, [f.b], [f.b])
            tt("pool", kin32.t[:, 0:n], f.t[:, 0:n], rP.t[:, 0:n], ALU.mult, [f.b, rP.b], [kin32.b])

        def plast_bc(P, dirn, nch):
            return fap(P.t, 127 if dirn == 0 else 0, [[128, nch], [0, 128]])

        def plast(P, dirn, nch):
            return fap(P.t, 127 if dirn == 0 else 0, [[128, nch]])

        def f_evac(ftile, n, idx, cbidx, eng2):
            def ev(bank):
                act(ftile.t[:, 0:n], bank.t[:, 0:n], AF.Tanh, [bank.b, hbinT.b], [ftile.b],
                    scale=0.5, bias=hbinT.t[:, cbidx:cbidx + 1])
                ts(eng2, ftile.t[:, 0:n], ftile.t[:, 0:n], fc1.t[:, idx:idx + 1], fc0.t[:, idx:idx + 1],
                   ALU.mult, ALU.add, [ftile.b, fc1.b, fc0.b], [ftile.b])
            return ev

        def mixer_bufs(name, nsub):
            return {kk: [k.buf(f"{name}_{kk}{i}") for i in range(nsub)] for kk in ("xc", "xb", "A", "B0", "B1")} | {"raw": [k.buf(f"{name}_raw{i}") for i in range(4)]}

        def mixer_b(j, raw, xc, xcb, A, B, Tn, W, sub, tmp, diag, mb, h0f, h0b, fin, wr_bf, wi_bf, G):
            rows = Tn // W
            nr = sub // W
            nsub = Tn // sub

            def pm(t, s0):
                return fap(t, s0 // W, [[1, nr], [rows, W]])

            def rv(t, s0):
                return fap(t, s0, [[W, nr], [1, W]])

            for i in range(5):
                ts("pool", diag.t[:, i, :], ident32.t[:, :], convw.t[:, j * 5 + i:j * 5 + i + 1], None, ALU.mult, None,
                   [ident32.b, convw.b], [diag.b])
            csz = max(Tn // 4, 1)

            def rawb(lo_, hi_):
                return mb["raw"][max(lo_, 0) // csz:min((min(hi_, Tn) - 1) // csz, 3) + 1]

            def front(si):
                s0 = si * sub
                rb = rawb(s0 - 2 * W, s0 + sub + 2 * W)
                bank = pj.next()
                order = [2, 1, 3]
                for n_, i in enumerate(order):
                    off = (i - 2) * W
                    lo, hi = max(s0, -off), min(s0 + sub, Tn - off)
                    mm(bank.t[:, lo - s0:hi - s0], diag.t[:, i, :], raw.t[:, lo + off:hi + off], n_ == 0, n_ == 2,
                       [diag.b] + rb, [bank.b])
                act(xc.t[:, s0:s0 + sub], bank.t[:, 0:sub], AF.Identity, [bank.b, convb.b], [mb["xc"][si]],
                    bias=convb.t[:, j:j + 1])
                for i in (0, 4):
                    off = (i - 2) * W
                    lo, hi = max(s0, -off), min(s0 + sub, Tn - off)
                    stt(xc.t[:, lo:hi], raw.t[:, lo + off:hi + off], convw.t[:, j * 5 + i:j * 5 + i + 1], xc.t[:, lo:hi],
                        ALU.mult, ALU.add, rb + [convw.b, mb["xc"][si]], [mb["xc"][si]])
                cp("pool", xcb.t[:, s0:s0 + sub], xc.t[:, s0:s0 + sub], [mb["xc"][si]], [mb["xb"][si]])

            for dirn in range(2):
                Bd = B if dirn == 0 else raw
                BS = mb["B0"] if dirn == 0 else mb["B1"]
                AS = mb["A"]
                gi = dirn * 8 + j
                def igate(si):
                    s0 = si * sub
                    pi = ps[4 + si % 2]
                    mm(pi.t[:, 0:sub], wi_bf.t[:, gi * 128:(gi + 1) * 128], xcb.t[:, s0:s0 + sub], True, True,
                       [wi_bf.b, mb["xb"][si]], [pi.b])
                    act(pm(Bd.t, s0), rv(pi.t, 0), AF.Tanh, [pi.b, hbi.b], [BS[si]] + (mb["raw"] if dirn == 1 else []),
                        scale=0.5, bias=hbi.t[:, gi:gi + 1])

                if dirn == 1:
                    for si in range(nsub):
                        igate(si)
                for g0 in range(0, nsub, G):
                    grp = list(range(g0, min(nsub, g0 + G)))
                    tms = {}
                    if dirn == 0:
                        for si in grp:
                            front(si)
                    for si in grp:
                        s0 = si * sub
                        sl = slice(s0, s0 + sub)
                        pr = ps[2 + si % 2]
                        mm(pr.t[:, 0:sub], wr_bf.t[:, gi * 128:(gi + 1) * 128], xcb.t[:, sl], True, True,
                           [wr_bf.b, mb["xb"][si]], [pr.b])
                        act(pm(A.t, s0), rv(pr.t, 0), AF.Tanh, [pr.b, hbr.b], [AS[si]], scale=0.5, bias=hbr.t[:, gi:gi + 1])
                        if dirn == 0:
                            igate(si)
                    for si in grp:
                        s0 = si * sub
                        t_m = tmp.next()
                        tms[si] = t_m
                        act(pm(A.t, s0), pm(A.t, s0), AF.Exp, [AS[si], hcA.b], [AS[si]], scale=hcA.t[:, gi:gi + 1],
                            bias=hcA.t[:, gi:gi + 1])
                        stt(rv(t_m.t, 0), pm(A.t, s0), 1.0, pm(A.t, s0), ALU.mult, ALU.mult, [AS[si]], [t_m.b])
                    for si in grp:
                        t_m = tms[si]
                        act(rv(t_m.t, 0), rv(t_m.t, 0), AF.Sqrt, [t_m.b], [t_m.b], scale=-0.25, bias=0.25)
                    for si in grp:
                        s0 = si * sub
                        t_m = tms[si]
                        stt(pm(Bd.t, s0), pm(Bd.t, s0), 1.0, rv(xc.t, s0), ALU.add, ALU.mult, [BS[si], mb["xc"][si]], [BS[si]])
                        tt("pool" if si % 2 else "dve", pm(Bd.t, s0), pm(Bd.t, s0), rv(t_m.t, 0), ALU.mult,
                           [BS[si], t_m.b], [BS[si]])
                if dirn == 0:
                    a_ap, b_ap = A.t[:, 0:Tn], Bd.t[:, 0:Tn]
                    init = h0f
                else:
                    a_ap, b_ap = fap(A.t, Tn - 1, [[-1, Tn]]), fap(Bd.t, Tn - 1, [[-1, Tn]])
                    init = h0b
                k.op("dve", lambda e: e.tensor_tensor_scan(out=b_ap, data0=a_ap, data1=b_ap, initial=init,
                                                           op0=ALU.mult, op1=ALU.add),
                     AS + BS + [hcf.b, hcb.b], BS)
                fin(dirn, Bd, BS)

        with ExitStack() as st:
            xring = ring(st, "xt", [128, D], F32, 6)
            uT_r = ring(st, "uT", [128, 8, NT], BF16, 2)
            wring = ring(st, "w", [128, 8, 128], BF16, 6)
            wv = sb(st, "wv", [128, 8, 8, 128], BF16)
            vtok = sb(st, "vtok", [128, NCH, D], BF16)
            qraw_r = ring(st, "qraw", [128, NT], F32, 2)
            ff_r = ring(st, "ff", [128, NT], F32, 2)
            fb_r = ring(st, "fb", [128, NT], F32, 2)
            P_r = ring(st, "P", [128, NT], F32, 4)
            rP_r = ring(st, "rP", [128, NT], F32, 2)
            kin_r = ring(st, "kin", [128, NT], F32, 2)
            kinv_r = ring(st, "kinv", [128, NT], BF16, 4)
            qdec_r = ring(st, "qdec", [128, NT], BF16, 4)
            kend_r = ring(st, "kend", [128, NT], BF16, 4)
            kendT_r = ring(st, "kendT", [128, 2 * NCH, 128], BF16, 2)
            d1_r = [ring(st, "d1f", [128, NT], F32, 2), ring(st, "d1b", [128, NT], F32, 2)]
            t1_r = ring(st, "t1", [128, NT], F32, 2)
            t2_r = ring(st, "t2", [128, NT], F32, 2)
            scs_r = ring(st, "scs", [128, NT], BF16, 2)
            decf_r = ring(st, "decf", [128, NCH], F32, 2)
            sst_r = ring(st, "sst", [128, NCH, 128], BF16, 2)
            s32_r = ring(st, "s32", [128, NCH, 128], F32, 2)
            ost_r = ring(st, "ost", [128, NT], F32, 2)
            ubst_r = ring(st, "ubst", [128, NCH, 128], F32, 2)
            z5st_r = ring(st, "z5st", [128, NT], F32, 2)
            craw = sb(st, "craw", [128, 8, CTXL], F32)
            cxc = sb(st, "cxc", [128, CTXL], F32)
            cxcb = sb(st, "cxcb", [128, CTXL], BF16)
            cA_ = sb(st, "cA_", [128, CTXL], F32)
            cB_ = sb(st, "cB_", [128, CTXL], F32)
            crawj = sb(st, "crawj", [128, CTXL], F32)
            ctmp = ring(st, "ctmp", [128, CTXL], F32, 3)

            k.dma("sp", wv.t[:], WB[24:32].rearrange("cb p kc c -> p cb kc c"), writes=[wv.b])
            maskf4 = sb(st, "maskf4", [128, 4, 128], F32)
            maskb4 = sb(st, "maskb4", [128, 4, 128], F32)
            k.dma("sp", maskf4.t[:], bass.AP(tensor=maskf_d.tensor, offset=maskf_d.offset,
                                              ap=[[128, 128], [0, 4], [1, 128]]), writes=[maskf4.b])
            k.dma("sp", maskb4.t[:], bass.AP(tensor=maskb_d.tensor, offset=maskb_d.offset,
                                              ap=[[128, 128], [0, 4], [1, 128]]), writes=[maskb4.b])
            Sf = [sb(st, f"Sf{h}", [128, 128], F32) for h in range(8)]
            for h in range(8):
                k.op("pool", lambda e: e.memset(Sf[h].t[:], 0.0), writes=[Sf[h].b])
            wr_bf = sb(st, "wr_bf", [128, 2048], BF16)
            wi_bf = sb(st, "wi_bf", [128, 2048], BF16)
            k.dma("pool", wr_bf.t[:], wr_d[:, :], writes=[wr_bf.b])
            k.dma("pool", wi_bf.t[:], wi_d[:, :], writes=[wi_bf.b])
            for rg in d1_r:
                for tl in rg.tiles:
                    k.op("pool", lambda e: e.memset(tl.t[:], 0.0), writes=[tl.b])

            def startA(h, uT, n, want_q, dirs=(0, 1)):
                hd = {"h": h}
                parts = []
                if want_q:
                    qraw = qraw_r.next()
                    hd["qraw"] = qraw
                    parts.append(lambda: proj_fm(h, uT, n, wring, lambda bank: act(
                        qraw.t[:, 0:n], bank.t[:, 0:n], AF.Silu, [bank.b, binT.b], [qraw.b], bias=binT.t[:, h:h + 1])))
                hd["f"] = {}
                if 0 in dirs:
                    ff = ff_r.next()
                    hd["f"][0] = ff
                    parts.append(lambda: proj_fm(8 + h, uT, n, wring, f_evac(ff, n, h, 8 + h, "pool")))
                if 1 in dirs:
                    fbt = fb_r.next()
                    hd["f"][1] = fbt
                    parts.append(lambda: proj_fm(16 + h, uT, n, wring, f_evac(fbt, n, 8 + h, 16 + h, "pool")))
                hd["parts"] = parts
                return hd

            def stageB(hd, n, nch, want_out, dirs=(0, 1)):
                tl = {}
                for dirn in dirs:
                    f = hd["f"][dirn]
                    P, rP, kin32, d1 = P_r.next(), rP_r.next(), kin_r.next(), d1_r[dirn].next()
                    tl[dirn] = (f, P, rP, kin32, d1)
                    hd[("P", dirn)] = P
                    pos = 0 if dirn == 0 else 127
                    cp("pool", fap(d1.t, pos, [[128, nch]]), fap(f.t, pos, [[128, nch]]), [f.b], [d1.b])
                for dirn in dirs:
                    f, P, rP, kin32, d1 = tl[dirn]
                    if dirn == 0:
                        o_ap, f_ap, d_ap = P.t[:, 0:n], f.t[:, 0:n], d1.t[:, 0:n]
                    else:
                        o_ap, f_ap, d_ap = (fap(P.t, n - 1, [[-1, n]]), fap(f.t, n - 1, [[-1, n]]), fap(d1.t, n - 1, [[-1, n]]))
                    k.op("dve", lambda e: e.tensor_tensor_scan(out=o_ap, data0=f_ap, data1=d_ap, initial=1.0,
                                                               op0=ALU.mult, op1=ALU.max), [f.b, d1.b], [P.b])
                    ts("pool", f.t[:, 0:n], f.t[:, 0:n], -1.0, 1.0, ALU.mult, ALU.add, [f.b], [f.b])
                for dirn in dirs:
                    f, P, rP, kin32, d1 = tl[dirn]
                    k.op("dve", lambda e: e.reciprocal(out=rP.t[:, 0:n], in_=P.t[:, 0:n]), [P.b], [rP.b])
                    if want_out:
                        qdec = qdec_r.next()
                        qraw = hd["qraw"]
                        stt(qdec.t[:, 0:n], qraw.t[:, 0:n], QSCALE, P.t[:, 0:n], ALU.mult, ALU.mult,
                            [qraw.b, P.b], [qdec.b])
                        hd[("qdec", dirn)] = qdec
                for dirn in dirs:
                    f, P, rP, kin32, d1 = tl[dirn]
                    tt("pool", kin32.t[:, 0:n], f.t[:, 0:n], rP.t[:, 0:n], ALU.mult, [f.b, rP.b], [kin32.b])
                    if want_out:
                        kinv = kinv_r.next()
                        cp("act", kinv.t[:, 0:n], kin32.t[:, 0:n], [kin32.b], [kinv.b])
                        hd[("kinv", dirn)] = kinv
                    kend = kend_r.next()
                    tt("pool", fap(kend.t, 0, [[128, nch], [1, 128]]), fap(kin32.t, 0, [[128, nch], [1, 128]]),
                       plast_bc(P, dirn, nch), ALU.mult, [kin32.b, P.b], [kend.b])
                    hd[("kend", dirn)] = kend

            def pe1(hd, nch, want_out, dirs=(0, 1)):
                if want_out:
                    for dirn, bank in ((0, ps[2]), (1, ps[3])):
                        kinv, qdec = hd[("kinv", dirn)], hd[("qdec", dirn)]
                        for c in range(nch):
                            sl = slice(c * 128, (c + 1) * 128)
                            mm(bank.t[:, sl], kinv.t[:, sl], qdec.t[:, sl], True, True, [kinv.b, qdec.b], [bank.b])
                kendT = kendT_r.next()
                hd["kendT"] = kendT
                for dirn in dirs:
                    kend = hd[("kend", dirn)]
                    for c in range(nch):
                        tr(trb.t[:, (dirn * nch + c) * 128:(dirn * nch + c + 1) * 128], kend.t[:, c * 128:(c + 1) * 128],
                           identb.t[:], [kend.b, identb.b], [trb.b])
                lo, hi = min(dirs) * nch * 128, (max(dirs) + 1) * nch * 128
                cp("act", fap(kendT.t, lo, [[1, hi - lo]]), trb.t[:, lo:hi], [trb.b], [kendT.b])

            def pe2(hd, nch, dirs=(0, 1)):
                h, kendT = hd["h"], hd["kendT"]
                for dirn in dirs:
                    bank = ps[4 + dirn]
                    for c in range(nch):
                        mm(bank.t[:, c * 128:(c + 1) * 128], kendT.t[:, dirn * nch + c, :],
                           vtok.t[:, c, h * 128:(h + 1) * 128], True, True, [kendT.b, vtok.b], [bank.b])

            cx = [xring.next() for _ in range(2)]
            for i in range(2):
                k.dma("sp", cx[i].t[:], ctx_d[i * 128:(i + 1) * 128, :], writes=[cx[i].b])
            uT = uT_r.next()
            make_uT(cx, uT, modc.t, scp1c.t, [modc.b, scp1c.b])
            v_proj(uT, 2, wv, vtok)
            for h in range(8):
                hd = startA(h, uT, CTXL, False)
                for p in hd["parts"]:
                    p()
                stageB(hd, CTXL, 2, False)
                pe1(hd, 2, False)
                pe2(hd, 2)
                Pf, Pb = hd[("P", 0)], hd[("P", 1)]
                for c in range(2):
                    stt(Sf[h].t[:], Sf[h].t[:], fap(Pf.t, c * 128 + 127, [[1, 1]]), ps[4].t[:, c * 128:(c + 1) * 128],
                        ALU.mult, ALU.add, [Sf[h].b, Pf.b, ps[4].b], [Sf[h].b])
                for c in (1, 0):
                    stt(Sb[h].t[:], Sb[h].t[:], fap(Pb.t, c * 128, [[1, 1]]), ps[5].t[:, c * 128:(c + 1) * 128],
                        ALU.mult, ALU.add, [Sb[h].b, Pb.b, ps[5].b], [Sb[h].b])
            for j in range(8):
                proj_fm(40 + j, uT, CTXL, wring,
                        lambda bank: act(craw.t[:, j, :], bank.t[:, 0:CTXL], AF.Identity, [bank.b, binT.b], [craw.b],
                                         bias=binT.t[:, 40 + j:41 + j]))
            cmb = mixer_bufs("cmb", 1)
            cdiag = sb(st, "cdiag", [128, 5, 128], F32)
            for j in range(8):
                cp("pool", crawj.t[:], craw.t[:, j, :], [craw.b], cmb["raw"] + cmb["B1"])

                def fin_ctx(dirn, Bd, BS):
                    if dirn == 0:
                        cp("dve", hcf.t[:, j:j + 1], Bd.t[:, CTXL - 1:CTXL], BS, [hcf.b])
                    else:
                        cp("dve", hcb.t[:, j:j + 1], Bd.t[:, 0:1], BS, [hcb.b])
                mixer_b(j, crawj, cxc, cxcb, cA_, cB_, CTXL, 1, CTXL, ctmp, cdiag, cmb, 0.0, 0.0, fin_ctx, wr_bf, wi_bf, 1)

            def load_uT(stile):
                t0 = stile * NT
                xs = [xring.next() for _ in range(NCH)]
                for i in range(NCH):
                    k.dma("sp", xs[i].t[:], x_d[t0 + i * 128:t0 + (i + 1) * 128, :], writes=[xs[i].b])
                u = uT_r.next()
                make_uT(xs, u, modx.t, scp1x.t, [modx.b, scp1x.b])
                return u

            uT = load_uT(0)
            for stile in range(NST):
                t0 = stile * NT
                uT_next = None
                if stile >= NST_OWN:
                    hd = startA(0, uT, NT, False, (1,))
                    hd["parts"][0]()
                    v_proj(uT, NCH, wv, vtok)
                    for h in range(8):
                        nxt = startA(h + 1, uT, NT, False, (1,)) if h < 7 else None
                        stageB(hd, NT, NCH, False, (1,))
                        cp("dve", dec_b.t[:, h, stile * NCH:(stile + 1) * NCH], plast(hd[("P", 1)], 1, NCH),
                           [hd[("P", 1)].b], [dec_b.b])
                        if nxt:
                            nxt["parts"][0]()
                        pe1(hd, NCH, False, (1,))
                        z5 = z5st_r.next()
                        proj_fm(40 + h, uT, NT, wring,
                                lambda bank: act(z5.t[:, :], bank.t[:, :], AF.Identity, [bank.b, binT.b], [z5.b],
                                                 bias=binT.t[:, 40 + h:41 + h]))
                        k.dma("act", XT[h, :, t0:t0 + NT], z5.t[:], reads=[z5.b])
                        pe2(hd, NCH, (1,))
                        ubst = ubst_r.next()
                        cp("act", fap(ubst.t, 0, [[1, NT]]), ps[5].t[:, :], [ps[5].b], [ubst.b])
                        k.dma("act", UB[h, :, stile * NCH:(stile + 1) * NCH, :], ubst.t[:], reads=[ubst.b])
                        if h == 5 and stile + 1 < NST:
                            uT_next = load_uT(stile + 1)
                        hd = nxt
                    uT = uT_next
                    continue
                hd = startA(0, uT, NT, True)
                for p in hd["parts"]:
                    p()
                v_proj(uT, NCH, wv, vtok)
                for h in range(8):
                    nxt = startA(h + 1, uT, NT, True) if h < 7 else None
                    stageB(hd, NT, NCH, True)
                    Pf, Pb = hd[("P", 0)], hd[("P", 1)]
                    decf = decf_r.next()
                    cp("dve", decf.t[:, :], plast(Pf, 0, NCH), [Pf.b], [decf.b])
                    cp("dve", dec_b.t[:, h, stile * NCH:(stile + 1) * NCH], plast(Pb, 1, NCH), [Pb.b], [dec_b.b])
                    if nxt:
                        nxt["parts"][0]()
                        nxt["parts"][1]()
                    pe1(hd, NCH, True)
                    t1, t2, scs = t1_r.next(), t2_r.next(), scs_r.next()
                    tt("dve", t1.t[:, :], ps[2].t[:, :], fap(maskf4.t, 0, [[1, 512]]), ALU.mult, [ps[2].b, maskf4.b], [t1.b])
                    tt("dve", t2.t[:, :], ps[3].t[:, :], fap(maskb4.t, 0, [[1, 512]]), ALU.mult, [ps[3].b, maskb4.b], [t2.b])
                    tt("pool", scs.t[:, :], t1.t[:, :], t2.t[:, :], ALU.add, [t1.b, t2.b], [scs.b])
                    if nxt:
                        nxt["parts"][2]()
                    pe2(hd, NCH)
                    ubst = ubst_r.next()
                    cp("act", fap(ubst.t, 0, [[1, NT]]), ps[5].t[:, :], [ps[5].b], [ubst.b])
                    k.dma("act", UB[h, :, stile * NCH:(stile + 1) * NCH, :], ubst.t[:], reads=[ubst.b])
                    sst, s32 = sst_r.next(), s32_r.next()
                    cp("pool", sst.t[:, 0, :], Sf[h].t[:], [Sf[h].b], [sst.b])
                    for c in range(NCH):
                        src = Sf[h].t[:] if c == 0 else s32.t[:, c - 1, :]
                        dst = Sf[h].t[:] if c == NCH - 1 else s32.t[:, c, :]
                        stt(dst, src, decf.t[:, c:c + 1], ps[4].t[:, c * 128:(c + 1) * 128], ALU.mult, ALU.add,
                            [Sf[h].b, s32.b, decf.b, ps[4].b], [Sf[h].b] if c == NCH - 1 else [s32.b])
                    cp("pool", fap(sst.t, 128, [[1, (NCH - 1) * 128]]), fap(s32.t, 0, [[1, (NCH - 1) * 128]]), [s32.b], [sst.b])
                    z5 = z5st_r.next()
                    proj_fm(40 + h, uT, NT, wring,
                            lambda bank: act(z5.t[:, :], bank.t[:, :], AF.Identity, [bank.b, binT.b], [z5.b],
                                             bias=binT.t[:, 40 + h:41 + h]))
                    k.dma("act", XT[h, :, t0:t0 + NT], z5.t[:], reads=[z5.b])
                    if h == 5 and stile + 1 < NST:
                        uT_next = load_uT(stile + 1)
                    qdf = hd[("qdec", 0)]
                    for c in range(NCH):
                        sl = slice(c * 128, (c + 1) * 128)
                        mm(ps[6].t[:, sl], vtok.t[:, c, h * 128:(h + 1) * 128], scs.t[:, sl], True, False,
                           [vtok.b, scs.b], [ps[6].b])
                        mm(ps[6].t[:, sl], sst.t[:, c, :], qdf.t[:, sl], False, True, [sst.b, qdf.b], [ps[6].b])
                    ost = ost_r.next()
                    cp("act", ost.t[:, :], ps[6].t[:, :], [ps[6].b], [ost.b])
                    k.dma("act", OP[h, :, t0:t0 + NT], ost.t[:], reads=[ost.b])
                    qdb = hd[("qdec", 1)]
                    k.dma("sp", QDB[h, :, t0:t0 + NT], qdb.t[:], reads=[qdb.b])
                    hd = nxt
                uT = uT_next
            k.barrier()

        with ExitStack() as st:
            raw = sb(st, "raw", [128, T], F32)
            xc = sb(st, "xc", [128, T], F32)
            xcb = sb(st, "xcb", [128, T], BF16)
            A = sb(st, "A", [128, T], F32)
            B = sb(st, "B", [128, T], F32)
            tmp = ring(st, "mtmp", [128, 512], F32, 10)
            diag = sb(st, "diag", [128, 5, 128], F32)
            wr_bf = sb(st, "wr_bf", [128, 2048], BF16)
            wi_bf = sb(st, "wi_bf", [128, 2048], BF16)
            k.dma("pool", wr_bf.t[:], wr_d[:, :], writes=[wr_bf.b])
            k.dma("pool", wi_bf.t[:], wi_d[:, :], writes=[wi_bf.b])
            xmb = mixer_bufs("xmb", T // 512)
            for j in range(8):
                for q4 in range(4):
                    k.dma("sp" if q4 % 2 == 0 else "act", raw.t[:, q4 * 2048:(q4 + 1) * 2048], XT[j, :, q4 * 2048:(q4 + 1) * 2048],
                          writes=[xmb["raw"][q4]] + xmb["B1"])

                def fin_x(dirn, Bd, BS):
                    if dirn == 1:
                        for q4 in range(2):
                            r0 = q4 * (ROWS // 4)
                            cm = [[1, ROWS // 4], [ROWS, GW]]
                            tt("dve" if q4 == 0 else "pool", fap(xc.t, r0 * GW, [[GW, ROWS // 4], [1, GW]]), fap(B.t, r0, cm),
                               fap(Bd.t, r0, cm), ALU.add, xmb["B0"] + xmb["B1"], xmb["xc"][q4 * 4:(q4 + 1) * 4])
                        for q4 in range(2):
                            k.dma("sp", HS[j, :, q4 * 2048:(q4 + 1) * 2048], xc.t[:, q4 * 2048:(q4 + 1) * 2048],
                                  reads=xmb["xc"][q4 * 4:(q4 + 1) * 4], key=xc.b)
                mixer_b(j, raw, xc, xcb, A, B, T, GW, 512, tmp, diag, xmb, hcf.t[:, j:j + 1], hcb.t[:, j:j + 1], fin_x,
                        wr_bf, wi_bf, 8)
            k.barrier()

        with ExitStack() as st:
            xring = ring(st, "xt2", [128, D], F32, 7)
            uT = sb(st, "uT2", [128, 8, NT], BF16)
            wring = ring(st, "w2", [128, 8, 128], BF16, 4)
            pa_bf = sb(st, "pa_bf", [128, 8, D], BF16)
            pb_bf = sb(st, "pb_bf", [128, 8, D], BF16)
            wo_bf = sb(st, "wo_bf", [128, 8, D], BF16)
            lng = sb(st, "lng", [128, D], F32)
            lnb = sb(st, "lnb", [128, D], F32)
            op_r = ring(st, "opl", [128, NT], F32, 2)
            qdb_r = ring(st, "qdbl", [128, NT], BF16, 2)
            ub_r = ring(st, "ubl", [128, NCH, 128], F32, 2)
            sbs_r = ring(st, "sbs", [128, NCH, 128], BF16, 2)
            s32_r = ring(st, "s32b", [128, NCH, 128], F32, 2)
            o_r = ring(st, "o", [128, NT], F32, 2)
            sq_r = ring(st, "sq", [128, NT], F32, 2)
            rs_r = ring(st, "rs", [128, NT], F32, 2)
            g4_r = ring(st, "g4", [128, NT], F32, 2)
            oaT = sb(st, "oaT", [128, 8, NT], BF16)
            obT = sb(st, "obT", [128, 8, NT], BF16)
            yT = sb(st, "yT", [128, 8, NT], BF16)
            h_r = ring(st, "hl", [128, NT], F32, 3)
            g6_r = ring(st, "g6", [128, NT], F32, 2)
            s7_r = ring(st, "s7", [128, NT], F32, 2)
            s8_r = ring(st, "s8", [128, NT], F32, 2)
            ta_r = ring(st, "ta", [128, NT], F32, 2)
            tb_r = ring(st, "tb", [128, NT], F32, 2)
            r_r = ring(st, "r", [128, D], F32, 2)
            st6_r = ring(st, "st6", [128, 12], F32, 2)
            mv_r = ring(st, "mv", [128, 4], F32, 2)
            rms_eps_t = sb(st, "rms_eps", [128, 1], F32)
            ln_eps_t = sb(st, "ln_eps", [128, 1], F32)
            k.op("dve", lambda e: e.memset(rms_eps_t.t[:], RMS_EPS), writes=[rms_eps_t.b])
            k.op("dve", lambda e: e.memset(ln_eps_t.t[:], LN_EPS), writes=[ln_eps_t.b])
            k.dma("sp", pa_bf.t[:], PAB[:, :, :], writes=[pa_bf.b])
            k.dma("sp", pb_bf.t[:], PBB[:, :, :], writes=[pb_bf.b])
            k.dma("sp", wo_bf.t[:], WOB[:, :, :], writes=[wo_bf.b])
            k.dma("sp", lng.t[:], lng_d[:, :], writes=[lng.b])
            k.dma("sp", lnb.t[:], lnb_d[:, :], writes=[lnb.b])
            for stile in range(NST - 1, -1, -1):
                t0 = stile * NT
                if stile >= NST_OWN:
                    for h in range(8):
                        ubl = ub_r.next()
                        k.dma("sp", ubl.t[:], UB[h, :, stile * NCH:(stile + 1) * NCH, :], writes=[ubl.b])
                        for c in range(NCH - 1, -1, -1):
                            stt(Sb[h].t[:], Sb[h].t[:], dec_b.t[:, h, stile * NCH + c:stile * NCH + c + 1], ubl.t[:, c, :],
                                ALU.mult, ALU.add, [Sb[h].b, dec_b.b, ubl.b], [Sb[h].b])
                    continue
                xs = [xring.next() for _ in range(NCH)]
                for i in range(NCH):
                    k.dma("sp", xs[i].t[:], x_d[t0 + i * 128:t0 + (i + 1) * 128, :], writes=[xs[i].b])
                make_uT(xs, uT, modx.t, scp1x.t, [modx.b, scp1x.b])
                def loads_rec(h):
                    opl, qdbl, ubl = op_r.next(), qdb_r.next(), ub_r.next()
                    k.dma("sp", opl.t[:], OP[h, :, t0:t0 + NT], writes=[opl.b])
                    k.dma("sp", qdbl.t[:], QDB[h, :, t0:t0 + NT], writes=[qdbl.b])
                    k.dma("sp", ubl.t[:], UB[h, :, stile * NCH:(stile + 1) * NCH, :], writes=[ubl.b])
                    sbs, s32 = sbs_r.next(), s32_r.next()
                    cp("pool", sbs.t[:, NCH - 1, :], Sb[h].t[:], [Sb[h].b], [sbs.b])
                    for c in range(NCH - 1, -1, -1):
                        src = Sb[h].t[:] if c == NCH - 1 else s32.t[:, c, :]
                        dst = Sb[h].t[:] if c == 0 else s32.t[:, c - 1, :]
                        stt(dst, src, dec_b.t[:, h, stile * NCH + c:stile * NCH + c + 1], ubl.t[:, c, :], ALU.mult, ALU.add,
                            [Sb[h].b, s32.b, dec_b.b, ubl.b], [Sb[h].b] if c == 0 else [s32.b])
                    cp("pool", fap(sbs.t, 0, [[1, (NCH - 1) * 128]]), fap(s32.t, 0, [[1, (NCH - 1) * 128]]), [s32.b], [sbs.b])
                    return opl, qdbl, sbs

                cur = loads_rec(0)
                for h in range(8):
                    opl, qdbl, sbs = cur
                    for c in range(NCH):
                        sl = slice(c * 128, (c + 1) * 128)
                        mm(ps[2].t[:, sl], sbs.t[:, c, :], qdbl.t[:, sl], True, True, [sbs.b, qdbl.b], [ps[2].b])
                    o = o_r.next()
                    tt("dve", o.t[:, :], ps[2].t[:, :], opl.t[:, :], ALU.add, [ps[2].b, opl.b], [o.b])
                    sq = sq_r.next()
                    act(sq.t[:, :], o.t[:, :], AF.Square, [o.b], [sq.b])
                    if h < 7:
                        cur = loads_rec(h + 1)
                    g4 = g4_r.next()
                    proj_fm(32 + h, uT, NT, wring, lambda bank: act(g4.t[:, :], bank.t[:, :], AF.Silu, [bank.b, binT.b],
                                                                     [g4.b], bias=binT.t[:, 32 + h:33 + h]))
                    hl, g6 = h_r.next(), g6_r.next()
                    k.dma("sp", hl.t[:], HS[h, :, t0:t0 + NT], writes=[hl.b])
                    proj_fm(48 + h, uT, NT, wring, lambda bank: act(g6.t[:, :], bank.t[:, :], AF.Silu, [bank.b, binT.b],
                                                                     [g6.b], bias=binT.t[:, 48 + h:49 + h]))
                    tt("pool", obT.t[:, h, :], hl.t[:, :], g6.t[:, :], ALU.mult, [hl.b, g6.b], [obT.b])
                    mm(ps[3].t[:, :], ones32.t[:, :], sq.t[:, :], True, True, [ones32.b, sq.b], [ps[3].b])
                    rs = rs_r.next()
                    act(rs.t[:, :], ps[3].t[:, :], AF.Ln, [ps[3].b], [rs.b], scale=1.0 / 128.0, bias=rms_eps_t.t[:, 0:1])
                    act(rs.t[:, :], rs.t[:, :], AF.Exp, [rs.b], [rs.b], scale=-0.5)
                    stt(o.t[:, :], o.t[:, :], nag.t[:, 0:1], rs.t[:, :], ALU.mult, ALU.mult, [o.b, nag.b, rs.b], [o.b])
                    tt("pool", oaT.t[:, h, :], o.t[:, :], g4.t[:, :], ALU.mult, [o.b, g4.b], [oaT.b])
                for j in range(8):
                    s7, s8 = s7_r.next(), s8_r.next()
                    proj_fm(56 + j, uT, NT, wring, lambda bank: act(s7.t[:, :], bank.t[:, :], AF.Tanh, [bank.b, hbinT.b],
                                                                     [s7.b], scale=0.5, bias=hbinT.t[:, 56 + j:57 + j]))
                    proj_fm(64 + j, uT, NT, wring, lambda bank: act(s8.t[:, :], bank.t[:, :], AF.Tanh, [bank.b, hbinT.b],
                                                                     [s8.b], scale=0.5, bias=hbinT.t[:, 64 + j:65 + j]))
                    ta, tb = ta_r.next(), tb_r.next()
                    for kc in range(8):
                        mm(ps[4].t[:, :], pa_bf.t[:, kc, j * 128:(j + 1) * 128], oaT.t[:, kc, :], kc == 0, kc == 7,
                           [pa_bf.b, oaT.b], [ps[4].b])
                    stt(ta.t[:, :], s7.t[:, :], 1.0, ps[4].t[:, :], ALU.add, ALU.mult, [ps[4].b, s7.b], [ta.b])
                    for kc in range(8):
                        mm(ps[5].t[:, :], pb_bf.t[:, kc, j * 128:(j + 1) * 128], obT.t[:, kc, :], kc == 0, kc == 7,
                           [pb_bf.b, obT.b], [ps[5].b])
                    stt(tb.t[:, :], s8.t[:, :], 1.0, ps[5].t[:, :], ALU.add, ALU.mult, [ps[5].b, s8.b], [tb.b])
                    tt("pool", yT.t[:, j, :], ta.t[:, :], tb.t[:, :], ALU.add, [ta.b, tb.b], [yT.b])
                for i in range(NCH):
                    r = r_r.next()
                    for half in range(2):
                        bank = ps[6] if half == 0 else ps[3]
                        hs = slice(half * 512, (half + 1) * 512)
                        for kc in range(8):
                            mm(bank.t[:, :], yT.t[:, kc, i * 128:(i + 1) * 128], wo_bf.t[:, kc, hs], kc == 0, kc == 7,
                               [yT.b, wo_bf.b], [bank.b])
                        tt("dve", r.t[:, hs], bank.t[:, :], gt_bc.t[:, hs], ALU.mult, [bank.b, gt_bc.b], [r.b])
                    stt(r.t[:, :], xs[i].t[:, :], ALPHA, r.t[:, :], ALU.mult, ALU.add, [xs[i].b, r.b], [r.b])
                    st6, mv = st6_r.next(), mv_r.next()
                    k.op("dve", lambda e: e.bn_stats(out=st6.t[:, 0:6], in_=r.t[:, 0:512]), [r.b], [st6.b])
                    k.op("dve", lambda e: e.bn_stats(out=st6.t[:, 6:12], in_=r.t[:, 512:1024]), [r.b], [st6.b])
                    k.op("dve", lambda e: e.bn_aggr(out=mv.t[:, 0:2], in_=st6.t[:, 0:12]), [st6.b], [mv.b])
                    act(mv.t[:, 2:3], mv.t[:, 1:2], AF.Ln, [mv.b], [mv.b], scale=1.0, bias=ln_eps_t.t[:, 0:1])
                    act(mv.t[:, 2:3], mv.t[:, 2:3], AF.Exp, [mv.b], [mv.b], scale=-0.5)
                    stt(mv.t[:, 3:4], mv.t[:, 0:1], -1.0, mv.t[:, 2:3], ALU.mult, ALU.mult, [mv.b], [mv.b])
                    act(r.t[:, :], r.t[:, :], AF.Identity, [r.b, mv.b], [r.b], scale=mv.t[:, 2:3], bias=mv.t[:, 3:4])
                    tt("pool", r.t[:, :], r.t[:, :], lng.t[:, :], ALU.mult, [r.b, lng.b], [r.b])
                    tt("pool", r.t[:, :], r.t[:, :], lnb.t[:, :], ALU.add, [r.b, lnb.b], [r.b])
                    k.dma("pool", out_d[t0 + i * 128:t0 + (i + 1) * 128, :], r.t[:, :], reads=[r.b])
            k.barrier()
        if debug:
            print("instr counts", {n: e.n for n, e in k.engs.items()}, "waits", k.nwaits)
    return nc


def _fm(v, n):
    return np.ascontiguousarray(np.asarray(v, np.float32).reshape(n, 128).T)


_CACHE = {}


def _shared_inputs(w_mod, b_mod, w_in, b_in, lb_logits, norm_a_g, conv_w, conv_b, w_r, b_r, w_i, b_i, lam,
                   p_a, p_b, w_out, ln_g, ln_b, flip):
    f = np.float32
    w_in0 = np.asarray(w_in, f)[0]
    w4 = w_in0.reshape(8, 128, 72, 128).transpose(2, 1, 0, 3)
    b_in0 = np.asarray(b_in, f)[0]
    binT = _fm(b_in0, 72)
    lbl = np.asarray(lb_logits, f).reshape(2, 2, 8, 128)
    wr = np.asarray(w_r, f)[0]
    wi = np.asarray(w_i, f)[0]
    br_ = np.asarray(b_r, f)[0]
    bi_ = np.asarray(b_i, f)[0]
    lam_ = np.asarray(lam, f)[0]
    cw = np.asarray(conv_w, f)[0]
    z = np.zeros_like(cw[0])
    if flip:
        order = list(range(0, 8)) + list(range(16, 24)) + list(range(8, 16)) + list(range(24, 72))
        w4 = w4[order]
        binT = binT[:, order]
        lbl = lbl[:, ::-1]
        wr, wi, br_, bi_, lam_ = wr[::-1], wi[::-1], br_[::-1], bi_[::-1], lam_[::-1]
        taps = np.stack([cw[3], cw[2], cw[1], cw[0], z], axis=0)
    else:
        taps = np.stack([z, cw[0], cw[1], cw[2], cw[3]], axis=0)
    return {
        "w_mod": np.ascontiguousarray(np.asarray(w_mod, f)[0]),
        "bmodT": _fm(np.asarray(b_mod, f)[0], 24),
        "bmod_row": np.ascontiguousarray(np.asarray(b_mod, f)[0][None, :]),
        "w4": np.ascontiguousarray(w4),
        "binT": np.ascontiguousarray(binT),
        "bv_row": np.ascontiguousarray(b_in0[None, 3072:4096]),
        "lbl": np.ascontiguousarray(lbl.transpose(3, 0, 1, 2).reshape(128, 32)),
        "nag": np.ascontiguousarray(np.asarray(norm_a_g, f)[0].reshape(128, 1)),
        "convw": np.ascontiguousarray(taps.reshape(5, 8, 128).transpose(2, 1, 0).reshape(128, 40)),
        "convb": _fm(np.asarray(conv_b, f)[0], 8),
        "wr": np.ascontiguousarray(wr.transpose(2, 0, 1, 3).reshape(128, 2048)),
        "wi": np.ascontiguousarray(wi.transpose(2, 0, 1, 3).reshape(128, 2048)),
        "br": _fm(np.ascontiguousarray(br_).reshape(-1), 16),
        "bi": _fm(np.ascontiguousarray(bi_).reshape(-1), 16),
        "lam": _fm(np.ascontiguousarray(lam_).reshape(-1), 16),
        "pa": np.ascontiguousarray(np.asarray(p_a, f)[0].reshape(8, 128, D).transpose(1, 0, 2)),
        "pb": np.ascontiguousarray(np.asarray(p_b, f)[0].reshape(8, 128, D).transpose(1, 0, 2)),
        "wo": np.ascontiguousarray(np.asarray(w_out, f)[0].reshape(8, 128, D).transpose(1, 0, 2)),
        "lng": np.ascontiguousarray(np.broadcast_to(np.asarray(ln_g, f)[0][None, :], (128, D))),
        "lnb": np.ascontiguousarray(np.broadcast_to(np.asarray(ln_b, f)[0][None, :], (128, D))),
        "ident": np.eye(128, dtype=f),
        "maskf": np.triu(np.ones((128, 128), f)),
        "maskb": np.tril(np.ones((128, 128), f)),
    }


def kernel(x, c, ctx, c_ctx, w_mod, b_mod, w_in, b_in, lb_logits, norm_a_g, conv_w, conv_b,
           w_r, b_r, w_i, b_i, lam, p_a, p_b, w_out, ln_g, ln_b):
    f = np.float32
    x = np.asarray(x, f); ctx = np.asarray(ctx, f); c = np.asarray(c, f); c_ctx = np.asarray(c_ctx, f)
    params = (w_mod, b_mod, w_in, b_in, lb_logits, norm_a_g, conv_w, conv_b, w_r, b_r, w_i, b_i, lam,
              p_a, p_b, w_out, ln_g, ln_b)
    shared = [_shared_inputs(*params, flip=False), _shared_inputs(*params, flip=True)]
    in_maps = []
    for core in range(8):
        b, half = core // 2, core % 2
        m = dict(shared[half])
        if half == 0:
            m["x"] = np.ascontiguousarray(x[b])
            m["ctx"] = np.ascontiguousarray(ctx[b])
        else:
            m["x"] = np.ascontiguousarray(x[b][::-1])
            m["ctx"] = np.ascontiguousarray(ctx[b][::-1])
        m["cvec"] = np.ascontiguousarray(np.concatenate([_fm(c[b], 8), _fm(c_ctx, 8)], axis=1))
        in_maps.append(m)
    debug = bool(os.environ.get("MK_DEBUG"))
    key = ("nc", debug)
    if key not in _CACHE:
        _CACHE[key] = build_program(debug)
    nc = _CACHE[key]
    res = run_bass_kernel_spmd(nc, in_maps, core_ids=list(range(8)))
    if debug:
        _CACHE["last"] = res
    out = np.empty((4, T, D), f)
    for b in range(4):
        out[b, :T_OWN] = np.asarray(res.results[2 * b]["out"], f)
        out[b, T_OWN:] = np.asarray(res.results[2 * b + 1]["out"], f)[::-1]
    return out
```

```python
import os
import numpy as np
from contextlib import ExitStack
import concourse.bass as bass
import concourse.mybir as mybir
from concourse.bass_utils import run_bass_kernel_spmd

F32 = mybir.dt.float32
BF16 = mybir.dt.bfloat16
ALU = mybir.AluOpType
AF = mybir.ActivationFunctionType

D = 1024
T = 8192
NT = 512
NST = T // NT
T_OWN = T // 2
NST_OWN = T_OWN // NT
NCH = NT // 128
CTXL = 256
GW = 64
ROWS = T // GW
QSCALE = 128 ** -0.5
ALPHA = 2.0 ** 0.25
LN_EPS = 1e-5
RMS_EPS = 1e-6


class Buf:
    __slots__ = ("name", "last_w", "readers", "dma_sem", "dma_cnt")

    def __init__(self, name):
        self.name = name
        self.last_w = None
        self.readers = {}
        self.dma_sem = None
        self.dma_cnt = 0


class Eng:
    def __init__(self, name, h, sem):
        self.name, self.h, self.sem, self.n = name, h, sem, 0
        self.seen = {}


class K:
    SAME_ENG_SYNC = True

    def __init__(self, nc, stack):
        self.nc = nc
        self.stack = stack
        self.engs = {}
        for name, h in (("pe", nc.tensor), ("act", nc.scalar), ("dve", nc.vector),
                        ("pool", nc.gpsimd), ("sp", nc.sync)):
            sem = stack.enter_context(nc.semaphore("s_" + name))
            self.engs[name] = Eng(name, h, sem)
        self.bufs = []
        self.nwaits = 0
        self.free_sems = []

    def buf(self, name):
        b = Buf(name)
        self.bufs.append(b)
        return b

    def _wait(self, E, deps):
        need = {}
        for sem, val in deps:
            if val <= 0:
                continue
            if need.get(id(sem), (None, 0))[1] < val:
                need[id(sem)] = (sem, val)
        for sem, val in need.values():
            if sem is E.sem:
                if E.name == "pe" or not self.SAME_ENG_SYNC:
                    continue
            if E.seen.get(id(sem), 0) >= val:
                continue
            E.h.wait_ge(sem, val)
            self.nwaits += 1
            E.seen[id(sem)] = val

    def _deps(self, reads, writes):
        deps = []
        for b in reads:
            if b.last_w:
                deps.append(b.last_w)
        for b in writes:
            if b.last_w:
                deps.append(b.last_w)
            deps.extend(b.readers.values())
        return deps

    def _record(self, ev, reads, writes):
        sem, val = ev
        for b in reads:
            if b.readers.get(id(sem), (None, 0))[1] < val:
                b.readers[id(sem)] = ev
        for b in writes:
            b.last_w = ev
            b.readers = {}

    def op(self, e, emit, reads=(), writes=()):
        E = self.engs[e]
        self._wait(E, self._deps(reads, writes))
        ins = emit(E.h)
        E.n += 1
        ins.then_inc(E.sem, 1)
        self._record((E.sem, E.n), reads, writes)
        return ins

    def dma(self, q, out, in_, reads=(), writes=(), key=None, **kw):
        E = self.engs[q]
        kb = key if key is not None else (writes[0] if writes else reads[0])
        if kb.dma_sem is None:
            kb.dma_sem = self.stack.enter_context(self.nc.semaphore("d_" + kb.name))
        deps = self._deps(reads, writes)
        if kb.dma_cnt:
            deps.append((kb.dma_sem, kb.dma_cnt))
        self._wait(E, deps)
        ins = E.h.dma_start(out=out, in_=in_, **kw)
        kb.dma_cnt += 16
        ins.then_inc(kb.dma_sem, 16)
        self._record((kb.dma_sem, kb.dma_cnt), reads, writes)
        return ins

    def barrier(self, skip=()):
        sp = self.engs["sp"]
        deps = [(E.sem, E.n) for E in self.engs.values() if E is not sp]
        deps += [(b.dma_sem, b.dma_cnt) for b in self.bufs if b.dma_sem is not None and b not in skip]
        self._wait(sp, deps)
        ins = sp.h.nop()
        sp.n += 1
        ins.then_inc(sp.sem, 1)
        for E in self.engs.values():
            if E is not sp:
                self._wait(E, [(sp.sem, sp.n)])
        for b in self.bufs:
            b.last_w = None
            b.readers = {}


class Tl:
    __slots__ = ("t", "b")

    def __init__(self, t, b):
        self.t, self.b = t, b


class Ring:
    def __init__(self, tiles):
        self.tiles, self.i = tiles, 0

    def next(self):
        t = self.tiles[self.i % len(self.tiles)]
        self.i += 1
        return t


def fap(t, off, dims):
    base = t[:]
    return bass.AP(tensor=base.tensor, offset=base.offset + off,
                   ap=[list(base.ap[0])] + [list(d) for d in dims])


def build_program(debug=False):
    nc = bass.Bass("TRN2", target_bir_lowering=False)

    def inp(name, shape, dt=F32):
        return nc.dram_tensor(name, shape, dt, kind="ExternalInput").ap()

    def scratch(name, shape, dt):
        return nc.dram_tensor(name, shape, dt, kind=("ExternalOutput" if debug else "Internal")).ap()

    x_d = inp("x", [T, D])
    ctx_d = inp("ctx", [CTXL, D])
    cvec_d = inp("cvec", [128, 16])
    wmod_d = inp("w_mod", [D, 3 * D])
    bmodT_d = inp("bmodT", [128, 24])
    bmodr_d = inp("bmod_row", [1, 3 * D])
    w4_d = inp("w4", [72, 128, 8, 128])
    binT_d = inp("binT", [128, 72])
    bv_d = inp("bv_row", [1, D])
    lbl_d = inp("lbl", [128, 32])
    nag_d = inp("nag", [128, 1])
    convw_d = inp("convw", [128, 40])
    convb_d = inp("convb", [128, 8])
    wr_d = inp("wr", [128, 2048])
    wi_d = inp("wi", [128, 2048])
    br_d = inp("br", [128, 16])
    bi_d = inp("bi", [128, 16])
    lam_d = inp("lam", [128, 16])
    pa_d = inp("pa", [128, 8, D])
    pb_d = inp("pb", [128, 8, D])
    wo_d = inp("wo", [128, 8, D])
    lng_d = inp("lng", [128, D])
    lnb_d = inp("lnb", [128, D])
    ident_d = inp("ident", [128, 128])
    maskf_d = inp("maskf", [128, 128])
    maskb_d = inp("maskb", [128, 128])
    out_d = nc.dram_tensor("out", [T_OWN, D], F32, kind="ExternalOutput").ap()

    WB = scratch("wb_s", [72, 128, 8, 128], BF16)
    PAB = scratch("pab_s", [128, 8, D], BF16)
    PBB = scratch("pbb_s", [128, 8, D], BF16)
    WOB = scratch("wob_s", [128, 8, D], BF16)
    XT = scratch("xt_s", [8, 128, T], F32)
    OP = scratch("op_s", [8, 128, T], F32)
    QDB = scratch("qdb_s", [8, 128, T], BF16)
    UB = scratch("ub_s", [8, 128, T // 128, 128], F32)
    HS = scratch("h_s", [8, 128, T], F32)
    if debug:
        DBG = nc.dram_tensor("dbg", [128, 4096], F32, kind="ExternalOutput").ap()

    with ExitStack() as pst:
        k = K(nc, pst)
        cnt = [0]

        def sb(stack, name, shape, dt):
            cnt[0] += 1
            nm = f"{name}_{cnt[0]}"
            return Tl(stack.enter_context(nc.sbuf_tensor(nm, shape, dt)), k.buf(nm))

        def ring(stack, name, shape, dt, n):
            return Ring([sb(stack, name, shape, dt) for _ in range(n)])

        ps = []
        for i in range(7):
            ps.append(Tl(pst.enter_context(nc.psum_tensor(f"ps{i}", [128, 512], F32)), k.buf(f"ps{i}")))
        trb = Tl(pst.enter_context(nc.psum_tensor("trb", [128, 1024], BF16)), k.buf("trb"))
        pj = Ring([ps[0], ps[1]])

        def act(out, in_, func, reads, writes, **kw):
            k.op("act", lambda e: e.activation(out=out, in_=in_, func=func, **kw), reads, writes)

        def tt(eng, out, in0, in1, op, reads, writes):
            k.op(eng, lambda e: e.tensor_tensor(out=out, in0=in0, in1=in1, op=op), reads, writes)

        def ts(eng, out, in0, s1, s2, op0, op1, reads, writes):
            if s2 is None:
                k.op(eng, lambda e: e.tensor_scalar(out=out, in0=in0, scalar1=s1, scalar2=None, op0=op0), reads, writes)
            else:
                k.op(eng, lambda e: e.tensor_scalar(out=out, in0=in0, scalar1=s1, scalar2=s2, op0=op0, op1=op1), reads, writes)

        def stt(out, in0, scalar, in1, op0, op1, reads, writes):
            k.op("dve", lambda e: e.scalar_tensor_tensor(out=out, in0=in0, scalar=scalar, in1=in1, op0=op0, op1=op1),
                 reads, writes)

        def cp(eng, out, in_, reads, writes):
            if eng == "act":
                k.op("act", lambda e: e.activation(out=out, in_=in_, func=AF.Identity), reads, writes)
            else:
                k.op(eng, lambda e: e.tensor_copy(out=out, in_=in_), reads, writes)

        def mm(out, lhsT, rhs, start, stop, reads, writes):
            k.op("pe", lambda e: e.matmul(out, lhsT=lhsT, rhs=rhs, start=start, stop=stop), reads, writes)

        def tr(out, in_, ident, reads, writes):
            k.op("pe", lambda e: e.transpose(out, in_, ident), reads, writes)

        ident32 = sb(pst, "ident32", [128, 128], F32)
        identb = sb(pst, "identb", [128, 128], BF16)
        ones32 = sb(pst, "ones32", [128, 128], F32)
        zeros32 = sb(pst, "zeros32", [128, 128], F32)
        onesb = sb(pst, "onesb", [1, 128], BF16)
        modx = sb(pst, "modx", [128, 24], F32)
        modc = sb(pst, "modc", [128, 24], F32)
        scp1x = sb(pst, "scp1x", [128, 8], F32)
        scp1c = sb(pst, "scp1c", [128, 8], F32)
        gt_bc = sb(pst, "gt_bc", [128, D], F32)
        lb = sb(pst, "lb", [128, 16], F32)
        oml = sb(pst, "oml", [128, 16], F32)
        binT = sb(pst, "binT", [128, 72], F32)
        bv_bf = sb(pst, "bv_bf", [1, D], BF16)
        nag = sb(pst, "nag", [128, 1], F32)
        convw = sb(pst, "convw", [128, 40], F32)
        convb = sb(pst, "convb", [128, 8], F32)
        br = sb(pst, "br", [128, 16], F32)
        bi = sb(pst, "bi", [128, 16], F32)
        cA = sb(pst, "cA", [128, 16], F32)
        hcA = sb(pst, "hcA", [128, 16], F32)
        hbr = sb(pst, "hbr", [128, 16], F32)
        hbi = sb(pst, "hbi", [128, 16], F32)
        fc0 = sb(pst, "fc0", [128, 16], F32)
        fc1 = sb(pst, "fc1", [128, 16], F32)
        hbinT = sb(pst, "hbinT", [128, 72], F32)
        Sb = [sb(pst, f"Sb{h}", [128, 128], F32) for h in range(8)]
        dec_b = sb(pst, "dec_b", [128, 8, T // 128], F32)
        hcf = sb(pst, "hcf", [128, 8], F32)
        hcb = sb(pst, "hcb", [128, 8], F32)
        dcast = k.buf("dcast")

        with ExitStack() as st:
            k.dma("sp", ident32.t[:], ident_d[:, :], writes=[ident32.b])
            k.op("dve", lambda e: e.memset(ones32.t[:], 1.0), writes=[ones32.b])
            k.op("dve", lambda e: e.memset(zeros32.t[:], 0.0), writes=[zeros32.b])
            k.op("dve", lambda e: e.memset(onesb.t[:], 1.0), writes=[onesb.b])
            cp("dve", identb.t[:], ident32.t[:], [ident32.b], [identb.b])

            cvec = sb(st, "cvec", [128, 16], F32)
            cs = sb(st, "cs", [128, 16], F32)
            lbl = sb(st, "lbl", [128, 32], F32)
            lam = sb(st, "lam", [128, 16], F32)
            bmodT = sb(st, "bmodT", [128, 24], F32)
            bmodr = sb(st, "bmodr", [1, 3 * D], F32)
            gt_row = sb(st, "gt_row", [1, D], F32)
            bv32 = sb(st, "bv32", [1, D], F32)
            wmod = sb(st, "wmod", [128, 8, 3 * D], F32)
            tmp16 = sb(st, "tmp16", [128, 16], F32)
            tmp16b = sb(st, "tmp16b", [128, 16], F32)
            for tl, src in ((cvec, cvec_d), (lbl, lbl_d), (lam, lam_d), (bmodT, bmodT_d), (bmodr, bmodr_d),
                            (binT, binT_d), (nag, nag_d), (convw, convw_d), (convb, convb_d), (br, br_d),
                            (bi, bi_d), (bv32, bv_d)):
                k.dma("sp", tl.t[:], src[:, :], writes=[tl.b])
            for kc in range(8):
                k.dma("sp" if kc % 2 == 0 else "act", wmod.t[:, kc, :], wmod_d[kc * 128:(kc + 1) * 128, :], writes=[wmod.b])
            late = []
            for g in (1, 2, 3, 5, 0, 4, 6, 7, 8):
                kb = k.buf(f"dcast{g}")
                k.dma("pool", WB[g * 8:(g + 1) * 8], w4_d[g * 8:(g + 1) * 8], key=kb, reads=[wmod.b])
                if g in (4, 6, 7, 8):
                    late.append(kb)
            for nm_, dst, src in (("dcpa", PAB, pa_d), ("dcpb", PBB, pb_d), ("dcwo", WOB, wo_d)):
                kb = k.buf(nm_)
                k.dma("pool", dst[:, :, :], src[:, :, :], key=kb, reads=[wmod.b])
                late.append(kb)
            act(cs.t[:], cvec.t[:], AF.Silu, [cvec.b], [cs.b])
            for oc in range(24):
                for kc in range(8):
                    mm(ps[0].t[:, 2 * oc:2 * oc + 2], wmod.t[:, kc, oc * 128:(oc + 1) * 128],
                       fap(cs.t, kc, [[8, 2]]), kc == 0, kc == 7, [wmod.b, cs.b], [ps[0].b])
            tt("dve", modx.t[:], fap(ps[0].t, 0, [[2, 24]]), bmodT.t[:], ALU.add, [ps[0].b, bmodT.b], [modx.b])
            tt("dve", modc.t[:], fap(ps[0].t, 1, [[2, 24]]), bmodT.t[:], ALU.add, [ps[0].b, bmodT.b], [modc.b])
            ts("dve", scp1x.t[:], modx.t[:, 8:16], 1.0, None, ALU.add, None, [modx.b], [scp1x.b])
            ts("dve", scp1c.t[:], modc.t[:, 8:16], 1.0, None, ALU.add, None, [modc.b], [scp1c.b])
            for half in range(2):
                pr = ps[1 + half]
                for kc in range(8):
                    mm(pr.t[0:1, :], cs.t[:, kc:kc + 1], wmod.t[:, kc, 2048 + half * 512:2048 + (half + 1) * 512],
                       kc == 0, kc == 7, [wmod.b, cs.b], [pr.b])
                tt("dve", gt_row.t[0:1, half * 512:(half + 1) * 512], pr.t[0:1, :],
                   bmodr.t[0:1, 2048 + half * 512:2048 + (half + 1) * 512], ALU.add, [pr.b, bmodr.b], [gt_row.b])
            for half in range(2):
                pr = ps[3 + half]
                mm(pr.t[:, :], ones32.t[0:1, :], gt_row.t[0:1, half * 512:(half + 1) * 512], True, True,
                   [ones32.b, gt_row.b], [pr.b])
                act(gt_bc.t[:, half * 512:(half + 1) * 512], pr.t[:, :], AF.Identity, [pr.b], [gt_bc.b], scale=0.5)
            tt("dve", tmp16.t[:], lbl.t[:, 0:16], lbl.t[:, 16:32], ALU.subtract, [lbl.b], [tmp16.b])
            act(lb.t[:], tmp16.t[:], AF.Sigmoid, [tmp16.b], [lb.b])
            ts("dve", oml.t[:], lb.t[:], -1.0, 1.0, ALU.mult, ALU.add, [lb.b], [oml.b])
            act(tmp16.t[:], lam.t[:], AF.Exp, [lam.b], [tmp16.b], scale=-1.0)
            ts("dve", tmp16b.t[:], tmp16.t[:], 1.0 / 3.0, -0.5, ALU.mult, ALU.add, [tmp16.b], [tmp16b.b])
            tt("dve", tmp16b.t[:], tmp16b.t[:], tmp16.t[:], ALU.mult, [tmp16.b, tmp16b.b], [tmp16b.b])
            ts("dve", tmp16b.t[:], tmp16b.t[:], 1.0, None, ALU.add, None, [tmp16b.b], [tmp16b.b])
            tt("dve", tmp16b.t[:], tmp16b.t[:], tmp16.t[:], ALU.mult, [tmp16.b, tmp16b.b], [tmp16b.b])
            ts("dve", cA.t[:], tmp16b.t[:], -8.0, None, ALU.mult, None, [tmp16b.b], [cA.b])
            ts("dve", hcA.t[:], cA.t[:], 0.5, None, ALU.mult, None, [cA.b], [hcA.b])
            ts("dve", hbr.t[:], br.t[:], 0.5, None, ALU.mult, None, [br.b], [hbr.b])
            ts("dve", hbi.t[:], bi.t[:], 0.5, None, ALU.mult, None, [bi.b], [hbi.b])
            ts("dve", hbinT.t[:], binT.t[:], 0.5, None, ALU.mult, None, [binT.b], [hbinT.b])
            ts("dve", fc1.t[:], oml.t[:], 0.5, None, ALU.mult, None, [oml.b], [fc1.b])
            tt("dve", fc0.t[:], fc1.t[:], lb.t[:], ALU.add, [fc1.b, lb.b], [fc0.b])
            cp("dve", bv_bf.t[:], bv32.t[:], [bv32.b], [bv_bf.b])
            for h in range(8):
                k.op("dve", lambda e: e.memset(Sb[h].t[:], 0.0), writes=[Sb[h].b])
            k.barrier(skip=late)

        def make_uT(xtiles, uT, sh_t, scp1_t, nb_reads):
            n = len(xtiles) * 128
            for j in range(8):
                bank = pj.next()
                for i, xt in enumerate(xtiles):
                    tr(bank.t[:, i * 128:(i + 1) * 128], xt.t[:, j * 128:(j + 1) * 128], ident32.t[:],
                       [xt.b, ident32.b], [bank.b])
                act(uT.t[:, j, 0:n], bank.t[:, 0:n], AF.Identity, [bank.b] + nb_reads, [uT.b],
                    scale=scp1_t[:, j:j + 1], bias=sh_t[:, j:j + 1])

        def proj_fm(cb, uT, n, wring, evac):
            w = wring.next()
            k.dma("sp", w.t[:], WB[cb], writes=[w.b])
            bank = pj.next()
            for kc in range(8):
                mm(bank.t[:, 0:n], w.t[:, kc, :], uT.t[:, kc, 0:n], kc == 0, kc == 7, [w.b, uT.b], [bank.b])
            evac(bank)

        def v_proj(uT, nch, wv, vtok):
            for c in range(nch):
                for half in range(2):
                    bank = pj.next()
                    for kc in range(8):
                        mm(bank.t[:, :], uT.t[:, kc, c * 128:(c + 1) * 128],
                           fap(wv.t, half * 4 * 1024 + kc * 128, [[1024, 4], [1, 128]]), kc == 0, False,
                           [uT.b, wv.b], [bank.b])
                    mm(bank.t[:, :], onesb.t[0:1, :], bv_bf.t[0:1, half * 512:(half + 1) * 512], False, True,
                       [onesb.b, bv_bf.b], [bank.b])
                    cp("act", vtok.t[:, c, half * 512:(half + 1) * 512], bank.t[:, :], [bank.b], [vtok.b])

        def gla_local(f, P, rP, kin32, d1, dirn, nch, n):
            pos = 0 if dirn == 0 else 127
            cp("dve", fap(d1.t, pos, [[128, nch]]), fap(f.t, pos, [[128, nch]]), [f.b], [d1.b])
            if dirn == 0:
                o_ap, f_ap, d_ap = P.t[:, 0:n], f.t[:, 0:n], d1.t[:, 0:n]
            else:
                o_ap, f_ap, d_ap = (fap(P.t, n - 1, [[-1, n]]), fap(f.t, n - 1, [[-1, n]]), fap(d1.t, n - 1, [[-1, n]]))
            k.op("dve", lambda e: e.tensor_tensor_scan(out=o_ap, data0=f_ap, data1=d_ap, initial=1.0,
                                                       op0=ALU.mult, op1=ALU.max), [f.b, d1.b], [P.b])
            k.op("dve", lambda e: e.reciprocal(out=rP.t[:, 0:n], in_=P.t[:, 0:n]), [P.b], [rP.b])
            ts("pool", f.t[:, 0:n], f.t[:, 0:n], -1.0, 1.0, ALU.mult, ALU.add, [f.b], [f.b])
            tt("pool", kin32.t[:, 0:n], f.t[:, 0:n], rP.t[:, 0:n], ALU.mult, [f.b, rP.b], [kin32.b])

        def plast_bc(P, dirn, nch):
            return fap(P.t, 127 if dirn == 0 else 0, [[128, nch], [0, 128]])

        def plast(P, dirn, nch):
            return fap(P.t, 127 if dirn == 0 else 0, [[128, nch]])

        def f_evac(ftile, n, idx, cbidx, eng2):
            def ev(bank):
                act(ftile.t[:, 0:n], bank.t[:, 0:n], AF.Tanh, [bank.b, hbinT.b], [ftile.b],
                    scale=0.5, bias=hbinT.t[:, cbidx:cbidx + 1])
                ts(eng2, ftile.t[:, 0:n], ftile.t[:, 0:n], fc1.t[:, idx:idx + 1], fc0.t[:, idx:idx + 1],
                   ALU.mult, ALU.add, [ftile.b, fc1.b, fc0.b], [ftile.b])
            return ev

        def mixer_bufs(name, nsub):
            return {kk: [k.buf(f"{name}_{kk}{i}") for i in range(nsub)] for kk in ("xc", "xb", "A", "B0", "B1")} | {"raw": [k.buf(f"{name}_raw{i}") for i in range(4)]}

        def mixer_b(j, raw, xc, xcb, A, B, Tn, W, sub, tmp, diag, mb, h0f, h0b, fin, wr_bf, wi_bf, G):
            rows = Tn // W
            nr = sub // W
            nsub = Tn // sub

            def pm(t, s0):
                return fap(t, s0 // W, [[1, nr], [rows, W]])

            def rv(t, s0):
                return fap(t, s0, [[W, nr], [1, W]])

            for i in range(5):
                ts("pool", diag.t[:, i, :], ident32.t[:, :], convw.t[:, j * 5 + i:j * 5 + i + 1], None, ALU.mult, None,
                   [ident32.b, convw.b], [diag.b])
            csz = max(Tn // 4, 1)

            def rawb(lo_, hi_):
                return mb["raw"][max(lo_, 0) // csz:min((min(hi_, Tn) - 1) // csz, 3) + 1]

            def front(si):
                s0 = si * sub
                rb = rawb(s0 - 2 * W, s0 + sub + 2 * W)
                bank = pj.next()
                order = [2, 1, 3]
                for n_, i in enumerate(order):
                    off = (i - 2) * W
                    lo, hi = max(s0, -off), min(s0 + sub, Tn - off)
                    mm(bank.t[:, lo - s0:hi - s0], diag.t[:, i, :], raw.t[:, lo + off:hi + off], n_ == 0, n_ == 2,
                       [diag.b] + rb, [bank.b])
                act(xc.t[:, s0:s0 + sub], bank.t[:, 0:sub], AF.Identity, [bank.b, convb.b], [mb["xc"][si]],
                    bias=convb.t[:, j:j + 1])
                for i in (0, 4):
                    off = (i - 2) * W
                    lo, hi = max(s0, -off), min(s0 + sub, Tn - off)
                    stt(xc.t[:, lo:hi], raw.t[:, lo + off:hi + off], convw.t[:, j * 5 + i:j * 5 + i + 1], xc.t[:, lo:hi],
                        ALU.mult, ALU.add, rb + [convw.b, mb["xc"][si]], [mb["xc"][si]])
                cp("pool", xcb.t[:, s0:s0 + sub], xc.t[:, s0:s0 + sub], [mb["xc"][si]], [mb["xb"][si]])

            for dirn in range(2):
                Bd = B if dirn == 0 else raw
                BS = mb["B0"] if dirn == 0 else mb["B1"]
                AS = mb["A"]
                gi = dirn * 8 + j
                def igate(si):
                    s0 = si * sub
                    pi = ps[4 + si % 2]
                    mm(pi.t[:, 0:sub], wi_bf.t[:, gi * 128:(gi + 1) * 128], xcb.t[:, s0:s0 + sub], True, True,
                       [wi_bf.b, mb["xb"][si]], [pi.b])
                    act(pm(Bd.t, s0), rv(pi.t, 0), AF.Tanh, [pi.b, hbi.b], [BS[si]] + (mb["raw"] if dirn == 1 else []),
                        scale=0.5, bias=hbi.t[:, gi:gi + 1])

                if dirn == 1:
                    for si in range(nsub):
                        igate(si)
                for g0 in range(0, nsub, G):
                    grp = list(range(g0, min(nsub, g0 + G)))
                    tms = {}
                    if dirn == 0:
                        for si in grp:
                            front(si)
                    for si in grp:
                        s0 = si * sub
                        sl = slice(s0, s0 + sub)
                        pr = ps[2 + si % 2]
                        mm(pr.t[:, 0:sub], wr_bf.t[:, gi * 128:(gi + 1) * 128], xcb.t[:, sl], True, True,
                           [wr_bf.b, mb["xb"][si]], [pr.b])
                        act(pm(A.t, s0), rv(pr.t, 0), AF.Tanh, [pr.b, hbr.b], [AS[si]], scale=0.5, bias=hbr.t[:, gi:gi + 1])
                        if dirn == 0:
                            igate(si)
                    for si in grp:
                        s0 = si * sub
                        t_m = tmp.next()
                        tms[si] = t_m
                        act(pm(A.t, s0), pm(A.t, s0), AF.Exp, [AS[si], hcA.b], [AS[si]], scale=hcA.t[:, gi:gi + 1],
                            bias=hcA.t[:, gi:gi + 1])
                        stt(rv(t_m.t, 0), pm(A.t, s0), 1.0, pm(A.t, s0), ALU.mult, ALU.mult, [AS[si]], [t_m.b])
                    for si in grp:
                        t_m = tms[si]
                        act(rv(t_m.t, 0), rv(t_m.t, 0), AF.Sqrt, [t_m.b], [t_m.b], scale=-0.25, bias=0.25)
                    for si in grp:
                        s0 = si * sub
                        t_m = tms[si]
                        stt(pm(Bd.t, s0), pm(Bd.t, s0), 1.0, rv(xc.t, s0), ALU.add, ALU.mult, [BS[si], mb["xc"][si]], [BS[si]])
                        tt("pool" if si % 2 else "dve", pm(Bd.t, s0), pm(Bd.t, s0), rv(t_m.t, 0), ALU.mult,
                           [BS[si], t_m.b], [BS[si]])
                if dirn == 0:
                    a_ap, b_ap = A.t[:, 0:Tn], Bd.t[:, 0:Tn]
                    init = h0f
                else:
                    a_ap, b_ap = fap(A.t, Tn - 1, [[-1, Tn]]), fap(Bd.t, Tn - 1, [[-1, Tn]])
                    init = h0b
                k.op("dve", lambda e: e.tensor_tensor_scan(out=b_ap, data0=a_ap, data1=b_ap, initial=init,
                                                           op0=ALU.mult, op1=ALU.add),
                     AS + BS + [hcf.b, hcb.b], BS)
                fin(dirn, Bd, BS)

        with ExitStack() as st:
            xring = ring(st, "xt", [128, D], F32, 6)
            uT_r = ring(st, "uT", [128, 8, NT], BF16, 2)
            wring = ring(st, "w", [128, 8, 128], BF16, 6)
            wv = sb(st, "wv", [128, 8, 8, 128], BF16)
            vtok = sb(st, "vtok", [128, NCH, D], BF16)
            qraw_r = ring(st, "qraw", [128, NT], F32, 2)
            ff_r = ring(st, "ff", [128, NT], F32, 2)
            fb_r = ring(st, "fb", [128, NT], F32, 2)
            P_r = ring(st, "P", [128, NT], F32, 4)
            rP_r = ring(st, "rP", [128, NT], F32, 2)
            kin_r = ring(st, "kin", [128, NT], F32, 2)
            kinv_r = ring(st, "kinv", [128, NT], BF16, 4)
            qdec_r = ring(st, "qdec", [128, NT], BF16, 4)
            kend_r = ring(st, "kend", [128, NT], BF16, 4)
            kendT_r = ring(st, "kendT", [128, 2 * NCH, 128], BF16, 2)
            d1_r = [ring(st, "d1f", [128, NT], F32, 2), ring(st, "d1b", [128, NT], F32, 2)]
            t1_r = ring(st, "t1", [128, NT], F32, 2)
            t2_r = ring(st, "t2", [128, NT], F32, 2)
            scs_r = ring(st, "scs", [128, NT], BF16, 2)
            decf_r = ring(st, "decf", [128, NCH], F32, 2)
            rpl_r = ring(st, "rpl", [128, NCH], F32, 4)
            sst_r = ring(st, "sst", [128, NCH, 128], BF16, 2)
            s32_r = ring(st, "s32", [128, NCH, 128], F32, 2)
            ost_r = ring(st, "ost", [128, NT], F32, 2)
            ubst_r = ring(st, "ubst", [128, NCH, 128], F32, 2)
            z5st_r = ring(st, "z5st", [128, NT], F32, 2)
            craw = sb(st, "craw", [128, 8, CTXL], F32)
            cxc = sb(st, "cxc", [128, CTXL], F32)
            cxcb = sb(st, "cxcb", [128, CTXL], BF16)
            cA_ = sb(st, "cA_", [128, CTXL], F32)
            cB_ = sb(st, "cB_", [128, CTXL], F32)
            crawj = sb(st, "crawj", [128, CTXL], F32)
            ctmp = ring(st, "ctmp", [128, CTXL], F32, 3)

            k.dma("sp", wv.t[:], WB[24:32].rearrange("cb p kc c -> p cb kc c"), writes=[wv.b])
            maskf4 = sb(st, "maskf4", [128, 4, 128], F32)
            maskb4 = sb(st, "maskb4", [128, 4, 128], F32)
            k.dma("sp", maskf4.t[:], bass.AP(tensor=maskf_d.tensor, offset=maskf_d.offset,
                                              ap=[[128, 128], [0, 4], [1, 128]]), writes=[maskf4.b])
            k.dma("sp", maskb4.t[:], bass.AP(tensor=maskb_d.tensor, offset=maskb_d.offset,
                                              ap=[[128, 128], [0, 4], [1, 128]]), writes=[maskb4.b])
            Sf = [sb(st, f"Sf{h}", [128, 128], F32) for h in range(8)]
            for h in range(8):
                k.op("pool", lambda e: e.memset(Sf[h].t[:], 0.0), writes=[Sf[h].b])
            wr_bf = sb(st, "wr_bf", [128, 2048], BF16)
            wi_bf = sb(st, "wi_bf", [128, 2048], BF16)
            k.dma("pool", wr_bf.t[:], wr_d[:, :], writes=[wr_bf.b])
            k.dma("pool", wi_bf.t[:], wi_d[:, :], writes=[wi_bf.b])
            for rg in d1_r:
                for tl in rg.tiles:
                    k.op("pool", lambda e: e.memset(tl.t[:], 0.0), writes=[tl.b])

            def startA(h, uT, n, want_q, dirs=(0, 1)):
                hd = {"h": h}
                parts = []
                if want_q:
                    qraw = qraw_r.next()
                    hd["qraw"] = qraw
                    parts.append(lambda: proj_fm(h, uT, n, wring, lambda bank: act(
                        qraw.t[:, 0:n], bank.t[:, 0:n], AF.Silu, [bank.b, binT.b], [qraw.b], bias=binT.t[:, h:h + 1])))
                hd["f"] = {}
                if 0 in dirs:
                    ff = ff_r.next()
                    hd["f"][0] = ff
                    parts.append(lambda: proj_fm(8 + h, uT, n, wring, f_evac(ff, n, h, 8 + h, "pool")))
                if 1 in dirs:
                    fbt = fb_r.next()
                    hd["f"][1] = fbt
                    parts.append(lambda: proj_fm(16 + h, uT, n, wring, f_evac(fbt, n, 8 + h, 16 + h, "pool")))
                hd["parts"] = parts
                return hd

            def stageB(hd, n, nch, want_out, dirs=(0, 1)):
                tl = {}

                def scan(out_t, f_t, d_t, fwd):
                    if fwd:
                        o_ap, f_ap, d_ap = out_t[:, 0:n], f_t[:, 0:n], d_t[:, 0:n]
                    else:
                        o_ap, f_ap, d_ap = (fap(out_t, n - 1, [[-1, n]]), fap(f_t, n - 1, [[-1, n]]), fap(d_t, n - 1, [[-1, n]]))
                    return lambda e: e.tensor_tensor_scan(out=o_ap, data0=f_ap, data1=d_ap, initial=1.0,
                                                          op0=ALU.mult, op1=ALU.max)

                for dirn in dirs:
                    f = hd["f"][dirn]
                    P, Q, k32 = P_r.next(), rP_r.next(), kin_r.next()
                    d1, d1o = d1_r[dirn].next(), d1_r[1 - dirn].next()
                    tl[dirn] = (f, P, Q, k32, d1, d1o)
                    hd[("P", dirn)] = P
                    pos = 0 if dirn == 0 else 127
                    cp("pool", fap(d1.t, pos, [[128, nch]]), fap(f.t, pos, [[128, nch]]), [f.b], [d1.b])
                    cp("pool", fap(d1o.t, 127 - pos, [[128, nch]]), fap(f.t, 127 - pos, [[128, nch]]), [f.b], [d1o.b])
                for dirn in dirs:
                    f, P, Q, k32, d1, d1o = tl[dirn]
                    k.op("dve", scan(P.t, f.t, d1.t, dirn == 0), [f.b, d1.b], [P.b])
                    k.op("dve", scan(Q.t, f.t, d1o.t, dirn == 1), [f.b, d1o.b], [Q.b])
                    ts("pool", f.t[:, 0:n], f.t[:, 0:n], -1.0, 1.0, ALU.mult, ALU.add, [f.b], [f.b])
                for dirn in dirs:
                    f, P, Q, k32, d1, d1o = tl[dirn]
                    rPl = rpl_r.next()
                    k.op("dve", lambda e: e.reciprocal(out=rPl.t[:, 0:nch], in_=plast(P, dirn, nch)), [P.b], [rPl.b])
                    tl[dirn] = tl[dirn] + (rPl,)
                    if want_out:
                        qdec = qdec_r.next()
                        qraw = hd["qraw"]
                        stt(qdec.t[:, 0:n], qraw.t[:, 0:n], QSCALE, P.t[:, 0:n], ALU.mult, ALU.mult,
                            [qraw.b, P.b], [qdec.b])
                        hd[("qdec", dirn)] = qdec
                for dirn in dirs:
                    f, P, Q, k32, d1, d1o, rPl = tl[dirn]
                    if dirn == 0:
                        tt("pool", k32.t[:, 0:n - 1], f.t[:, 0:n - 1], Q.t[:, 1:n], ALU.mult, [f.b, Q.b], [k32.b])
                        cp("pool", fap(k32.t, 127, [[128, nch]]), fap(f.t, 127, [[128, nch]]), [f.b], [k32.b])
                    else:
                        tt("pool", k32.t[:, 1:n], f.t[:, 1:n], Q.t[:, 0:n - 1], ALU.mult, [f.b, Q.b], [k32.b])
                        cp("pool", fap(k32.t, 0, [[128, nch]]), fap(f.t, 0, [[128, nch]]), [f.b], [k32.b])
                    kend = kend_r.next()
                    cp("act", kend.t[:, 0:n], k32.t[:, 0:n], [k32.b], [kend.b])
                    hd[("kend", dirn)] = kend
                    if want_out:
                        kinv = kinv_r.next()
                        tt("pool", fap(kinv.t, 0, [[128, nch], [1, 128]]), fap(k32.t, 0, [[128, nch], [1, 128]]),
                           fap(rPl.t, 0, [[1, nch], [0, 128]]), ALU.mult, [k32.b, rPl.b], [kinv.b])
                        hd[("kinv", dirn)] = kinv

            def pe1(hd, nch, want_out, dirs=(0, 1)):
                if want_out:
                    for dirn, bank in ((0, ps[2]), (1, ps[3])):
                        kinv, qdec = hd[("kinv", dirn)], hd[("qdec", dirn)]
                        for c in range(nch):
                            sl = slice(c * 128, (c + 1) * 128)
                            mm(bank.t[:, sl], kinv.t[:, sl], qdec.t[:, sl], True, True, [kinv.b, qdec.b], [bank.b])
                kendT = kendT_r.next()
                hd["kendT"] = kendT
                for dirn in dirs:
                    kend = hd[("kend", dirn)]
                    for c in range(nch):
                        tr(trb.t[:, (dirn * nch + c) * 128:(dirn * nch + c + 1) * 128], kend.t[:, c * 128:(c + 1) * 128],
                           identb.t[:], [kend.b, identb.b], [trb.b])
                lo, hi = min(dirs) * nch * 128, (max(dirs) + 1) * nch * 128
                cp("act", fap(kendT.t, lo, [[1, hi - lo]]), trb.t[:, lo:hi], [trb.b], [kendT.b])

            def pe2(hd, nch, dirs=(0, 1)):
                h, kendT = hd["h"], hd["kendT"]
                for dirn in dirs:
                    bank = ps[4 + dirn]
                    for c in range(nch):
                        mm(bank.t[:, c * 128:(c + 1) * 128], kendT.t[:, dirn * nch + c, :],
                           vtok.t[:, c, h * 128:(h + 1) * 128], True, True, [kendT.b, vtok.b], [bank.b])

            cx = [xring.next() for _ in range(2)]
            for i in range(2):
                k.dma("sp", cx[i].t[:], ctx_d[i * 128:(i + 1) * 128, :], writes=[cx[i].b])
            uT = uT_r.next()
            make_uT(cx, uT, modc.t, scp1c.t, [modc.b, scp1c.b])
            v_proj(uT, 2, wv, vtok)
            for h in range(8):
                hd = startA(h, uT, CTXL, False)
                for p in hd["parts"]:
                    p()
                stageB(hd, CTXL, 2, False)
                pe1(hd, 2, False)
                pe2(hd, 2)
                Pf, Pb = hd[("P", 0)], hd[("P", 1)]
                for c in range(2):
                    stt(Sf[h].t[:], Sf[h].t[:], fap(Pf.t, c * 128 + 127, [[1, 1]]), ps[4].t[:, c * 128:(c + 1) * 128],
                        ALU.mult, ALU.add, [Sf[h].b, Pf.b, ps[4].b], [Sf[h].b])
                for c in (1, 0):
                    stt(Sb[h].t[:], Sb[h].t[:], fap(Pb.t, c * 128, [[1, 1]]), ps[5].t[:, c * 128:(c + 1) * 128],
                        ALU.mult, ALU.add, [Sb[h].b, Pb.b, ps[5].b], [Sb[h].b])
            for j in range(8):
                proj_fm(40 + j, uT, CTXL, wring,
                        lambda bank: act(craw.t[:, j, :], bank.t[:, 0:CTXL], AF.Identity, [bank.b, binT.b], [craw.b],
                                         bias=binT.t[:, 40 + j:41 + j]))
            cmb = mixer_bufs("cmb", 1)
            cdiag = sb(st, "cdiag", [128, 5, 128], F32)
            for j in range(8):
                cp("pool", crawj.t[:], craw.t[:, j, :], [craw.b], cmb["raw"] + cmb["B1"])

                def fin_ctx(dirn, Bd, BS):
                    if dirn == 0:
                        cp("dve", hcf.t[:, j:j + 1], Bd.t[:, CTXL - 1:CTXL], BS, [hcf.b])
                    else:
                        cp("dve", hcb.t[:, j:j + 1], Bd.t[:, 0:1], BS, [hcb.b])
                mixer_b(j, crawj, cxc, cxcb, cA_, cB_, CTXL, 1, CTXL, ctmp, cdiag, cmb, 0.0, 0.0, fin_ctx, wr_bf, wi_bf, 1)

            def load_uT(stile):
                t0 = stile * NT
                xs = [xring.next() for _ in range(NCH)]
                for i in range(NCH):
                    k.dma("sp", xs[i].t[:], x_d[t0 + i * 128:t0 + (i + 1) * 128, :], writes=[xs[i].b])
                u = uT_r.next()
                make_uT(xs, u, modx.t, scp1x.t, [modx.b, scp1x.b])
                return u

            uT = load_uT(0)
            for stile in range(NST):
                t0 = stile * NT
                uT_next = None
                if stile >= NST_OWN:
                    hd = startA(0, uT, NT, False, (1,))
                    hd["parts"][0]()
                    v_proj(uT, NCH, wv, vtok)
                    for h in range(8):
                        nxt = startA(h + 1, uT, NT, False, (1,)) if h < 7 else None
                        stageB(hd, NT, NCH, False, (1,))
                        cp("dve", dec_b.t[:, h, stile * NCH:(stile + 1) * NCH], plast(hd[("P", 1)], 1, NCH),
                           [hd[("P", 1)].b], [dec_b.b])
                        if nxt:
                            nxt["parts"][0]()
                        pe1(hd, NCH, False, (1,))
                        z5 = z5st_r.next()
                        proj_fm(40 + h, uT, NT, wring,
                                lambda bank: act(z5.t[:, :], bank.t[:, :], AF.Identity, [bank.b, binT.b], [z5.b],
                                                 bias=binT.t[:, 40 + h:41 + h]))
                        k.dma("act", XT[h, :, t0:t0 + NT], z5.t[:], reads=[z5.b])
                        pe2(hd, NCH, (1,))
                        ubst = ubst_r.next()
                        cp("act", fap(ubst.t, 0, [[1, NT]]), ps[5].t[:, :], [ps[5].b], [ubst.b])
                        k.dma("act", UB[h, :, stile * NCH:(stile + 1) * NCH, :], ubst.t[:], reads=[ubst.b])
                        if h == 5 and stile + 1 < NST:
                            uT_next = load_uT(stile + 1)
                        hd = nxt
                    uT = uT_next
                    continue
                hd = startA(0, uT, NT, True)
                for p in hd["parts"]:
                    p()
                v_proj(uT, NCH, wv, vtok)
                for h in range(8):
                    nxt = startA(h + 1, uT, NT, True) if h < 7 else None
                    stageB(hd, NT, NCH, True)
                    Pf, Pb = hd[("P", 0)], hd[("P", 1)]
                    decf = decf_r.next()
                    cp("dve", decf.t[:, :], plast(Pf, 0, NCH), [Pf.b], [decf.b])
                    cp("dve", dec_b.t[:, h, stile * NCH:(stile + 1) * NCH], plast(Pb, 1, NCH), [Pb.b], [dec_b.b])
                    if nxt:
                        nxt["parts"][0]()
                        nxt["parts"][1]()
                    pe1(hd, NCH, True)
                    t1, t2, scs = t1_r.next(), t2_r.next(), scs_r.next()
                    tt("dve", t1.t[:, :], ps[2].t[:, :], fap(maskf4.t, 0, [[1, 512]]), ALU.mult, [ps[2].b, maskf4.b], [t1.b])
                    tt("dve", t2.t[:, :], ps[3].t[:, :], fap(maskb4.t, 0, [[1, 512]]), ALU.mult, [ps[3].b, maskb4.b], [t2.b])
                    tt("pool", scs.t[:, :], t1.t[:, :], t2.t[:, :], ALU.add, [t1.b, t2.b], [scs.b])
                    if nxt:
                        nxt["parts"][2]()
                    pe2(hd, NCH)
                    ubst = ubst_r.next()
                    cp("act", fap(ubst.t, 0, [[1, NT]]), ps[5].t[:, :], [ps[5].b], [ubst.b])
                    k.dma("act", UB[h, :, stile * NCH:(stile + 1) * NCH, :], ubst.t[:], reads=[ubst.b])
                    sst, s32 = sst_r.next(), s32_r.next()
                    cp("pool", sst.t[:, 0, :], Sf[h].t[:], [Sf[h].b], [sst.b])
                    for c in range(NCH):
                        src = Sf[h].t[:] if c == 0 else s32.t[:, c - 1, :]
                        dst = Sf[h].t[:] if c == NCH - 1 else s32.t[:, c, :]
                        stt(dst, src, decf.t[:, c:c + 1], ps[4].t[:, c * 128:(c + 1) * 128], ALU.mult, ALU.add,
                            [Sf[h].b, s32.b, decf.b, ps[4].b], [Sf[h].b] if c == NCH - 1 else [s32.b])
                    cp("pool", fap(sst.t, 128, [[1, (NCH - 1) * 128]]), fap(s32.t, 0, [[1, (NCH - 1) * 128]]), [s32.b], [sst.b])
                    z5 = z5st_r.next()
                    proj_fm(40 + h, uT, NT, wring,
                            lambda bank: act(z5.t[:, :], bank.t[:, :], AF.Identity, [bank.b, binT.b], [z5.b],
                                             bias=binT.t[:, 40 + h:41 + h]))
                    k.dma("act", XT[h, :, t0:t0 + NT], z5.t[:], reads=[z5.b])
                    if h == 5 and stile + 1 < NST:
                        uT_next = load_uT(stile + 1)
                    qdf = hd[("qdec", 0)]
                    for c in range(NCH):
                        sl = slice(c * 128, (c + 1) * 128)
                        mm(ps[6].t[:, sl], vtok.t[:, c, h * 128:(h + 1) * 128], scs.t[:, sl], True, False,
                           [vtok.b, scs.b], [ps[6].b])
                        mm(ps[6].t[:, sl], sst.t[:, c, :], qdf.t[:, sl], False, True, [sst.b, qdf.b], [ps[6].b])
                    ost = ost_r.next()
                    cp("act", ost.t[:, :], ps[6].t[:, :], [ps[6].b], [ost.b])
                    k.dma("act", OP[h, :, t0:t0 + NT], ost.t[:], reads=[ost.b])
                    qdb = hd[("qdec", 1)]
                    k.dma("sp", QDB[h, :, t0:t0 + NT], qdb.t[:], reads=[qdb.b])
                    hd = nxt
                uT = uT_next
            k.barrier()

        with ExitStack() as st:
            raw = sb(st, "raw", [128, T], F32)
            xc = sb(st, "xc", [128, T], F32)
            xcb = sb(st, "xcb", [128, T], BF16)
            A = sb(st, "A", [128, T], F32)
            B = sb(st, "B", [128, T], F32)
            tmp = ring(st, "mtmp", [128, 512], F32, 10)
            diag = sb(st, "diag", [128, 5, 128], F32)
            wr_bf = sb(st, "wr_bf", [128, 2048], BF16)
            wi_bf = sb(st, "wi_bf", [128, 2048], BF16)
            k.dma("pool", wr_bf.t[:], wr_d[:, :], writes=[wr_bf.b])
            k.dma("pool", wi_bf.t[:], wi_d[:, :], writes=[wi_bf.b])
            xmb = mixer_bufs("xmb", T // 512)
            for j in range(8):
                for q4 in range(4):
                    k.dma("sp" if q4 % 2 == 0 else "act", raw.t[:, q4 * 2048:(q4 + 1) * 2048], XT[j, :, q4 * 2048:(q4 + 1) * 2048],
                          writes=[xmb["raw"][q4]] + xmb["B1"])

                def fin_x(dirn, Bd, BS):
                    if dirn == 1:
                        for q4 in range(2):
                            r0 = q4 * (ROWS // 4)
                            cm = [[1, ROWS // 4], [ROWS, GW]]
                            tt("dve" if q4 == 0 else "pool", fap(xc.t, r0 * GW, [[GW, ROWS // 4], [1, GW]]), fap(B.t, r0, cm),
                               fap(Bd.t, r0, cm), ALU.add, xmb["B0"] + xmb["B1"], xmb["xc"][q4 * 4:(q4 + 1) * 4])
                        for q4 in range(2):
                            k.dma("sp", HS[j, :, q4 * 2048:(q4 + 1) * 2048], xc.t[:, q4 * 2048:(q4 + 1) * 2048],
                                  reads=xmb["xc"][q4 * 4:(q4 + 1) * 4], key=xc.b)
                mixer_b(j, raw, xc, xcb, A, B, T, GW, 512, tmp, diag, xmb, hcf.t[:, j:j + 1], hcb.t[:, j:j + 1], fin_x,
                        wr_bf, wi_bf, 8)
            k.barrier()

        with ExitStack() as st:
            xring = ring(st, "xt2", [128, D], F32, 8)
            uT = sb(st, "uT2", [128, 8, NT], BF16)
            wring = ring(st, "w2", [128, 8, 128], BF16, 4)
            pa_bf = sb(st, "pa_bf", [128, 8, D], BF16)
            pb_bf = sb(st, "pb_bf", [128, 8, D], BF16)
            wo_bf = sb(st, "wo_bf", [128, 8, D], BF16)
            lng = sb(st, "lng", [128, D], F32)
            lnb = sb(st, "lnb", [128, D], F32)
            op_r = ring(st, "opl", [128, NT], F32, 2)
            qdb_r = ring(st, "qdbl", [128, NT], BF16, 2)
            ub_r = ring(st, "ubl", [128, NCH, 128], F32, 2)
            sbs_r = ring(st, "sbs", [128, NCH, 128], BF16, 2)
            s32_r = ring(st, "s32b", [128, NCH, 128], F32, 1)
            o_r = ring(st, "o", [128, NT], F32, 2)
            sq_r = ring(st, "sq", [128, NT], F32, 2)
            rs_r = ring(st, "rs", [128, NT], F32, 2)
            g4_r = ring(st, "g4", [128, NT], F32, 2)
            oaT = sb(st, "oaT", [128, 8, NT], BF16)
            obT = sb(st, "obT", [128, 8, NT], BF16)
            yT = sb(st, "yT", [128, 8, NT], BF16)
            h_r = ring(st, "hl", [128, NT], F32, 2)
            g6_r = ring(st, "g6", [128, NT], F32, 2)
            s7_r = ring(st, "s7", [128, NT], F32, 2)
            s8_r = ring(st, "s8", [128, NT], F32, 2)
            ta_r = ring(st, "ta", [128, NT], F32, 2)
            tb_r = ring(st, "tb", [128, NT], F32, 2)
            r_r = ring(st, "r", [128, D], F32, 2)
            st6_r = ring(st, "st6", [128, 12], F32, 2)
            mv_r = ring(st, "mv", [128, 4], F32, 2)
            rms_eps_t = sb(st, "rms_eps", [128, 1], F32)
            ln_eps_t = sb(st, "ln_eps", [128, 1], F32)
            k.op("dve", lambda e: e.memset(rms_eps_t.t[:], RMS_EPS), writes=[rms_eps_t.b])
            k.op("dve", lambda e: e.memset(ln_eps_t.t[:], LN_EPS), writes=[ln_eps_t.b])
            k.dma("sp", pa_bf.t[:], PAB[:, :, :], writes=[pa_bf.b])
            k.dma("sp", pb_bf.t[:], PBB[:, :, :], writes=[pb_bf.b])
            k.dma("sp", wo_bf.t[:], WOB[:, :, :], writes=[wo_bf.b])
            k.dma("sp", lng.t[:], lng_d[:, :], writes=[lng.b])
            k.dma("sp", lnb.t[:], lnb_d[:, :], writes=[lnb.b])
            def start_tile2(stile_):
                t0_ = stile_ * NT
                xs_ = [xring.next() for _ in range(NCH)]
                for i in range(NCH):
                    k.dma("sp", xs_[i].t[:], x_d[t0_ + i * 128:t0_ + (i + 1) * 128, :], writes=[xs_[i].b])
                make_uT(xs_, uT, modx.t, scp1x.t, [modx.b, scp1x.b])
                return xs_

            xs_pref = None
            for stile in range(NST - 1, -1, -1):
                t0 = stile * NT
                if stile >= NST_OWN:
                    for h in range(8):
                        ubl = ub_r.next()
                        k.dma("sp", ubl.t[:], UB[h, :, stile * NCH:(stile + 1) * NCH, :], writes=[ubl.b])
                        for c in range(NCH - 1, -1, -1):
                            stt(Sb[h].t[:], Sb[h].t[:], dec_b.t[:, h, stile * NCH + c:stile * NCH + c + 1], ubl.t[:, c, :],
                                ALU.mult, ALU.add, [Sb[h].b, dec_b.b, ubl.b], [Sb[h].b])
                    continue
                if xs_pref is None:
                    xs_pref = start_tile2(stile)
                xs = xs_pref
                xs_pref = None
                def loads_rec(h):
                    opl, qdbl, ubl = op_r.next(), qdb_r.next(), ub_r.next()
                    k.dma("sp", opl.t[:], OP[h, :, t0:t0 + NT], writes=[opl.b])
                    k.dma("sp", qdbl.t[:], QDB[h, :, t0:t0 + NT], writes=[qdbl.b])
                    k.dma("sp", ubl.t[:], UB[h, :, stile * NCH:(stile + 1) * NCH, :], writes=[ubl.b])
                    sbs, s32 = sbs_r.next(), s32_r.next()
                    cp("pool", sbs.t[:, NCH - 1, :], Sb[h].t[:], [Sb[h].b], [sbs.b])
                    for c in range(NCH - 1, -1, -1):
                        src = Sb[h].t[:] if c == NCH - 1 else s32.t[:, c, :]
                        dst = Sb[h].t[:] if c == 0 else s32.t[:, c - 1, :]
                        stt(dst, src, dec_b.t[:, h, stile * NCH + c:stile * NCH + c + 1], ubl.t[:, c, :], ALU.mult, ALU.add,
                            [Sb[h].b, s32.b, dec_b.b, ubl.b], [Sb[h].b] if c == 0 else [s32.b])
                    cp("pool", fap(sbs.t, 0, [[1, (NCH - 1) * 128]]), fap(s32.t, 0, [[1, (NCH - 1) * 128]]), [s32.b], [sbs.b])
                    return opl, qdbl, sbs

                cur = loads_rec(0)
                for h in range(8):
                    opl, qdbl, sbs = cur
                    for c in range(NCH):
                        sl = slice(c * 128, (c + 1) * 128)
                        mm(ps[2].t[:, sl], sbs.t[:, c, :], qdbl.t[:, sl], True, True, [sbs.b, qdbl.b], [ps[2].b])
                    o = o_r.next()
                    tt("dve", o.t[:, :], ps[2].t[:, :], opl.t[:, :], ALU.add, [ps[2].b, opl.b], [o.b])
                    sq = sq_r.next()
                    act(sq.t[:, :], o.t[:, :], AF.Square, [o.b], [sq.b])
                    if h < 7:
                        cur = loads_rec(h + 1)
                    g4 = g4_r.next()
                    proj_fm(32 + h, uT, NT, wring, lambda bank: act(g4.t[:, :], bank.t[:, :], AF.Silu, [bank.b, binT.b],
                                                                     [g4.b], bias=binT.t[:, 32 + h:33 + h]))
                    hl, g6 = h_r.next(), g6_r.next()
                    k.dma("sp", hl.t[:], HS[h, :, t0:t0 + NT], writes=[hl.b])
                    proj_fm(48 + h, uT, NT, wring, lambda bank: act(g6.t[:, :], bank.t[:, :], AF.Silu, [bank.b, binT.b],
                                                                     [g6.b], bias=binT.t[:, 48 + h:49 + h]))
                    tt("pool", obT.t[:, h, :], hl.t[:, :], g6.t[:, :], ALU.mult, [hl.b, g6.b], [obT.b])
                    mm(ps[3].t[:, :], ones32.t[:, :], sq.t[:, :], True, True, [ones32.b, sq.b], [ps[3].b])
                    rs = rs_r.next()
                    act(rs.t[:, :], ps[3].t[:, :], AF.Ln, [ps[3].b], [rs.b], scale=1.0 / 128.0, bias=rms_eps_t.t[:, 0:1])
                    act(rs.t[:, :], rs.t[:, :], AF.Exp, [rs.b], [rs.b], scale=-0.5)
                    stt(o.t[:, :], o.t[:, :], nag.t[:, 0:1], rs.t[:, :], ALU.mult, ALU.mult, [o.b, nag.b, rs.b], [o.b])
                    tt("pool", oaT.t[:, h, :], o.t[:, :], g4.t[:, :], ALU.mult, [o.b, g4.b], [oaT.b])
                for j in range(8):
                    s7, s8 = s7_r.next(), s8_r.next()
                    proj_fm(56 + j, uT, NT, wring, lambda bank: act(s7.t[:, :], bank.t[:, :], AF.Tanh, [bank.b, hbinT.b],
                                                                     [s7.b], scale=0.5, bias=hbinT.t[:, 56 + j:57 + j]))
                    proj_fm(64 + j, uT, NT, wring, lambda bank: act(s8.t[:, :], bank.t[:, :], AF.Tanh, [bank.b, hbinT.b],
                                                                     [s8.b], scale=0.5, bias=hbinT.t[:, 64 + j:65 + j]))
                    ta, tb = ta_r.next(), tb_r.next()
                    for kc in range(8):
                        mm(ps[4].t[:, :], pa_bf.t[:, kc, j * 128:(j + 1) * 128], oaT.t[:, kc, :], kc == 0, kc == 7,
                           [pa_bf.b, oaT.b], [ps[4].b])
                    stt(ta.t[:, :], s7.t[:, :], 1.0, ps[4].t[:, :], ALU.add, ALU.mult, [ps[4].b, s7.b], [ta.b])
                    for kc in range(8):
                        mm(ps[5].t[:, :], pb_bf.t[:, kc, j * 128:(j + 1) * 128], obT.t[:, kc, :], kc == 0, kc == 7,
                           [pb_bf.b, obT.b], [ps[5].b])
                    stt(tb.t[:, :], s8.t[:, :], 1.0, ps[5].t[:, :], ALU.add, ALU.mult, [ps[5].b, s8.b], [tb.b])
                    tt("pool", yT.t[:, j, :], ta.t[:, :], tb.t[:, :], ALU.add, [ta.b, tb.b], [yT.b])
                if stile > 0:
                    xs_pref = start_tile2(stile - 1)
                for i in range(NCH):
                    r = r_r.next()
                    for half in range(2):
                        bank = ps[6] if half == 0 else ps[3]
                        hs = slice(half * 512, (half + 1) * 512)
                        for kc in range(8):
                            mm(bank.t[:, :], yT.t[:, kc, i * 128:(i + 1) * 128], wo_bf.t[:, kc, hs], kc == 0, kc == 7,
                               [yT.b, wo_bf.b], [bank.b])
                        tt("dve", r.t[:, hs], bank.t[:, :], gt_bc.t[:, hs], ALU.mult, [bank.b, gt_bc.b], [r.b])
                    stt(r.t[:, :], xs[i].t[:, :], ALPHA, r.t[:, :], ALU.mult, ALU.add, [xs[i].b, r.b], [r.b])
                    st6, mv = st6_r.next(), mv_r.next()
                    k.op("dve", lambda e: e.bn_stats(out=st6.t[:, 0:6], in_=r.t[:, 0:512]), [r.b], [st6.b])
                    k.op("dve", lambda e: e.bn_stats(out=st6.t[:, 6:12], in_=r.t[:, 512:1024]), [r.b], [st6.b])
                    k.op("dve", lambda e: e.bn_aggr(out=mv.t[:, 0:2], in_=st6.t[:, 0:12]), [st6.b], [mv.b])
                    act(mv.t[:, 2:3], mv.t[:, 1:2], AF.Ln, [mv.b], [mv.b], scale=1.0, bias=ln_eps_t.t[:, 0:1])
                    act(mv.t[:, 2:3], mv.t[:, 2:3], AF.Exp, [mv.b], [mv.b], scale=-0.5)
                    stt(mv.t[:, 3:4], mv.t[:, 0:1], -1.0, mv.t[:, 2:3], ALU.mult, ALU.mult, [mv.b], [mv.b])
                    act(r.t[:, :], r.t[:, :], AF.Identity, [r.b, mv.b], [r.b], scale=mv.t[:, 2:3], bias=mv.t[:, 3:4])
                    tt("pool", r.t[:, :], r.t[:, :], lng.t[:, :], ALU.mult, [r.b, lng.b], [r.b])
                    tt("pool", r.t[:, :], r.t[:, :], lnb.t[:, :], ALU.add, [r.b, lnb.b], [r.b])
                    k.dma("pool", out_d[t0 + i * 128:t0 + (i + 1) * 128, :], r.t[:, :], reads=[r.b])
            k.barrier()
        if debug:
            print("instr counts", {n: e.n for n, e in k.engs.items()}, "waits", k.nwaits)
    return nc


def _fm(v, n):
    return np.ascontiguousarray(np.asarray(v, np.float32).reshape(n, 128).T)


_CACHE = {}


def _shared_inputs(w_mod, b_mod, w_in, b_in, lb_logits, norm_a_g, conv_w, conv_b, w_r, b_r, w_i, b_i, lam,
                   p_a, p_b, w_out, ln_g, ln_b, flip):
    f = np.float32
    w_in0 = np.asarray(w_in, f)[0]
    w4 = w_in0.reshape(8, 128, 72, 128).transpose(2, 1, 0, 3)
    b_in0 = np.asarray(b_in, f)[0]
    binT = _fm(b_in0, 72)
    lbl = np.asarray(lb_logits, f).reshape(2, 2, 8, 128)
    wr = np.asarray(w_r, f)[0]
    wi = np.asarray(w_i, f)[0]
    br_ = np.asarray(b_r, f)[0]
    bi_ = np.asarray(b_i, f)[0]
    lam_ = np.asarray(lam, f)[0]
    cw = np.asarray(conv_w, f)[0]
    z = np.zeros_like(cw[0])
    if flip:
        order = list(range(0, 8)) + list(range(16, 24)) + list(range(8, 16)) + list(range(24, 72))
        w4 = w4[order]
        binT = binT[:, order]
        lbl = lbl[:, ::-1]
        wr, wi, br_, bi_, lam_ = wr[::-1], wi[::-1], br_[::-1], bi_[::-1], lam_[::-1]
        taps = np.stack([cw[3], cw[2], cw[1], cw[0], z], axis=0)
    else:
        taps = np.stack([z, cw[0], cw[1], cw[2], cw[3]], axis=0)
    return {
        "w_mod": np.ascontiguousarray(np.asarray(w_mod, f)[0]),
        "bmodT": _fm(np.asarray(b_mod, f)[0], 24),
        "bmod_row": np.ascontiguousarray(np.asarray(b_mod, f)[0][None, :]),
        "w4": np.ascontiguousarray(w4),
        "binT": np.ascontiguousarray(binT),
        "bv_row": np.ascontiguousarray(b_in0[None, 3072:4096]),
        "lbl": np.ascontiguousarray(lbl.transpose(3, 0, 1, 2).reshape(128, 32)),
        "nag": np.ascontiguousarray(np.asarray(norm_a_g, f)[0].reshape(128, 1)),
        "convw": np.ascontiguousarray(taps.reshape(5, 8, 128).transpose(2, 1, 0).reshape(128, 40)),
        "convb": _fm(np.asarray(conv_b, f)[0], 8),
        "wr": np.ascontiguousarray(wr.transpose(2, 0, 1, 3).reshape(128, 2048)),
        "wi": np.ascontiguousarray(wi.transpose(2, 0, 1, 3).reshape(128, 2048)),
        "br": _fm(np.ascontiguousarray(br_).reshape(-1), 16),
        "bi": _fm(np.ascontiguousarray(bi_).reshape(-1), 16),
        "lam": _fm(np.ascontiguousarray(lam_).reshape(-1), 16),
        "pa": np.ascontiguousarray(np.asarray(p_a, f)[0].reshape(8, 128, D).transpose(1, 0, 2)),
        "pb": np.ascontiguousarray(np.asarray(p_b, f)[0].reshape(8, 128, D).transpose(1, 0, 2)),
        "wo": np.ascontiguousarray(np.asarray(w_out, f)[0].reshape(8, 128, D).transpose(1, 0, 2)),
        "lng": np.ascontiguousarray(np.broadcast_to(np.asarray(ln_g, f)[0][None, :], (128, D))),
        "lnb": np.ascontiguousarray(np.broadcast_to(np.asarray(ln_b, f)[0][None, :], (128, D))),
        "ident": np.eye(128, dtype=f),
        "maskf": np.triu(np.ones((128, 128), f)),
        "maskb": np.tril(np.ones((128, 128), f)),
    }


def kernel(x, c, ctx, c_ctx, w_mod, b_mod, w_in, b_in, lb_logits, norm_a_g, conv_w, conv_b,
           w_r, b_r, w_i, b_i, lam, p_a, p_b, w_out, ln_g, ln_b):
    f = np.float32
    x = np.asarray(x, f); ctx = np.asarray(ctx, f); c = np.asarray(c, f); c_ctx = np.asarray(c_ctx, f)
    params = (w_mod, b_mod, w_in, b_in, lb_logits, norm_a_g, conv_w, conv_b, w_r, b_r, w_i, b_i, lam,
              p_a, p_b, w_out, ln_g, ln_b)
    shared = [_shared_inputs(*params, flip=False), _shared_inputs(*params, flip=True)]
    in_maps = []
    for core in range(8):
        b, half = core // 2, core % 2
        m = dict(shared[half])
        if half == 0:
            m["x"] = np.ascontiguousarray(x[b])
            m["ctx"] = np.ascontiguousarray(ctx[b])
        else:
            m["x"] = np.ascontiguousarray(x[b][::-1])
            m["ctx"] = np.ascontiguousarray(ctx[b][::-1])
        m["cvec"] = np.ascontiguousarray(np.concatenate([_fm(c[b], 8), _fm(c_ctx, 8)], axis=1))
        in_maps.append(m)
    debug = bool(os.environ.get("MK_DEBUG"))
    key = ("nc", debug)
    if key not in _CACHE:
        _CACHE[key] = build_program(debug)
    nc = _CACHE[key]
    res = run_bass_kernel_spmd(nc, in_maps, core_ids=list(range(8)))
    if debug:
        _CACHE["last"] = res
    out = np.empty((4, T, D), f)
    for b in range(4):
        out[b, :T_OWN] = np.asarray(res.results[2 * b]["out"], f)
        out[b, T_OWN:] = np.asarray(res.results[2 * b + 1]["out"], f)[::-1]
    return out
```

```python
import os
import numpy as np
from contextlib import ExitStack
import concourse.bass as bass
import concourse.mybir as mybir
from concourse.bass_utils import run_bass_kernel_spmd

F32 = mybir.dt.float32
BF16 = mybir.dt.bfloat16
ALU = mybir.AluOpType
AF = mybir.ActivationFunctionType

D = 1024
T = 8192
NT = 512
NST = T // NT
T_OWN = T // 2
NST_OWN = T_OWN // NT
NCH = NT // 128
CTXL = 256
GW = 64
ROWS = T // GW
QSCALE = 128 ** -0.5
ALPHA = 2.0 ** 0.25
LN_EPS = 1e-5
RMS_EPS = 1e-6


class Buf:
    __slots__ = ("name", "last_w", "readers", "dma_sem", "dma_cnt")

    def __init__(self, name):
        self.name = name
        self.last_w = None
        self.readers = {}
        self.dma_sem = None
        self.dma_cnt = 0


class Eng:
    def __init__(self, name, h, sem):
        self.name, self.h, self.sem, self.n = name, h, sem, 0
        self.seen = {}


class K:
    SAME_ENG_SYNC = True

    def __init__(self, nc, stack):
        self.nc = nc
        self.stack = stack
        self.engs = {}
        for name, h in (("pe", nc.tensor), ("act", nc.scalar), ("dve", nc.vector),
                        ("pool", nc.gpsimd), ("sp", nc.sync)):
            sem = stack.enter_context(nc.semaphore("s_" + name))
            self.engs[name] = Eng(name, h, sem)
        self.bufs = []
        self.nwaits = 0
        self.free_sems = []

    def buf(self, name):
        b = Buf(name)
        self.bufs.append(b)
        return b

    def _wait(self, E, deps):
        need = {}
        for sem, val in deps:
            if val <= 0:
                continue
            if need.get(id(sem), (None, 0))[1] < val:
                need[id(sem)] = (sem, val)
        for sem, val in need.values():
            if sem is E.sem:
                if E.name == "pe" or not self.SAME_ENG_SYNC:
                    continue
            if E.seen.get(id(sem), 0) >= val:
                continue
            E.h.wait_ge(sem, val)
            self.nwaits += 1
            E.seen[id(sem)] = val

    def _deps(self, reads, writes):
        deps = []
        for b in reads:
            if b.last_w:
                deps.append(b.last_w)
        for b in writes:
            if b.last_w:
                deps.append(b.last_w)
            deps.extend(b.readers.values())
        return deps

    def _record(self, ev, reads, writes):
        sem, val = ev
        for b in reads:
            if b.readers.get(id(sem), (None, 0))[1] < val:
                b.readers[id(sem)] = ev
        for b in writes:
            b.last_w = ev
            b.readers = {}

    def op(self, e, emit, reads=(), writes=()):
        E = self.engs[e]
        self._wait(E, self._deps(reads, writes))
        ins = emit(E.h)
        E.n += 1
        ins.then_inc(E.sem, 1)
        self._record((E.sem, E.n), reads, writes)
        return ins

    def dma(self, q, out, in_, reads=(), writes=(), key=None, **kw):
        E = self.engs[q]
        kb = key if key is not None else (writes[0] if writes else reads[0])
        if kb.dma_sem is None:
            kb.dma_sem = self.stack.enter_context(self.nc.semaphore("d_" + kb.name))
        deps = self._deps(reads, writes)
        if kb.dma_cnt:
            deps.append((kb.dma_sem, kb.dma_cnt))
        self._wait(E, deps)
        ins = E.h.dma_start(out=out, in_=in_, **kw)
        kb.dma_cnt += 16
        ins.then_inc(kb.dma_sem, 16)
        self._record((kb.dma_sem, kb.dma_cnt), reads, writes)
        return ins

    def barrier(self, skip=()):
        sp = self.engs["sp"]
        deps = [(E.sem, E.n) for E in self.engs.values() if E is not sp]
        deps += [(b.dma_sem, b.dma_cnt) for b in self.bufs if b.dma_sem is not None and b not in skip]
        self._wait(sp, deps)
        ins = sp.h.nop()
        sp.n += 1
        ins.then_inc(sp.sem, 1)
        for E in self.engs.values():
            if E is not sp:
                self._wait(E, [(sp.sem, sp.n)])
        for b in self.bufs:
            b.last_w = None
            b.readers = {}


class Tl:
    __slots__ = ("t", "b")

    def __init__(self, t, b):
        self.t, self.b = t, b


class Ring:
    def __init__(self, tiles):
        self.tiles, self.i = tiles, 0

    def next(self):
        t = self.tiles[self.i % len(self.tiles)]
        self.i += 1
        return t


def fap(t, off, dims):
    base = t[:]
    return bass.AP(tensor=base.tensor, offset=base.offset + off,
                   ap=[list(base.ap[0])] + [list(d) for d in dims])


def build_program(debug=False):
    nc = bass.Bass("TRN2", target_bir_lowering=False)

    def inp(name, shape, dt=F32):
        return nc.dram_tensor(name, shape, dt, kind="ExternalInput").ap()

    def scratch(name, shape, dt):
        return nc.dram_tensor(name, shape, dt, kind=("ExternalOutput" if debug else "Internal")).ap()

    x_d = inp("x", [T, D])
    ctx_d = inp("ctx", [CTXL, D])
    cvec_d = inp("cvec", [128, 16])
    wmod_d = inp("w_mod", [D, 3 * D])
    bmodT_d = inp("bmodT", [128, 24])
    bmodr_d = inp("bmod_row", [1, 3 * D])
    w4_d = inp("w4", [72, 128, 8, 128])
    binT_d = inp("binT", [128, 72])
    bv_d = inp("bv_row", [1, D])
    lbl_d = inp("lbl", [128, 32])
    nag_d = inp("nag", [128, 1])
    convw_d = inp("convw", [128, 40])
    convb_d = inp("convb", [128, 8])
    wr_d = inp("wr", [128, 2048])
    wi_d = inp("wi", [128, 2048])
    br_d = inp("br", [128, 16])
    bi_d = inp("bi", [128, 16])
    lam_d = inp("lam", [128, 16])
    pa_d = inp("pa", [128, 8, D])
    pb_d = inp("pb", [128, 8, D])
    wo_d = inp("wo", [128, 8, D])
    lng_d = inp("lng", [128, D])
    lnb_d = inp("lnb", [128, D])
    ident_d = inp("ident", [128, 128])
    maskf_d = inp("maskf", [128, 128])
    maskb_d = inp("maskb", [128, 128])
    out_d = nc.dram_tensor("out", [T_OWN, D], F32, kind="ExternalOutput").ap()

    WB = scratch("wb_s", [72, 128, 8, 128], BF16)
    PAB = scratch("pab_s", [128, 8, D], BF16)
    PBB = scratch("pbb_s", [128, 8, D], BF16)
    WOB = scratch("wob_s", [128, 8, D], BF16)
    XT = scratch("xt_s", [8, 128, T], F32)
    OP = scratch("op_s", [8, 128, T], F32)
    QDB = scratch("qdb_s", [8, 128, T], BF16)
    UB = scratch("ub_s", [8, 128, T // 128, 128], F32)
    HS = scratch("h_s", [8, 128, T], F32)
    if debug:
        DBG = nc.dram_tensor("dbg", [128, 4096], F32, kind="ExternalOutput").ap()

    with ExitStack() as pst:
        k = K(nc, pst)
        cnt = [0]

        def sb(stack, name, shape, dt):
            cnt[0] += 1
            nm = f"{name}_{cnt[0]}"
            return Tl(stack.enter_context(nc.sbuf_tensor(nm, shape, dt)), k.buf(nm))

        def ring(stack, name, shape, dt, n):
            return Ring([sb(stack, name, shape, dt) for _ in range(n)])

        ps = []
        for i in range(7):
            ps.append(Tl(pst.enter_context(nc.psum_tensor(f"ps{i}", [128, 512], F32)), k.buf(f"ps{i}")))
        trb = Tl(pst.enter_context(nc.psum_tensor("trb", [128, 1024], BF16)), k.buf("trb"))
        pj = Ring([ps[0], ps[1]])

        def act(out, in_, func, reads, writes, **kw):
            k.op("act", lambda e: e.activation(out=out, in_=in_, func=func, **kw), reads, writes)

        def tt(eng, out, in0, in1, op, reads, writes):
            k.op(eng, lambda e: e.tensor_tensor(out=out, in0=in0, in1=in1, op=op), reads, writes)

        def ts(eng, out, in0, s1, s2, op0, op1, reads, writes):
            if s2 is None:
                k.op(eng, lambda e: e.tensor_scalar(out=out, in0=in0, scalar1=s1, scalar2=None, op0=op0), reads, writes)
            else:
                k.op(eng, lambda e: e.tensor_scalar(out=out, in0=in0, scalar1=s1, scalar2=s2, op0=op0, op1=op1), reads, writes)

        def stt(out, in0, scalar, in1, op0, op1, reads, writes):
            k.op("dve", lambda e: e.scalar_tensor_tensor(out=out, in0=in0, scalar=scalar, in1=in1, op0=op0, op1=op1),
                 reads, writes)

        def cp(eng, out, in_, reads, writes):
            if eng == "act":
                k.op("act", lambda e: e.activation(out=out, in_=in_, func=AF.Identity), reads, writes)
            else:
                k.op(eng, lambda e: e.tensor_copy(out=out, in_=in_), reads, writes)

        def mm(out, lhsT, rhs, start, stop, reads, writes):
            k.op("pe", lambda e: e.matmul(out, lhsT=lhsT, rhs=rhs, start=start, stop=stop), reads, writes)

        def tr(out, in_, ident, reads, writes):
            k.op("pe", lambda e: e.transpose(out, in_, ident), reads, writes)

        ident32 = sb(pst, "ident32", [128, 128], F32)
        identb = sb(pst, "identb", [128, 128], BF16)
        ones32 = sb(pst, "ones32", [128, 128], F32)
        zeros32 = sb(pst, "zeros32", [128, 128], F32)
        onesb = sb(pst, "onesb", [1, 128], BF16)
        modx = sb(pst, "modx", [128, 24], F32)
        modc = sb(pst, "modc", [128, 24], F32)
        scp1x = sb(pst, "scp1x", [128, 8], F32)
        scp1c = sb(pst, "scp1c", [128, 8], F32)
        gt_bc = sb(pst, "gt_bc", [128, D], F32)
        lb = sb(pst, "lb", [128, 16], F32)
        oml = sb(pst, "oml", [128, 16], F32)
        binT = sb(pst, "binT", [128, 72], F32)
        bv_bf = sb(pst, "bv_bf", [1, D], BF16)
        nag = sb(pst, "nag", [128, 1], F32)
        convw = sb(pst, "convw", [128, 40], F32)
        convb = sb(pst, "convb", [128, 8], F32)
        br = sb(pst, "br", [128, 16], F32)
        bi = sb(pst, "bi", [128, 16], F32)
        cA = sb(pst, "cA", [128, 16], F32)
        hcA = sb(pst, "hcA", [128, 16], F32)
        hbr = sb(pst, "hbr", [128, 16], F32)
        hbi = sb(pst, "hbi", [128, 16], F32)
        fc0 = sb(pst, "fc0", [128, 16], F32)
        fc1 = sb(pst, "fc1", [128, 16], F32)
        hbinT = sb(pst, "hbinT", [128, 72], F32)
        Sb = [sb(pst, f"Sb{h}", [128, 128], F32) for h in range(8)]
        dec_b = sb(pst, "dec_b", [128, 8, T // 128], F32)
        hcf = sb(pst, "hcf", [128, 8], F32)
        hcb = sb(pst, "hcb", [128, 8], F32)
        dcast = k.buf("dcast")

        with ExitStack() as st:
            k.dma("sp", ident32.t[:], ident_d[:, :], writes=[ident32.b])
            k.op("dve", lambda e: e.memset(ones32.t[:], 1.0), writes=[ones32.b])
            k.op("dve", lambda e: e.memset(zeros32.t[:], 0.0), writes=[zeros32.b])
            k.op("dve", lambda e: e.memset(onesb.t[:], 1.0), writes=[onesb.b])
            cp("dve", identb.t[:], ident32.t[:], [ident32.b], [identb.b])

            cvec = sb(st, "cvec", [128, 16], F32)
            cs = sb(st, "cs", [128, 16], F32)
            lbl = sb(st, "lbl", [128, 32], F32)
            lam = sb(st, "lam", [128, 16], F32)
            bmodT = sb(st, "bmodT", [128, 24], F32)
            bmodr = sb(st, "bmodr", [1, 3 * D], F32)
            gt_row = sb(st, "gt_row", [1, D], F32)
            bv32 = sb(st, "bv32", [1, D], F32)
            wmod = sb(st, "wmod", [128, 8, 3 * D], F32)
            tmp16 = sb(st, "tmp16", [128, 16], F32)
            tmp16b = sb(st, "tmp16b", [128, 16], F32)
            for tl, src in ((cvec, cvec_d), (lbl, lbl_d), (lam, lam_d), (bmodT, bmodT_d), (bmodr, bmodr_d),
                            (binT, binT_d), (nag, nag_d), (convw, convw_d), (convb, convb_d), (br, br_d),
                            (bi, bi_d), (bv32, bv_d)):
                k.dma("sp", tl.t[:], src[:, :], writes=[tl.b])
            for kc in range(8):
                k.dma("sp" if kc % 2 == 0 else "act", wmod.t[:, kc, :], wmod_d[kc * 128:(kc + 1) * 128, :], writes=[wmod.b])
            late = []
            for g in (1, 2, 3, 5, 0, 4, 6, 7, 8):
                kb = k.buf(f"dcast{g}")
                k.dma("pool", WB[g * 8:(g + 1) * 8], w4_d[g * 8:(g + 1) * 8], key=kb, reads=[wmod.b])
                if g in (4, 6, 7, 8):
                    late.append(kb)
            for nm_, dst, src in (("dcpa", PAB, pa_d), ("dcpb", PBB, pb_d), ("dcwo", WOB, wo_d)):
                kb = k.buf(nm_)
                k.dma("pool", dst[:, :, :], src[:, :, :], key=kb, reads=[wmod.b])
                late.append(kb)
            act(cs.t[:], cvec.t[:], AF.Silu, [cvec.b], [cs.b])
            for oc in range(24):
                for kc in range(8):
                    mm(ps[0].t[:, 2 * oc:2 * oc + 2], wmod.t[:, kc, oc * 128:(oc + 1) * 128],
                       fap(cs.t, kc, [[8, 2]]), kc == 0, kc == 7, [wmod.b, cs.b], [ps[0].b])
            tt("dve", modx.t[:], fap(ps[0].t, 0, [[2, 24]]), bmodT.t[:], ALU.add, [ps[0].b, bmodT.b], [modx.b])
            tt("dve", modc.t[:], fap(ps[0].t, 1, [[2, 24]]), bmodT.t[:], ALU.add, [ps[0].b, bmodT.b], [modc.b])
            ts("dve", scp1x.t[:], modx.t[:, 8:16], 1.0, None, ALU.add, None, [modx.b], [scp1x.b])
            ts("dve", scp1c.t[:], modc.t[:, 8:16], 1.0, None, ALU.add, None, [modc.b], [scp1c.b])
            for half in range(2):
                pr = ps[1 + half]
                for kc in range(8):
                    mm(pr.t[0:1, :], cs.t[:, kc:kc + 1], wmod.t[:, kc, 2048 + half * 512:2048 + (half + 1) * 512],
                       kc == 0, kc == 7, [wmod.b, cs.b], [pr.b])
                tt("dve", gt_row.t[0:1, half * 512:(half + 1) * 512], pr.t[0:1, :],
                   bmodr.t[0:1, 2048 + half * 512:2048 + (half + 1) * 512], ALU.add, [pr.b, bmodr.b], [gt_row.b])
            for half in range(2):
                pr = ps[3 + half]
                mm(pr.t[:, :], ones32.t[0:1, :], gt_row.t[0:1, half * 512:(half + 1) * 512], True, True,
                   [ones32.b, gt_row.b], [pr.b])
                act(gt_bc.t[:, half * 512:(half + 1) * 512], pr.t[:, :], AF.Identity, [pr.b], [gt_bc.b], scale=0.5)
            tt("dve", tmp16.t[:], lbl.t[:, 0:16], lbl.t[:, 16:32], ALU.subtract, [lbl.b], [tmp16.b])
            act(lb.t[:], tmp16.t[:], AF.Sigmoid, [tmp16.b], [lb.b])
            ts("dve", oml.t[:], lb.t[:], -1.0, 1.0, ALU.mult, ALU.add, [lb.b], [oml.b])
            act(tmp16.t[:], lam.t[:], AF.Exp, [lam.b], [tmp16.b], scale=-1.0)
            ts("dve", tmp16b.t[:], tmp16.t[:], 1.0 / 3.0, -0.5, ALU.mult, ALU.add, [tmp16.b], [tmp16b.b])
            tt("dve", tmp16b.t[:], tmp16b.t[:], tmp16.t[:], ALU.mult, [tmp16.b, tmp16b.b], [tmp16b.b])
            ts("dve", tmp16b.t[:], tmp16b.t[:], 1.0, None, ALU.add, None, [tmp16b.b], [tmp16b.b])
            tt("dve", tmp16b.t[:], tmp16b.t[:], tmp16.t[:], ALU.mult, [tmp16.b, tmp16b.b], [tmp16b.b])
            ts("dve", cA.t[:], tmp16b.t[:], -8.0, None, ALU.mult, None, [tmp16b.b], [cA.b])
            ts("dve", hcA.t[:], cA.t[:], 0.5, None, ALU.mult, None, [cA.b], [hcA.b])
            ts("dve", hbr.t[:], br.t[:], 0.5, None, ALU.mult, None, [br.b], [hbr.b])
            ts("dve", hbi.t[:], bi.t[:], 0.5, None, ALU.mult, None, [bi.b], [hbi.b])
            ts("dve", hbinT.t[:], binT.t[:], 0.5, None, ALU.mult, None, [binT.b], [hbinT.b])
            ts("dve", fc1.t[:], oml.t[:], 0.5, None, ALU.mult, None, [oml.b], [fc1.b])
            tt("dve", fc0.t[:], fc1.t[:], lb.t[:], ALU.add, [fc1.b, lb.b], [fc0.b])
            cp("dve", bv_bf.t[:], bv32.t[:], [bv32.b], [bv_bf.b])
            for h in range(8):
                k.op("dve", lambda e: e.memset(Sb[h].t[:], 0.0), writes=[Sb[h].b])
            k.barrier(skip=late)

        def make_uT(xtiles, uT, sh_t, scp1_t, nb_reads):
            n = len(xtiles) * 128
            for j in range(8):
                bank = pj.next()
                for i, xt in enumerate(xtiles):
                    tr(bank.t[:, i * 128:(i + 1) * 128], xt.t[:, j * 128:(j + 1) * 128], ident32.t[:],
                       [xt.b, ident32.b], [bank.b])
                act(uT.t[:, j, 0:n], bank.t[:, 0:n], AF.Identity, [bank.b] + nb_reads, [uT.b],
                    scale=scp1_t[:, j:j + 1], bias=sh_t[:, j:j + 1])

        def proj_fm(cb, uT, n, wring, evac):
            w = wring.next()
            k.dma("sp", w.t[:], WB[cb], writes=[w.b])
            bank = pj.next()
            for kc in range(8):
                mm(bank.t[:, 0:n], w.t[:, kc, :], uT.t[:, kc, 0:n], kc == 0, kc == 7, [w.b, uT.b], [bank.b])
            evac(bank)

        def v_proj(uT, nch, wv, vtok):
            for c in range(nch):
                for half in range(2):
                    bank = pj.next()
                    for kc in range(8):
                        mm(bank.t[:, :], uT.t[:, kc, c * 128:(c + 1) * 128],
                           fap(wv.t, half * 4 * 1024 + kc * 128, [[1024, 4], [1, 128]]), kc == 0, False,
                           [uT.b, wv.b], [bank.b])
                    mm(bank.t[:, :], onesb.t[0:1, :], bv_bf.t[0:1, half * 512:(half + 1) * 512], False, True,
                       [onesb.b, bv_bf.b], [bank.b])
                    cp("act", vtok.t[:, c, half * 512:(half + 1) * 512], bank.t[:, :], [bank.b], [vtok.b])

        def gla_local(f, P, rP, kin32, d1, dirn, nch, n):
            pos = 0 if dirn == 0 else 127
            cp("dve", fap(d1.t, pos, [[128, nch]]), fap(f.t, pos, [[128, nch]]), [f.b], [d1.b])
            if dirn == 0:
                o_ap, f_ap, d_ap = P.t[:, 0:n], f.t[:, 0:n], d1.t[:, 0:n]
            else:
                o_ap, f_ap, d_ap = (fap(P.t, n - 1, [[-1, n]]), fap(f.t, n - 1, [[-1, n]]), fap(d1.t, n - 1, [[-1, n]]))
            k.op("dve", lambda e: e.tensor_tensor_scan(out=o_ap, data0=f_ap, data1=d_ap, initial=1.0,
                                                       op0=ALU.mult, op1=ALU.max), [f.b, d1.b], [P.b])
            k.op("dve", lambda e: e.reciprocal(out=rP.t[:, 0:n], in_=P.t[:, 0:n]), [P.b], [rP.b])
            ts("pool", f.t[:, 0:n], f.t[:, 0:n], -1.0, 1.0, ALU.mult, ALU.add, [f.b], [f.b])
            tt("pool", kin32.t[:, 0:n], f.t[:, 0:n], rP.t[:, 0:n], ALU.mult, [f.b, rP.b], [kin32.b])

        def plast_bc(P, dirn, nch):
            return fap(P.t, 127 if dirn == 0 else 0, [[128, nch], [0, 128]])

        def plast(P, dirn, nch):
            return fap(P.t, 127 if dirn == 0 else 0, [[128, nch]])

        def f_evac(ftile, n, idx, cbidx, eng2):
            def ev(bank):
                act(ftile.t[:, 0:n], bank.t[:, 0:n], AF.Tanh, [bank.b, hbinT.b], [ftile.b],
                    scale=0.5, bias=hbinT.t[:, cbidx:cbidx + 1])
                ts(eng2, ftile.t[:, 0:n], ftile.t[:, 0:n], fc1.t[:, idx:idx + 1], fc0.t[:, idx:idx + 1],
                   ALU.mult, ALU.add, [ftile.b, fc1.b, fc0.b], [ftile.b])
            return ev

        def mixer_bufs(name, nsub):
            return {kk: [k.buf(f"{name}_{kk}{i}") for i in range(nsub)] for kk in ("xc", "xb", "A", "B0", "B1")} | {"raw": [k.buf(f"{name}_raw{i}") for i in range(4)]}

        def mixer_b(j, raw, xc, xcb, A, B, Tn, W, sub, tmp, diag, mb, h0f, h0b, fin, wr_bf, wi_bf, G):
            rows = Tn // W
            nr = sub // W
            nsub = Tn // sub

            def pm(t, s0):
                return fap(t, s0 // W, [[1, nr], [rows, W]])

            def rv(t, s0):
                return fap(t, s0, [[W, nr], [1, W]])

            for i in range(5):
                ts("pool", diag.t[:, i, :], ident32.t[:, :], convw.t[:, j * 5 + i:j * 5 + i + 1], None, ALU.mult, None,
                   [ident32.b, convw.b], [diag.b])
            csz = max(Tn // 4, 1)

            def rawb(lo_, hi_):
                return mb["raw"][max(lo_, 0) // csz:min((min(hi_, Tn) - 1) // csz, 3) + 1]

            def front(si):
                s0 = si * sub
                rb = rawb(s0 - 2 * W, s0 + sub + 2 * W)
                bank = pj.next()
                order = [2, 1, 3, 0]
                for n_, i in enumerate(order):
                    off = (i - 2) * W
                    lo, hi = max(s0, -off), min(s0 + sub, Tn - off)
                    mm(bank.t[:, lo - s0:hi - s0], diag.t[:, i, :], raw.t[:, lo + off:hi + off], n_ == 0, n_ == 3,
                       [diag.b] + rb, [bank.b])
                act(xc.t[:, s0:s0 + sub], bank.t[:, 0:sub], AF.Identity, [bank.b, convb.b], [mb["xc"][si]],
                    bias=convb.t[:, j:j + 1])
                for i in (4,):
                    off = (i - 2) * W
                    lo, hi = max(s0, -off), min(s0 + sub, Tn - off)
                    stt(xc.t[:, lo:hi], raw.t[:, lo + off:hi + off], convw.t[:, j * 5 + i:j * 5 + i + 1], xc.t[:, lo:hi],
                        ALU.mult, ALU.add, rb + [convw.b, mb["xc"][si]], [mb["xc"][si]])
                cp("pool", xcb.t[:, s0:s0 + sub], xc.t[:, s0:s0 + sub], [mb["xc"][si]], [mb["xb"][si]])

            for dirn in range(2):
                Bd = B if dirn == 0 else raw
                BS = mb["B0"] if dirn == 0 else mb["B1"]
                AS = mb["A"]
                gi = dirn * 8 + j
                def igate(si):
                    s0 = si * sub
                    pi = ps[4 + si % 2]
                    mm(pi.t[:, 0:sub], wi_bf.t[:, gi * 128:(gi + 1) * 128], xcb.t[:, s0:s0 + sub], True, True,
                       [wi_bf.b, mb["xb"][si]], [pi.b])
                    act(pm(Bd.t, s0), rv(pi.t, 0), AF.Tanh, [pi.b, hbi.b], [BS[si]] + (mb["raw"] if dirn == 1 else []),
                        scale=0.5, bias=hbi.t[:, gi:gi + 1])

                if dirn == 1:
                    for si in range(nsub):
                        igate(si)
                for g0 in range(0, nsub, G):
                    grp = list(range(g0, min(nsub, g0 + G)))
                    tms = {}
                    if dirn == 0:
                        for si in grp:
                            front(si)
                    for si in grp:
                        s0 = si * sub
                        sl = slice(s0, s0 + sub)
                        pr = ps[2 + si % 2]
                        mm(pr.t[:, 0:sub], wr_bf.t[:, gi * 128:(gi + 1) * 128], xcb.t[:, sl], True, True,
                           [wr_bf.b, mb["xb"][si]], [pr.b])
                        act(pm(A.t, s0), rv(pr.t, 0), AF.Tanh, [pr.b, hbr.b], [AS[si]], scale=0.5, bias=hbr.t[:, gi:gi + 1])
                        if dirn == 0:
                            igate(si)
                    for si in grp:
                        s0 = si * sub
                        t_m = tmp.next()
                        tms[si] = t_m
                        act(pm(A.t, s0), pm(A.t, s0), AF.Exp, [AS[si], hcA.b], [AS[si]], scale=hcA.t[:, gi:gi + 1],
                            bias=hcA.t[:, gi:gi + 1])
                        stt(rv(t_m.t, 0), pm(A.t, s0), 1.0, pm(A.t, s0), ALU.mult, ALU.mult, [AS[si]], [t_m.b])
                    for si in grp:
                        t_m = tms[si]
                        act(rv(t_m.t, 0), rv(t_m.t, 0), AF.Sqrt, [t_m.b], [t_m.b], scale=-0.25, bias=0.25)
                    for si in grp:
                        s0 = si * sub
                        t_m = tms[si]
                        stt(pm(Bd.t, s0), pm(Bd.t, s0), 1.0, rv(xc.t, s0), ALU.add, ALU.mult, [BS[si], mb["xc"][si]], [BS[si]])
                        tt("pool" if si % 3 else "dve", pm(Bd.t, s0), pm(Bd.t, s0), rv(t_m.t, 0), ALU.mult,
                           [BS[si], t_m.b], [BS[si]])
                if dirn == 0:
                    a_ap, b_ap = A.t[:, 0:Tn], Bd.t[:, 0:Tn]
                    init = h0f
                else:
                    a_ap, b_ap = fap(A.t, Tn - 1, [[-1, Tn]]), fap(Bd.t, Tn - 1, [[-1, Tn]])
                    init = h0b
                k.op("dve", lambda e: e.tensor_tensor_scan(out=b_ap, data0=a_ap, data1=b_ap, initial=init,
                                                           op0=ALU.mult, op1=ALU.add),
                     AS + BS + [hcf.b, hcb.b], BS)
                fin(dirn, Bd, BS)

        with ExitStack() as st:
            xring = ring(st, "xt", [128, D], F32, 6)
            uT_r = ring(st, "uT", [128, 8, NT], BF16, 2)
            wring = ring(st, "w", [128, 8, 128], BF16, 6)
            wv = sb(st, "wv", [128, 8, 8, 128], BF16)
            vtok = sb(st, "vtok", [128, NCH, D], BF16)
            qraw_r = ring(st, "qraw", [128, NT], F32, 2)
            ff_r = ring(st, "ff", [128, NT], F32, 2)
            fb_r = ring(st, "fb", [128, NT], F32, 2)
            P_r = ring(st, "P", [128, NT], F32, 4)
            rP_r = ring(st, "rP", [128, NT], F32, 2)
            kin_r = ring(st, "kin", [128, NT], F32, 2)
            kinv_r = ring(st, "kinv", [128, NT], BF16, 4)
            qdec_r = ring(st, "qdec", [128, NT], BF16, 4)
            kend_r = ring(st, "kend", [128, NT], BF16, 4)
            kendT_r = ring(st, "kendT", [128, 2 * NCH, 128], BF16, 2)
            d1_r = [ring(st, "d1f", [128, NT], F32, 2), ring(st, "d1b", [128, NT], F32, 2)]
            t1_r = ring(st, "t1", [128, NT], F32, 2)
            t2_r = ring(st, "t2", [128, NT], F32, 2)
            scs_r = ring(st, "scs", [128, NT], BF16, 2)
            decf_r = ring(st, "decf", [128, NCH], F32, 2)
            rpl_r = ring(st, "rpl", [128, NCH], F32, 4)
            sst_r = ring(st, "sst", [128, NCH, 128], BF16, 2)
            s32_r = ring(st, "s32", [128, NCH, 128], F32, 2)
            ost_r = ring(st, "ost", [128, NT], F32, 2)
            ubst_r = ring(st, "ubst", [128, NCH, 128], F32, 2)
            z5st_r = ring(st, "z5st", [128, NT], F32, 2)
            craw = sb(st, "craw", [128, 8, CTXL], F32)
            cxc = sb(st, "cxc", [128, CTXL], F32)
            cxcb = sb(st, "cxcb", [128, CTXL], BF16)
            cA_ = sb(st, "cA_", [128, CTXL], F32)
            cB_ = sb(st, "cB_", [128, CTXL], F32)
            crawj = sb(st, "crawj", [128, CTXL], F32)
            ctmp = ring(st, "ctmp", [128, CTXL], F32, 3)

            k.dma("sp", wv.t[:], WB[24:32].rearrange("cb p kc c -> p cb kc c"), writes=[wv.b])
            maskf4 = sb(st, "maskf4", [128, 4, 128], F32)
            maskb4 = sb(st, "maskb4", [128, 4, 128], F32)
            k.dma("sp", maskf4.t[:], bass.AP(tensor=maskf_d.tensor, offset=maskf_d.offset,
                                              ap=[[128, 128], [0, 4], [1, 128]]), writes=[maskf4.b])
            k.dma("sp", maskb4.t[:], bass.AP(tensor=maskb_d.tensor, offset=maskb_d.offset,
                                              ap=[[128, 128], [0, 4], [1, 128]]), writes=[maskb4.b])
            Sf = [sb(st, f"Sf{h}", [128, 128], F32) for h in range(8)]
            for h in range(8):
                k.op("pool", lambda e: e.memset(Sf[h].t[:], 0.0), writes=[Sf[h].b])
            wr_bf = sb(st, "wr_bf", [128, 2048], BF16)
            wi_bf = sb(st, "wi_bf", [128, 2048], BF16)
            k.dma("pool", wr_bf.t[:], wr_d[:, :], writes=[wr_bf.b])
            k.dma("pool", wi_bf.t[:], wi_d[:, :], writes=[wi_bf.b])
            for rg in d1_r:
                for tl in rg.tiles:
                    k.op("pool", lambda e: e.memset(tl.t[:], 0.0), writes=[tl.b])

            def startA(h, uT, n, want_q, dirs=(0, 1)):
                hd = {"h": h}
                parts = []
                if want_q:
                    qraw = qraw_r.next()
                    hd["qraw"] = qraw
                    parts.append(lambda: proj_fm(h, uT, n, wring, lambda bank: act(
                        qraw.t[:, 0:n], bank.t[:, 0:n], AF.Silu, [bank.b, binT.b], [qraw.b], bias=binT.t[:, h:h + 1])))
                hd["f"] = {}
                if 0 in dirs:
                    ff = ff_r.next()
                    hd["f"][0] = ff
                    parts.append(lambda: proj_fm(8 + h, uT, n, wring, f_evac(ff, n, h, 8 + h, "pool")))
                if 1 in dirs:
                    fbt = fb_r.next()
                    hd["f"][1] = fbt
                    parts.append(lambda: proj_fm(16 + h, uT, n, wring, f_evac(fbt, n, 8 + h, 16 + h, "pool")))
                hd["parts"] = parts
                return hd

            def stageB(hd, n, nch, want_out, dirs=(0, 1)):
                tl = {}

                def scan(out_t, f_t, d_t, fwd):
                    if fwd:
                        o_ap, f_ap, d_ap = out_t[:, 0:n], f_t[:, 0:n], d_t[:, 0:n]
                    else:
                        o_ap, f_ap, d_ap = (fap(out_t, n - 1, [[-1, n]]), fap(f_t, n - 1, [[-1, n]]), fap(d_t, n - 1, [[-1, n]]))
                    return lambda e: e.tensor_tensor_scan(out=o_ap, data0=f_ap, data1=d_ap, initial=1.0,
                                                          op0=ALU.mult, op1=ALU.max)

                for dirn in dirs:
                    f = hd["f"][dirn]
                    P, Q, k32 = P_r.next(), rP_r.next(), kin_r.next()
                    d1, d1o = d1_r[dirn].next(), d1_r[1 - dirn].next()
                    tl[dirn] = (f, P, Q, k32, d1, d1o)
                    hd[("P", dirn)] = P
                    pos = 0 if dirn == 0 else 127
                    cp("pool", fap(d1.t, pos, [[128, nch]]), fap(f.t, pos, [[128, nch]]), [f.b], [d1.b])
                    cp("pool", fap(d1o.t, 127 - pos, [[128, nch]]), fap(f.t, 127 - pos, [[128, nch]]), [f.b], [d1o.b])
                for dirn in dirs:
                    f, P, Q, k32, d1, d1o = tl[dirn]
                    k.op("dve", scan(P.t, f.t, d1.t, dirn == 0), [f.b, d1.b], [P.b])
                    k.op("dve", scan(Q.t, f.t, d1o.t, dirn == 1), [f.b, d1o.b], [Q.b])
                    ts("pool", f.t[:, 0:n], f.t[:, 0:n], -1.0, 1.0, ALU.mult, ALU.add, [f.b], [f.b])
                for dirn in dirs:
                    f, P, Q, k32, d1, d1o = tl[dirn]
                    rPl = rpl_r.next()
                    k.op("dve", lambda e: e.reciprocal(out=rPl.t[:, 0:nch], in_=plast(P, dirn, nch)), [P.b], [rPl.b])
                    tl[dirn] = tl[dirn] + (rPl,)
                    if want_out:
                        qdec = qdec_r.next()
                        qraw = hd["qraw"]
                        stt(qdec.t[:, 0:n], qraw.t[:, 0:n], QSCALE, P.t[:, 0:n], ALU.mult, ALU.mult,
                            [qraw.b, P.b], [qdec.b])
                        hd[("qdec", dirn)] = qdec
                for dirn in dirs:
                    f, P, Q, k32, d1, d1o, rPl = tl[dirn]
                    if dirn == 0:
                        tt("pool", k32.t[:, 0:n - 1], f.t[:, 0:n - 1], Q.t[:, 1:n], ALU.mult, [f.b, Q.b], [k32.b])
                        cp("pool", fap(k32.t, 127, [[128, nch]]), fap(f.t, 127, [[128, nch]]), [f.b], [k32.b])
                    else:
                        tt("pool", k32.t[:, 1:n], f.t[:, 1:n], Q.t[:, 0:n - 1], ALU.mult, [f.b, Q.b], [k32.b])
                        cp("pool", fap(k32.t, 0, [[128, nch]]), fap(f.t, 0, [[128, nch]]), [f.b], [k32.b])
                    kend = kend_r.next()
                    cp("act", kend.t[:, 0:n], k32.t[:, 0:n], [k32.b], [kend.b])
                    hd[("kend", dirn)] = kend
                    if want_out:
                        kinv = kinv_r.next()
                        tt("dve", fap(kinv.t, 0, [[128, nch], [1, 128]]), fap(k32.t, 0, [[128, nch], [1, 128]]),
                           fap(rPl.t, 0, [[1, nch], [0, 128]]), ALU.mult, [k32.b, rPl.b], [kinv.b])
                        hd[("kinv", dirn)] = kinv

            def pe1(hd, nch, want_out, dirs=(0, 1)):
                if want_out:
                    for dirn, bank in ((0, ps[2]), (1, ps[3])):
                        kinv, qdec = hd[("kinv", dirn)], hd[("qdec", dirn)]
                        for c in range(nch):
                            sl = slice(c * 128, (c + 1) * 128)
                            mm(bank.t[:, sl], kinv.t[:, sl], qdec.t[:, sl], True, True, [kinv.b, qdec.b], [bank.b])
                kendT = kendT_r.next()
                hd["kendT"] = kendT
                for dirn in dirs:
                    kend = hd[("kend", dirn)]
                    for c in range(nch):
                        tr(trb.t[:, (dirn * nch + c) * 128:(dirn * nch + c + 1) * 128], kend.t[:, c * 128:(c + 1) * 128],
                           identb.t[:], [kend.b, identb.b], [trb.b])
                lo, hi = min(dirs) * nch * 128, (max(dirs) + 1) * nch * 128
                cp("act", fap(kendT.t, lo, [[1, hi - lo]]), trb.t[:, lo:hi], [trb.b], [kendT.b])

            def pe2(hd, nch, dirs=(0, 1)):
                h, kendT = hd["h"], hd["kendT"]
                for dirn in dirs:
                    bank = ps[4 + dirn]
                    for c in range(nch):
                        mm(bank.t[:, c * 128:(c + 1) * 128], kendT.t[:, dirn * nch + c, :],
                           vtok.t[:, c, h * 128:(h + 1) * 128], True, True, [kendT.b, vtok.b], [bank.b])

            cx = [xring.next() for _ in range(2)]
            for i in range(2):
                k.dma("sp", cx[i].t[:], ctx_d[i * 128:(i + 1) * 128, :], writes=[cx[i].b])
            uT = uT_r.next()
            make_uT(cx, uT, modc.t, scp1c.t, [modc.b, scp1c.b])
            v_proj(uT, 2, wv, vtok)
            for h in range(8):
                hd = startA(h, uT, CTXL, False)
                for p in hd["parts"]:
                    p()
                stageB(hd, CTXL, 2, False)
                pe1(hd, 2, False)
                pe2(hd, 2)
                Pf, Pb = hd[("P", 0)], hd[("P", 1)]
                for c in range(2):
                    stt(Sf[h].t[:], Sf[h].t[:], fap(Pf.t, c * 128 + 127, [[1, 1]]), ps[4].t[:, c * 128:(c + 1) * 128],
                        ALU.mult, ALU.add, [Sf[h].b, Pf.b, ps[4].b], [Sf[h].b])
                for c in (1, 0):
                    stt(Sb[h].t[:], Sb[h].t[:], fap(Pb.t, c * 128, [[1, 1]]), ps[5].t[:, c * 128:(c + 1) * 128],
                        ALU.mult, ALU.add, [Sb[h].b, Pb.b, ps[5].b], [Sb[h].b])
            for j in range(8):
                proj_fm(40 + j, uT, CTXL, wring,
                        lambda bank: act(craw.t[:, j, :], bank.t[:, 0:CTXL], AF.Identity, [bank.b, binT.b], [craw.b],
                                         bias=binT.t[:, 40 + j:41 + j]))
            cmb = mixer_bufs("cmb", 1)
            cdiag = sb(st, "cdiag", [128, 5, 128], F32)
            for j in range(8):
                cp("pool", crawj.t[:], craw.t[:, j, :], [craw.b], cmb["raw"] + cmb["B1"])

                def fin_ctx(dirn, Bd, BS):
                    if dirn == 0:
                        cp("dve", hcf.t[:, j:j + 1], Bd.t[:, CTXL - 1:CTXL], BS, [hcf.b])
                    else:
                        cp("dve", hcb.t[:, j:j + 1], Bd.t[:, 0:1], BS, [hcb.b])
                mixer_b(j, crawj, cxc, cxcb, cA_, cB_, CTXL, 1, CTXL, ctmp, cdiag, cmb, 0.0, 0.0, fin_ctx, wr_bf, wi_bf, 1)

            def load_uT(stile):
                t0 = stile * NT
                xs = [xring.next() for _ in range(NCH)]
                for i in range(NCH):
                    k.dma("sp", xs[i].t[:], x_d[t0 + i * 128:t0 + (i + 1) * 128, :], writes=[xs[i].b])
                u = uT_r.next()
                make_uT(xs, u, modx.t, scp1x.t, [modx.b, scp1x.b])
                return u

            uT = load_uT(0)
            for stile in range(NST):
                t0 = stile * NT
                uT_next = None
                if stile >= NST_OWN:
                    hd = startA(0, uT, NT, False, (1,))
                    hd["parts"][0]()
                    v_proj(uT, NCH, wv, vtok)
                    for h in range(8):
                        nxt = startA(h + 1, uT, NT, False, (1,)) if h < 7 else None
                        stageB(hd, NT, NCH, False, (1,))
                        cp("dve", dec_b.t[:, h, stile * NCH:(stile + 1) * NCH], plast(hd[("P", 1)], 1, NCH),
                           [hd[("P", 1)].b], [dec_b.b])
                        if nxt:
                            nxt["parts"][0]()
                        pe1(hd, NCH, False, (1,))
                        z5 = z5st_r.next()
                        proj_fm(40 + h, uT, NT, wring,
                                lambda bank: act(z5.t[:, :], bank.t[:, :], AF.Identity, [bank.b, binT.b], [z5.b],
                                                 bias=binT.t[:, 40 + h:41 + h]))
                        k.dma("act", XT[h, :, t0:t0 + NT], z5.t[:], reads=[z5.b])
                        pe2(hd, NCH, (1,))
                        ubst = ubst_r.next()
                        cp("act", fap(ubst.t, 0, [[1, NT]]), ps[5].t[:, :], [ps[5].b], [ubst.b])
                        k.dma("act", UB[h, :, stile * NCH:(stile + 1) * NCH, :], ubst.t[:], reads=[ubst.b])
                        if h == 5 and stile + 1 < NST:
                            uT_next = load_uT(stile + 1)
                        hd = nxt
                    uT = uT_next
                    continue
                hd = startA(0, uT, NT, True)
                for p in hd["parts"]:
                    p()
                v_proj(uT, NCH, wv, vtok)
                for h in range(8):
                    nxt = startA(h + 1, uT, NT, True) if h < 7 else None
                    stageB(hd, NT, NCH, True)
                    Pf, Pb = hd[("P", 0)], hd[("P", 1)]
                    decf = decf_r.next()
                    cp("dve", decf.t[:, :], plast(Pf, 0, NCH), [Pf.b], [decf.b])
                    cp("dve", dec_b.t[:, h, stile * NCH:(stile + 1) * NCH], plast(Pb, 1, NCH), [Pb.b], [dec_b.b])
                    if nxt:
                        nxt["parts"][0]()
                        nxt["parts"][1]()
                    pe1(hd, NCH, True)
                    t1, t2, scs = t1_r.next(), t2_r.next(), scs_r.next()
                    tt("dve", t1.t[:, :], ps[2].t[:, :], fap(maskf4.t, 0, [[1, 512]]), ALU.mult, [ps[2].b, maskf4.b], [t1.b])
                    tt("dve", t2.t[:, :], ps[3].t[:, :], fap(maskb4.t, 0, [[1, 512]]), ALU.mult, [ps[3].b, maskb4.b], [t2.b])
                    tt("pool", scs.t[:, :], t1.t[:, :], t2.t[:, :], ALU.add, [t1.b, t2.b], [scs.b])
                    if nxt:
                        nxt["parts"][2]()
                    pe2(hd, NCH)
                    ubst = ubst_r.next()
                    cp("act", fap(ubst.t, 0, [[1, NT]]), ps[5].t[:, :], [ps[5].b], [ubst.b])
                    k.dma("act", UB[h, :, stile * NCH:(stile + 1) * NCH, :], ubst.t[:], reads=[ubst.b])
                    sst, s32 = sst_r.next(), s32_r.next()
                    cp("pool", sst.t[:, 0, :], Sf[h].t[:], [Sf[h].b], [sst.b])
                    for c in range(NCH):
                        src = Sf[h].t[:] if c == 0 else s32.t[:, c - 1, :]
                        dst = Sf[h].t[:] if c == NCH - 1 else s32.t[:, c, :]
                        stt(dst, src, decf.t[:, c:c + 1], ps[4].t[:, c * 128:(c + 1) * 128], ALU.mult, ALU.add,
                            [Sf[h].b, s32.b, decf.b, ps[4].b], [Sf[h].b] if c == NCH - 1 else [s32.b])
                    cp("pool", fap(sst.t, 128, [[1, (NCH - 1) * 128]]), fap(s32.t, 0, [[1, (NCH - 1) * 128]]), [s32.b], [sst.b])
                    z5 = z5st_r.next()
                    proj_fm(40 + h, uT, NT, wring,
                            lambda bank: act(z5.t[:, :], bank.t[:, :], AF.Identity, [bank.b, binT.b], [z5.b],
                                             bias=binT.t[:, 40 + h:41 + h]))
                    k.dma("act", XT[h, :, t0:t0 + NT], z5.t[:], reads=[z5.b])
                    if h == 5 and stile + 1 < NST:
                        uT_next = load_uT(stile + 1)
                    qdf = hd[("qdec", 0)]
                    for c in range(NCH):
                        sl = slice(c * 128, (c + 1) * 128)
                        mm(ps[6].t[:, sl], vtok.t[:, c, h * 128:(h + 1) * 128], scs.t[:, sl], True, False,
                           [vtok.b, scs.b], [ps[6].b])
                        mm(ps[6].t[:, sl], sst.t[:, c, :], qdf.t[:, sl], False, True, [sst.b, qdf.b], [ps[6].b])
                    ost = ost_r.next()
                    cp("act", ost.t[:, :], ps[6].t[:, :], [ps[6].b], [ost.b])
                    k.dma("act", OP[h, :, t0:t0 + NT], ost.t[:], reads=[ost.b])
                    qdb = hd[("qdec", 1)]
                    k.dma("sp", QDB[h, :, t0:t0 + NT], qdb.t[:], reads=[qdb.b])
                    hd = nxt
                uT = uT_next
            k.barrier()

        with ExitStack() as st:
            raw = sb(st, "raw", [128, T], F32)
            xc = sb(st, "xc", [128, T], F32)
            xcb = sb(st, "xcb", [128, T], BF16)
            A = sb(st, "A", [128, T], F32)
            B = sb(st, "B", [128, T], F32)
            tmp = ring(st, "mtmp", [128, 512], F32, 10)
            diag = sb(st, "diag", [128, 5, 128], F32)
            wr_bf = sb(st, "wr_bf", [128, 2048], BF16)
            wi_bf = sb(st, "wi_bf", [128, 2048], BF16)
            k.dma("pool", wr_bf.t[:], wr_d[:, :], writes=[wr_bf.b])
            k.dma("pool", wi_bf.t[:], wi_d[:, :], writes=[wi_bf.b])
            xmb = mixer_bufs("xmb", T // 512)
            for j in range(8):
                for q4 in range(4):
                    k.dma("sp" if q4 % 2 == 0 else "act", raw.t[:, q4 * 2048:(q4 + 1) * 2048], XT[j, :, q4 * 2048:(q4 + 1) * 2048],
                          writes=[xmb["raw"][q4]] + xmb["B1"])

                def fin_x(dirn, Bd, BS):
                    if dirn == 1:
                        for q4 in range(2):
                            r0 = q4 * (ROWS // 4)
                            cm = [[1, ROWS // 4], [ROWS, GW]]
                            tt("dve" if q4 == 0 else "pool", fap(xc.t, r0 * GW, [[GW, ROWS // 4], [1, GW]]), fap(B.t, r0, cm),
                               fap(Bd.t, r0, cm), ALU.add, xmb["B0"] + xmb["B1"], xmb["xc"][q4 * 4:(q4 + 1) * 4])
                        for q4 in range(2):
                            k.dma("sp", HS[j, :, q4 * 2048:(q4 + 1) * 2048], xc.t[:, q4 * 2048:(q4 + 1) * 2048],
                                  reads=xmb["xc"][q4 * 4:(q4 + 1) * 4], key=xc.b)
                mixer_b(j, raw, xc, xcb, A, B, T, GW, 512, tmp, diag, xmb, hcf.t[:, j:j + 1], hcb.t[:, j:j + 1], fin_x,
                        wr_bf, wi_bf, 8)
            k.barrier()

        with ExitStack() as st:
            xring = ring(st, "xt2", [128, D], F32, 8)
            uT = sb(st, "uT2", [128, 8, NT], BF16)
            wring = ring(st, "w2", [128, 8, 128], BF16, 4)
            pa_bf = sb(st, "pa_bf", [128, 8, D], BF16)
            pb_bf = sb(st, "pb_bf", [128, 8, D], BF16)
            wo_bf = sb(st, "wo_bf", [128, 8, D], BF16)
            lng = sb(st, "lng", [128, D], F32)
            lnb = sb(st, "lnb", [128, D], F32)
            op_r = ring(st, "opl", [128, NT], F32, 2)
            qdb_r = ring(st, "qdbl", [128, NT], BF16, 2)
            ub_r = ring(st, "ubl", [128, NCH, 128], F32, 2)
            sbs_r = ring(st, "sbs", [128, NCH, 128], BF16, 2)
            s32_r = ring(st, "s32b", [128, NCH, 128], F32, 1)
            o_r = ring(st, "o", [128, NT], F32, 2)
            sq_r = ring(st, "sq", [128, NT], F32, 2)
            rs_r = ring(st, "rs", [128, NT], F32, 2)
            g4_r = ring(st, "g4", [128, NT], F32, 2)
            oaT = sb(st, "oaT", [128, 8, NT], BF16)
            obT = sb(st, "obT", [128, 8, NT], BF16)
            yT = sb(st, "yT", [128, 8, NT], BF16)
            h_r = ring(st, "hl", [128, NT], F32, 2)
            g6_r = ring(st, "g6", [128, NT], F32, 2)
            s7_r = ring(st, "s7", [128, NT], F32, 2)
            s8_r = ring(st, "s8", [128, NT], F32, 2)
            ta_r = ring(st, "ta", [128, NT], F32, 2)
            tb_r = ring(st, "tb", [128, NT], F32, 2)
            r_r = ring(st, "r", [128, D], F32, 2)
            st6_r = ring(st, "st6", [128, 12], F32, 2)
            mv_r = ring(st, "mv", [128, 4], F32, 2)
            rms_eps_t = sb(st, "rms_eps", [128, 1], F32)
            ln_eps_t = sb(st, "ln_eps", [128, 1], F32)
            k.op("dve", lambda e: e.memset(rms_eps_t.t[:], RMS_EPS), writes=[rms_eps_t.b])
            k.op("dve", lambda e: e.memset(ln_eps_t.t[:], LN_EPS), writes=[ln_eps_t.b])
            k.dma("sp", pa_bf.t[:], PAB[:, :, :], writes=[pa_bf.b])
            k.dma("sp", pb_bf.t[:], PBB[:, :, :], writes=[pb_bf.b])
            k.dma("sp", wo_bf.t[:], WOB[:, :, :], writes=[wo_bf.b])
            k.dma("sp", lng.t[:], lng_d[:, :], writes=[lng.b])
            k.dma("sp", lnb.t[:], lnb_d[:, :], writes=[lnb.b])
            def start_tile2(stile_):
                t0_ = stile_ * NT
                xs_ = [xring.next() for _ in range(NCH)]
                for i in range(NCH):
                    k.dma("sp", xs_[i].t[:], x_d[t0_ + i * 128:t0_ + (i + 1) * 128, :], writes=[xs_[i].b])
                make_uT(xs_, uT, modx.t, scp1x.t, [modx.b, scp1x.b])
                return xs_

            xs_pref = None
            for stile in range(NST - 1, -1, -1):
                t0 = stile * NT
                if stile >= NST_OWN:
                    for h in range(8):
                        ubl = ub_r.next()
                        k.dma("sp", ubl.t[:], UB[h, :, stile * NCH:(stile + 1) * NCH, :], writes=[ubl.b])
                        for c in range(NCH - 1, -1, -1):
                            stt(Sb[h].t[:], Sb[h].t[:], dec_b.t[:, h, stile * NCH + c:stile * NCH + c + 1], ubl.t[:, c, :],
                                ALU.mult, ALU.add, [Sb[h].b, dec_b.b, ubl.b], [Sb[h].b])
                    continue
                if xs_pref is None:
                    xs_pref = start_tile2(stile)
                xs = xs_pref
                xs_pref = None
                def loads_rec(h):
                    opl, qdbl, ubl = op_r.next(), qdb_r.next(), ub_r.next()
                    k.dma("sp", opl.t[:], OP[h, :, t0:t0 + NT], writes=[opl.b])
                    k.dma("sp", qdbl.t[:], QDB[h, :, t0:t0 + NT], writes=[qdbl.b])
                    k.dma("sp", ubl.t[:], UB[h, :, stile * NCH:(stile + 1) * NCH, :], writes=[ubl.b])
                    sbs, s32 = sbs_r.next(), s32_r.next()
                    cp("pool", sbs.t[:, NCH - 1, :], Sb[h].t[:], [Sb[h].b], [sbs.b])
                    for c in range(NCH - 1, -1, -1):
                        src = Sb[h].t[:] if c == NCH - 1 else s32.t[:, c, :]
                        dst = Sb[h].t[:] if c == 0 else s32.t[:, c - 1, :]
                        stt(dst, src, dec_b.t[:, h, stile * NCH + c:stile * NCH + c + 1], ubl.t[:, c, :], ALU.mult, ALU.add,
                            [Sb[h].b, s32.b, dec_b.b, ubl.b], [Sb[h].b] if c == 0 else [s32.b])
                    cp("pool", fap(sbs.t, 0, [[1, (NCH - 1) * 128]]), fap(s32.t, 0, [[1, (NCH - 1) * 128]]), [s32.b], [sbs.b])
                    return opl, qdbl, sbs

                cur = loads_rec(0)
                for h in range(8):
                    opl, qdbl, sbs = cur
                    for c in range(NCH):
                        sl = slice(c * 128, (c + 1) * 128)
                        mm(ps[2].t[:, sl], sbs.t[:, c, :], qdbl.t[:, sl], True, True, [sbs.b, qdbl.b], [ps[2].b])
                    o = o_r.next()
                    tt("dve", o.t[:, :], ps[2].t[:, :], opl.t[:, :], ALU.add, [ps[2].b, opl.b], [o.b])
                    sq = sq_r.next()
                    act(sq.t[:, :], o.t[:, :], AF.Square, [o.b], [sq.b])
                    if h < 7:
                        cur = loads_rec(h + 1)
                    g4 = g4_r.next()
                    proj_fm(32 + h, uT, NT, wring, lambda bank: act(g4.t[:, :], bank.t[:, :], AF.Silu, [bank.b, binT.b],
                                                                     [g4.b], bias=binT.t[:, 32 + h:33 + h]))
                    hl, g6 = h_r.next(), g6_r.next()
                    k.dma("sp", hl.t[:], HS[h, :, t0:t0 + NT], writes=[hl.b])
                    proj_fm(48 + h, uT, NT, wring, lambda bank: act(g6.t[:, :], bank.t[:, :], AF.Silu, [bank.b, binT.b],
                                                                     [g6.b], bias=binT.t[:, 48 + h:49 + h]))
                    tt("pool", obT.t[:, h, :], hl.t[:, :], g6.t[:, :], ALU.mult, [hl.b, g6.b], [obT.b])
                    mm(ps[3].t[:, :], ones32.t[:, :], sq.t[:, :], True, True, [ones32.b, sq.b], [ps[3].b])
                    rs = rs_r.next()
                    act(rs.t[:, :], ps[3].t[:, :], AF.Ln, [ps[3].b], [rs.b], scale=1.0 / 128.0, bias=rms_eps_t.t[:, 0:1])
                    act(rs.t[:, :], rs.t[:, :], AF.Exp, [rs.b], [rs.b], scale=-0.5)
                    stt(o.t[:, :], o.t[:, :], nag.t[:, 0:1], rs.t[:, :], ALU.mult, ALU.mult, [o.b, nag.b, rs.b], [o.b])
                    tt("pool", oaT.t[:, h, :], o.t[:, :], g4.t[:, :], ALU.mult, [o.b, g4.b], [oaT.b])
                for j in range(8):
                    s7, s8 = s7_r.next(), s8_r.next()
                    proj_fm(56 + j, uT, NT, wring, lambda bank: act(s7.t[:, :], bank.t[:, :], AF.Tanh, [bank.b, hbinT.b],
                                                                     [s7.b], scale=0.5, bias=hbinT.t[:, 56 + j:57 + j]))
                    proj_fm(64 + j, uT, NT, wring, lambda bank: act(s8.t[:, :], bank.t[:, :], AF.Tanh, [bank.b, hbinT.b],
                                                                     [s8.b], scale=0.5, bias=hbinT.t[:, 64 + j:65 + j]))
                    ta, tb = ta_r.next(), tb_r.next()
                    for kc in range(8):
                        mm(ps[4].t[:, :], pa_bf.t[:, kc, j * 128:(j + 1) * 128], oaT.t[:, kc, :], kc == 0, kc == 7,
                           [pa_bf.b, oaT.b], [ps[4].b])
                    stt(ta.t[:, :], s7.t[:, :], 1.0, ps[4].t[:, :], ALU.add, ALU.mult, [ps[4].b, s7.b], [ta.b])
                    for kc in range(8):
                        mm(ps[5].t[:, :], pb_bf.t[:, kc, j * 128:(j + 1) * 128], obT.t[:, kc, :], kc == 0, kc == 7,
                           [pb_bf.b, obT.b], [ps[5].b])
                    stt(tb.t[:, :], s8.t[:, :], 1.0, ps[5].t[:, :], ALU.add, ALU.mult, [ps[5].b, s8.b], [tb.b])
                    tt("pool", yT.t[:, j, :], ta.t[:, :], tb.t[:, :], ALU.add, [ta.b, tb.b], [yT.b])
                if stile > 0:
                    xs_pref = start_tile2(stile - 1)
                for i in range(NCH):
                    r = r_r.next()
                    for half in range(2):
                        bank = ps[6] if half == 0 else ps[3]
                        hs = slice(half * 512, (half + 1) * 512)
                        for kc in range(8):
                            mm(bank.t[:, :], yT.t[:, kc, i * 128:(i + 1) * 128], wo_bf.t[:, kc, hs], kc == 0, kc == 7,
                               [yT.b, wo_bf.b], [bank.b])
                        tt("dve", r.t[:, hs], bank.t[:, :], gt_bc.t[:, hs], ALU.mult, [bank.b, gt_bc.b], [r.b])
                    stt(r.t[:, :], xs[i].t[:, :], ALPHA, r.t[:, :], ALU.mult, ALU.add, [xs[i].b, r.b], [r.b])
                    st6, mv = st6_r.next(), mv_r.next()
                    k.op("dve", lambda e: e.bn_stats(out=st6.t[:, 0:6], in_=r.t[:, 0:512]), [r.b], [st6.b])
                    k.op("dve", lambda e: e.bn_stats(out=st6.t[:, 6:12], in_=r.t[:, 512:1024]), [r.b], [st6.b])
                    k.op("dve", lambda e: e.bn_aggr(out=mv.t[:, 0:2], in_=st6.t[:, 0:12]), [st6.b], [mv.b])
                    act(mv.t[:, 2:3], mv.t[:, 1:2], AF.Ln, [mv.b], [mv.b], scale=1.0, bias=ln_eps_t.t[:, 0:1])
                    act(mv.t[:, 2:3], mv.t[:, 2:3], AF.Exp, [mv.b], [mv.b], scale=-0.5)
                    stt(mv.t[:, 3:4], mv.t[:, 0:1], -1.0, mv.t[:, 2:3], ALU.mult, ALU.mult, [mv.b], [mv.b])
                    act(r.t[:, :], r.t[:, :], AF.Identity, [r.b, mv.b], [r.b], scale=mv.t[:, 2:3], bias=mv.t[:, 3:4])
                    tt("pool", r.t[:, :], r.t[:, :], lng.t[:, :], ALU.mult, [r.b, lng.b], [r.b])
                    tt("pool", r.t[:, :], r.t[:, :], lnb.t[:, :], ALU.add, [r.b, lnb.b], [r.b])
                    k.dma("pool", out_d[t0 + i * 128:t0 + (i + 1) * 128, :], r.t[:, :], reads=[r.b])
            k.barrier()
        if debug:
            print("instr counts", {n: e.n for n, e in k.engs.items()}, "waits", k.nwaits)
    return nc


def _fm(v, n):
    return np.ascontiguousarray(np.asarray(v, np.float32).reshape(n, 128).T)


_CACHE = {}


def _shared_inputs(w_mod, b_mod, w_in, b_in, lb_logits, norm_a_g, conv_w, conv_b, w_r, b_r, w_i, b_i, lam,
                   p_a, p_b, w_out, ln_g, ln_b, flip):
    f = np.float32
    w_in0 = np.asarray(w_in, f)[0]
    w4 = w_in0.reshape(8, 128, 72, 128).transpose(2, 1, 0, 3)
    b_in0 = np.asarray(b_in, f)[0]
    binT = _fm(b_in0, 72)
    lbl = np.asarray(lb_logits, f).reshape(2, 2, 8, 128)
    wr = np.asarray(w_r, f)[0]
    wi = np.asarray(w_i, f)[0]
    br_ = np.asarray(b_r, f)[0]
    bi_ = np.asarray(b_i, f)[0]
    lam_ = np.asarray(lam, f)[0]
    cw = np.asarray(conv_w, f)[0]
    z = np.zeros_like(cw[0])
    if flip:
        order = list(range(0, 8)) + list(range(16, 24)) + list(range(8, 16)) + list(range(24, 72))
        w4 = w4[order]
        binT = binT[:, order]
        lbl = lbl[:, ::-1]
        wr, wi, br_, bi_, lam_ = wr[::-1], wi[::-1], br_[::-1], bi_[::-1], lam_[::-1]
        taps = np.stack([cw[3], cw[2], cw[1], cw[0], z], axis=0)
    else:
        taps = np.stack([z, cw[0], cw[1], cw[2], cw[3]], axis=0)
    return {
        "w_mod": np.ascontiguousarray(np.asarray(w_mod, f)[0]),
        "bmodT": _fm(np.asarray(b_mod, f)[0], 24),
        "bmod_row": np.ascontiguousarray(np.asarray(b_mod, f)[0][None, :]),
        "w4": np.ascontiguousarray(w4),
        "binT": np.ascontiguousarray(binT),
        "bv_row": np.ascontiguousarray(b_in0[None, 3072:4096]),
        "lbl": np.ascontiguousarray(lbl.transpose(3, 0, 1, 2).reshape(128, 32)),
        "nag": np.ascontiguousarray(np.asarray(norm_a_g, f)[0].reshape(128, 1)),
        "convw": np.ascontiguousarray(taps.reshape(5, 8, 128).transpose(2, 1, 0).reshape(128, 40)),
        "convb": _fm(np.asarray(conv_b, f)[0], 8),
        "wr": np.ascontiguousarray(wr.transpose(2, 0, 1, 3).reshape(128, 2048)),
        "wi": np.ascontiguousarray(wi.transpose(2, 0, 1, 3).reshape(128, 2048)),
        "br": _fm(np.ascontiguousarray(br_).reshape(-1), 16),
        "bi": _fm(np.ascontiguousarray(bi_).reshape(-1), 16),
        "lam": _fm(np.ascontiguousarray(lam_).reshape(-1), 16),
        "pa": np.ascontiguousarray(np.asarray(p_a, f)[0].reshape(8, 128, D).transpose(1, 0, 2)),
        "pb": np.ascontiguousarray(np.asarray(p_b, f)[0].reshape(8, 128, D).transpose(1, 0, 2)),
        "wo": np.ascontiguousarray(np.asarray(w_out, f)[0].reshape(8, 128, D).transpose(1, 0, 2)),
        "lng": np.ascontiguousarray(np.broadcast_to(np.asarray(ln_g, f)[0][None, :], (128, D))),
        "lnb": np.ascontiguousarray(np.broadcast_to(np.asarray(ln_b, f)[0][None, :], (128, D))),
        "ident": np.eye(128, dtype=f),
        "maskf": np.triu(np.ones((128, 128), f)),
        "maskb": np.tril(np.ones((128, 128), f)),
    }


def kernel(x, c, ctx, c_ctx, w_mod, b_mod, w_in, b_in, lb_logits, norm_a_g, conv_w, conv_b,
           w_r, b_r, w_i, b_i, lam, p_a, p_b, w_out, ln_g, ln_b):
    f = np.float32
    x = np.asarray(x, f); ctx = np.asarray(ctx, f); c = np.asarray(c, f); c_ctx = np.asarray(c_ctx, f)
    params = (w_mod, b_mod, w_in, b_in, lb_logits, norm_a_g, conv_w, conv_b, w_r, b_r, w_i, b_i, lam,
              p_a, p_b, w_out, ln_g, ln_b)
    shared = [_shared_inputs(*params, flip=False), _shared_inputs(*params, flip=True)]
    in_maps = []
    for core in range(8):
        b, half = core // 2, core % 2
        m = dict(shared[half])
        if half == 0:
            m["x"] = np.ascontiguousarray(x[b])
            m["ctx"] = np.ascontiguousarray(ctx[b])
        else:
            m["x"] = np.ascontiguousarray(x[b][::-1])
            m["ctx"] = np.ascontiguousarray(ctx[b][::-1])
        m["cvec"] = np.ascontiguousarray(np.concatenate([_fm(c[b], 8), _fm(c_ctx, 8)], axis=1))
        in_maps.append(m)
    debug = bool(os.environ.get("MK_DEBUG"))
    key = ("nc", debug)
    if key not in _CACHE:
        _CACHE[key] = build_program(debug)
    nc = _CACHE[key]
    res = run_bass_kernel_spmd(nc, in_maps, core_ids=list(range(8)))
    if debug:
        _CACHE["last"] = res
    out = np.empty((4, T, D), f)
    for b in range(4):
        out[b, :T_OWN] = np.asarray(res.results[2 * b]["out"], f)
        out[b, T_OWN:] = np.asarray(res.results[2 * b + 1]["out"], f)[::-1]
    return out
```

```python
import os
import numpy as np
from contextlib import ExitStack
import concourse.bass as bass
import concourse.mybir as mybir
from concourse.bass_utils import run_bass_kernel_spmd

F32 = mybir.dt.float32
BF16 = mybir.dt.bfloat16
ALU = mybir.AluOpType
AF = mybir.ActivationFunctionType

D = 1024
T = 8192
NT = 512
NST = T // NT
T_OWN = T // 2
NST_OWN = T_OWN // NT
NCH = NT // 128
CTXL = 256
GW = 64
ROWS = T // GW
QSCALE = 128 ** -0.5
ALPHA = 2.0 ** 0.25
LN_EPS = 1e-5
RMS_EPS = 1e-6


class Buf:
    __slots__ = ("name", "last_w", "readers", "dma_sem", "dma_cnt")

    def __init__(self, name):
        self.name = name
        self.last_w = None
        self.readers = {}
        self.dma_sem = None
        self.dma_cnt = 0


class Eng:
    def __init__(self, name, h, sem):
        self.name, self.h, self.sem, self.n = name, h, sem, 0
        self.seen = {}


class K:
    SAME_ENG_SYNC = True

    def __init__(self, nc, stack):
        self.nc = nc
        self.stack = stack
        self.engs = {}
        for name, h in (("pe", nc.tensor), ("act", nc.scalar), ("dve", nc.vector),
                        ("pool", nc.gpsimd), ("sp", nc.sync)):
            sem = stack.enter_context(nc.semaphore("s_" + name))
            self.engs[name] = Eng(name, h, sem)
        self.bufs = []
        self.nwaits = 0
        self.free_sems = []

    def buf(self, name):
        b = Buf(name)
        self.bufs.append(b)
        return b

    def _wait(self, E, deps):
        need = {}
        for sem, val in deps:
            if val <= 0:
                continue
            if need.get(id(sem), (None, 0))[1] < val:
                need[id(sem)] = (sem, val)
        for sem, val in need.values():
            if sem is E.sem:
                if E.name == "pe" or not self.SAME_ENG_SYNC:
                    continue
            if E.seen.get(id(sem), 0) >= val:
                continue
            E.h.wait_ge(sem, val)
            self.nwaits += 1
            E.seen[id(sem)] = val

    def _deps(self, reads, writes):
        deps = []
        for b in reads:
            if b.last_w:
                deps.append(b.last_w)
        for b in writes:
            if b.last_w:
                deps.append(b.last_w)
            deps.extend(b.readers.values())
        return deps

    def _record(self, ev, reads, writes):
        sem, val = ev
        for b in reads:
            if b.readers.get(id(sem), (None, 0))[1] < val:
                b.readers[id(sem)] = ev
        for b in writes:
            b.last_w = ev
            b.readers = {}

    def op(self, e, emit, reads=(), writes=()):
        E = self.engs[e]
        self._wait(E, self._deps(reads, writes))
        ins = emit(E.h)
        E.n += 1
        ins.then_inc(E.sem, 1)
        self._record((E.sem, E.n), reads, writes)
        return ins

    def dma(self, q, out, in_, reads=(), writes=(), key=None, **kw):
        E = self.engs[q]
        kb = key if key is not None else (writes[0] if writes else reads[0])
        if kb.dma_sem is None:
            kb.dma_sem = self.stack.enter_context(self.nc.semaphore("d_" + kb.name))
        deps = self._deps(reads, writes)
        if kb.dma_cnt:
            deps.append((kb.dma_sem, kb.dma_cnt))
        self._wait(E, deps)
        ins = E.h.dma_start(out=out, in_=in_, **kw)
        kb.dma_cnt += 16
        ins.then_inc(kb.dma_sem, 16)
        self._record((kb.dma_sem, kb.dma_cnt), reads, writes)
        return ins

    def barrier(self, skip=()):
        sp = self.engs["sp"]
        deps = [(E.sem, E.n) for E in self.engs.values() if E is not sp]
        deps += [(b.dma_sem, b.dma_cnt) for b in self.bufs if b.dma_sem is not None and b not in skip]
        self._wait(sp, deps)
        ins = sp.h.nop()
        sp.n += 1
        ins.then_inc(sp.sem, 1)
        for E in self.engs.values():
            if E is not sp:
                self._wait(E, [(sp.sem, sp.n)])
        for b in self.bufs:
            b.last_w = None
            b.readers = {}


class Tl:
    __slots__ = ("t", "b")

    def __init__(self, t, b):
        self.t, self.b = t, b


class Ring:
    def __init__(self, tiles):
        self.tiles, self.i = tiles, 0

    def next(self):
        t = self.tiles[self.i % len(self.tiles)]
        self.i += 1
        return t


def fap(t, off, dims):
    base = t[:]
    return bass.AP(tensor=base.tensor, offset=base.offset + off,
                   ap=[list(base.ap[0])] + [list(d) for d in dims])


def build_program(debug=False):
    nc = bass.Bass("TRN2", target_bir_lowering=False)

    def inp(name, shape, dt=F32):
        return nc.dram_tensor(name, shape, dt, kind="ExternalInput").ap()

    def scratch(name, shape, dt):
        return nc.dram_tensor(name, shape, dt, kind=("ExternalOutput" if debug else "Internal")).ap()

    x_d = inp("x", [T, D])
    ctx_d = inp("ctx", [CTXL, D])
    cvec_d = inp("cvec", [128, 16])
    wmod_d = inp("w_mod", [D, 3 * D])
    bmodT_d = inp("bmodT", [128, 24])
    bmodr_d = inp("bmod_row", [1, 3 * D])
    w4_d = inp("w4", [72, 128, 8, 128])
    binT_d = inp("binT", [128, 72])
    bv_d = inp("bv_row", [1, D])
    lbl_d = inp("lbl", [128, 32])
    nag_d = inp("nag", [128, 1])
    convw_d = inp("convw", [128, 40])
    convb_d = inp("convb", [128, 8])
    wr_d = inp("wr", [128, 2048])
    wi_d = inp("wi", [128, 2048])
    br_d = inp("br", [128, 16])
    bi_d = inp("bi", [128, 16])
    lam_d = inp("lam", [128, 16])
    pa_d = inp("pa", [128, 8, D])
    pb_d = inp("pb", [128, 8, D])
    wo_d = inp("wo", [128, 8, D])
    lng_d = inp("lng", [128, D])
    lnb_d = inp("lnb", [128, D])
    ident_d = inp("ident", [128, 128])
    maskf_d = inp("maskf", [128, 128])
    maskb_d = inp("maskb", [128, 128])
    out_d = nc.dram_tensor("out", [T_OWN, D], F32, kind="ExternalOutput").ap()

    WB = scratch("wb_s", [72, 128, 8, 128], BF16)
    PAB = scratch("pab_s", [128, 8, D], BF16)
    PBB = scratch("pbb_s", [128, 8, D], BF16)
    WOB = scratch("wob_s", [128, 8, D], BF16)
    XT = scratch("xt_s", [8, 128, T], F32)
    OP = scratch("op_s", [8, 128, T], F32)
    QDB = scratch("qdb_s", [8, 128, T], BF16)
    UB = scratch("ub_s", [8, 128, T // 128, 128], F32)
    HS = scratch("h_s", [8, 128, T], F32)
    if debug:
        DBG = nc.dram_tensor("dbg", [128, 4096], F32, kind="ExternalOutput").ap()

    with ExitStack() as pst:
        k = K(nc, pst)
        cnt = [0]

        def sb(stack, name, shape, dt):
            cnt[0] += 1
            nm = f"{name}_{cnt[0]}"
            return Tl(stack.enter_context(nc.sbuf_tensor(nm, shape, dt)), k.buf(nm))

        def ring(stack, name, shape, dt, n):
            return Ring([sb(stack, name, shape, dt) for _ in range(n)])

        ps = []
        for i in range(7):
            ps.append(Tl(pst.enter_context(nc.psum_tensor(f"ps{i}", [128, 512], F32)), k.buf(f"ps{i}")))
        trb = Tl(pst.enter_context(nc.psum_tensor("trb", [128, 1024], BF16)), k.buf("trb"))
        pj = Ring([ps[0], ps[1]])

        def act(out, in_, func, reads, writes, **kw):
            k.op("act", lambda e: e.activation(out=out, in_=in_, func=func, **kw), reads, writes)

        def tt(eng, out, in0, in1, op, reads, writes):
            k.op(eng, lambda e: e.tensor_tensor(out=out, in0=in0, in1=in1, op=op), reads, writes)

        def ts(eng, out, in0, s1, s2, op0, op1, reads, writes):
            if s2 is None:
                k.op(eng, lambda e: e.tensor_scalar(out=out, in0=in0, scalar1=s1, scalar2=None, op0=op0), reads, writes)
            else:
                k.op(eng, lambda e: e.tensor_scalar(out=out, in0=in0, scalar1=s1, scalar2=s2, op0=op0, op1=op1), reads, writes)

        def stt(out, in0, scalar, in1, op0, op1, reads, writes):
            k.op("dve", lambda e: e.scalar_tensor_tensor(out=out, in0=in0, scalar=scalar, in1=in1, op0=op0, op1=op1),
                 reads, writes)

        def cp(eng, out, in_, reads, writes):
            if eng == "act":
                k.op("act", lambda e: e.activation(out=out, in_=in_, func=AF.Identity), reads, writes)
            else:
                k.op(eng, lambda e: e.tensor_copy(out=out, in_=in_), reads, writes)

        def mm(out, lhsT, rhs, start, stop, reads, writes):
            k.op("pe", lambda e: e.matmul(out, lhsT=lhsT, rhs=rhs, start=start, stop=stop), reads, writes)

        def tr(out, in_, ident, reads, writes):
            k.op("pe", lambda e: e.transpose(out, in_, ident), reads, writes)

        ident32 = sb(pst, "ident32", [128, 128], F32)
        identb = sb(pst, "identb", [128, 128], BF16)
        ones32 = sb(pst, "ones32", [128, 128], F32)
        zeros32 = sb(pst, "zeros32", [128, 128], F32)
        onesb = sb(pst, "onesb", [1, 128], BF16)
        modx = sb(pst, "modx", [128, 24], F32)
        modc = sb(pst, "modc", [128, 24], F32)
        scp1x = sb(pst, "scp1x", [128, 8], F32)
        scp1c = sb(pst, "scp1c", [128, 8], F32)
        gt_bc = sb(pst, "gt_bc", [128, D], F32)
        lb = sb(pst, "lb", [128, 16], F32)
        oml = sb(pst, "oml", [128, 16], F32)
        binT = sb(pst, "binT", [128, 72], F32)
        bv_bf = sb(pst, "bv_bf", [1, D], BF16)
        nag = sb(pst, "nag", [128, 1], F32)
        convw = sb(pst, "convw", [128, 40], F32)
        convb = sb(pst, "convb", [128, 8], F32)
        br = sb(pst, "br", [128, 16], F32)
        bi = sb(pst, "bi", [128, 16], F32)
        cA = sb(pst, "cA", [128, 16], F32)
        hcA = sb(pst, "hcA", [128, 16], F32)
        hbr = sb(pst, "hbr", [128, 16], F32)
        hbi = sb(pst, "hbi", [128, 16], F32)
        fc0 = sb(pst, "fc0", [128, 16], F32)
        fc1 = sb(pst, "fc1", [128, 16], F32)
        hbinT = sb(pst, "hbinT", [128, 72], F32)
        Sb = [sb(pst, f"Sb{h}", [128, 128], F32) for h in range(8)]
        dec_b = sb(pst, "dec_b", [128, 8, T // 128], F32)
        hcf = sb(pst, "hcf", [128, 8], F32)
        hcb = sb(pst, "hcb", [128, 8], F32)
        dcast = k.buf("dcast")

        with ExitStack() as st:
            k.dma("sp", ident32.t[:], ident_d[:, :], writes=[ident32.b])
            k.op("dve", lambda e: e.memset(ones32.t[:], 1.0), writes=[ones32.b])
            k.op("dve", lambda e: e.memset(zeros32.t[:], 0.0), writes=[zeros32.b])
            k.op("dve", lambda e: e.memset(onesb.t[:], 1.0), writes=[onesb.b])
            cp("dve", identb.t[:], ident32.t[:], [ident32.b], [identb.b])

            cvec = sb(st, "cvec", [128, 16], F32)
            cs = sb(st, "cs", [128, 16], F32)
            lbl = sb(st, "lbl", [128, 32], F32)
            lam = sb(st, "lam", [128, 16], F32)
            bmodT = sb(st, "bmodT", [128, 24], F32)
            bmodr = sb(st, "bmodr", [1, 3 * D], F32)
            gt_row = sb(st, "gt_row", [1, D], F32)
            bv32 = sb(st, "bv32", [1, D], F32)
            wmod = sb(st, "wmod", [128, 8, 3 * D], F32)
            tmp16 = sb(st, "tmp16", [128, 16], F32)
            tmp16b = sb(st, "tmp16b", [128, 16], F32)
            for tl, src in ((cvec, cvec_d), (lbl, lbl_d), (lam, lam_d), (bmodT, bmodT_d), (bmodr, bmodr_d),
                            (binT, binT_d), (nag, nag_d), (convw, convw_d), (convb, convb_d), (br, br_d),
                            (bi, bi_d), (bv32, bv_d)):
                k.dma("sp", tl.t[:], src[:, :], writes=[tl.b])
            for kc in range(8):
                k.dma("sp" if kc % 2 == 0 else "act", wmod.t[:, kc, :], wmod_d[kc * 128:(kc + 1) * 128, :], writes=[wmod.b])
            late = []
            for g in (1, 2, 3, 5, 0, 4, 6, 7, 8):
                kb = k.buf(f"dcast{g}")
                k.dma("pool", WB[g * 8:(g + 1) * 8], w4_d[g * 8:(g + 1) * 8], key=kb, reads=[wmod.b])
                if g in (4, 6, 7, 8):
                    late.append(kb)
            for nm_, dst, src in (("dcpa", PAB, pa_d), ("dcpb", PBB, pb_d), ("dcwo", WOB, wo_d)):
                kb = k.buf(nm_)
                k.dma("pool", dst[:, :, :], src[:, :, :], key=kb, reads=[wmod.b])
                late.append(kb)
            act(cs.t[:], cvec.t[:], AF.Silu, [cvec.b], [cs.b])
            for oc in range(24):
                for kc in range(8):
                    mm(ps[0].t[:, 2 * oc:2 * oc + 2], wmod.t[:, kc, oc * 128:(oc + 1) * 128],
                       fap(cs.t, kc, [[8, 2]]), kc == 0, kc == 7, [wmod.b, cs.b], [ps[0].b])
            tt("dve", modx.t[:], fap(ps[0].t, 0, [[2, 24]]), bmodT.t[:], ALU.add, [ps[0].b, bmodT.b], [modx.b])
            tt("dve", modc.t[:], fap(ps[0].t, 1, [[2, 24]]), bmodT.t[:], ALU.add, [ps[0].b, bmodT.b], [modc.b])
            ts("dve", scp1x.t[:], modx.t[:, 8:16], 1.0, None, ALU.add, None, [modx.b], [scp1x.b])
            ts("dve", scp1c.t[:], modc.t[:, 8:16], 1.0, None, ALU.add, None, [modc.b], [scp1c.b])
            for half in range(2):
                pr = ps[1 + half]
                for kc in range(8):
                    mm(pr.t[0:1, :], cs.t[:, kc:kc + 1], wmod.t[:, kc, 2048 + half * 512:2048 + (half + 1) * 512],
                       kc == 0, kc == 7, [wmod.b, cs.b], [pr.b])
                tt("dve", gt_row.t[0:1, half * 512:(half + 1) * 512], pr.t[0:1, :],
                   bmodr.t[0:1, 2048 + half * 512:2048 + (half + 1) * 512], ALU.add, [pr.b, bmodr.b], [gt_row.b])
            for half in range(2):
                pr = ps[3 + half]
                mm(pr.t[:, :], ones32.t[0:1, :], gt_row.t[0:1, half * 512:(half + 1) * 512], True, True,
                   [ones32.b, gt_row.b], [pr.b])
                act(gt_bc.t[:, half * 512:(half + 1) * 512], pr.t[:, :], AF.Identity, [pr.b], [gt_bc.b], scale=0.5)
            tt("dve", tmp16.t[:], lbl.t[:, 0:16], lbl.t[:, 16:32], ALU.subtract, [lbl.b], [tmp16.b])
            act(lb.t[:], tmp16.t[:], AF.Sigmoid, [tmp16.b], [lb.b])
            ts("dve", oml.t[:], lb.t[:], -1.0, 1.0, ALU.mult, ALU.add, [lb.b], [oml.b])
            act(tmp16.t[:], lam.t[:], AF.Exp, [lam.b], [tmp16.b], scale=-1.0)
            ts("dve", tmp16b.t[:], tmp16.t[:], 1.0 / 3.0, -0.5, ALU.mult, ALU.add, [tmp16.b], [tmp16b.b])
            tt("dve", tmp16b.t[:], tmp16b.t[:], tmp16.t[:], ALU.mult, [tmp16.b, tmp16b.b], [tmp16b.b])
            ts("dve", tmp16b.t[:], tmp16b.t[:], 1.0, None, ALU.add, None, [tmp16b.b], [tmp16b.b])
            tt("dve", tmp16b.t[:], tmp16b.t[:], tmp16.t[:], ALU.mult, [tmp16.b, tmp16b.b], [tmp16b.b])
            ts("dve", cA.t[:], tmp16b.t[:], -8.0, None, ALU.mult, None, [tmp16b.b], [cA.b])
            ts("dve", hcA.t[:], cA.t[:], 0.5, None, ALU.mult, None, [cA.b], [hcA.b])
            ts("dve", hbr.t[:], br.t[:], 0.5, None, ALU.mult, None, [br.b], [hbr.b])
            ts("dve", hbi.t[:], bi.t[:], 0.5, None, ALU.mult, None, [bi.b], [hbi.b])
            ts("dve", hbinT.t[:], binT.t[:], 0.5, None, ALU.mult, None, [binT.b], [hbinT.b])
            ts("dve", fc1.t[:], oml.t[:], 0.5, None, ALU.mult, None, [oml.b], [fc1.b])
            tt("dve", fc0.t[:], fc1.t[:], lb.t[:], ALU.add, [fc1.b, lb.b], [fc0.b])
            cp("dve", bv_bf.t[:], bv32.t[:], [bv32.b], [bv_bf.b])
            for h in range(8):
                k.op("dve", lambda e: e.memset(Sb[h].t[:], 0.0), writes=[Sb[h].b])
            k.barrier(skip=late)

        def make_uT(xtiles, uT, sh_t, scp1_t, nb_reads):
            n = len(xtiles) * 128
            for j in range(8):
                bank = pj.next()
                for i, xt in enumerate(xtiles):
                    tr(bank.t[:, i * 128:(i + 1) * 128], xt.t[:, j * 128:(j + 1) * 128], ident32.t[:],
                       [xt.b, ident32.b], [bank.b])
                act(uT.t[:, j, 0:n], bank.t[:, 0:n], AF.Identity, [bank.b] + nb_reads, [uT.b],
                    scale=scp1_t[:, j:j + 1], bias=sh_t[:, j:j + 1])

        def proj_fm(cb, uT, n, wring, evac):
            w = wring.next()
            k.dma("sp", w.t[:], WB[cb], writes=[w.b])
            bank = pj.next()
            for kc in range(8):
                mm(bank.t[:, 0:n], w.t[:, kc, :], uT.t[:, kc, 0:n], kc == 0, kc == 7, [w.b, uT.b], [bank.b])
            evac(bank)

        def v_proj(uT, nch, wv, vtok):
            for c in range(nch):
                for half in range(2):
                    bank = pj.next()
                    for kc in range(8):
                        mm(bank.t[:, :], uT.t[:, kc, c * 128:(c + 1) * 128],
                           fap(wv.t, half * 4 * 1024 + kc * 128, [[1024, 4], [1, 128]]), kc == 0, False,
                           [uT.b, wv.b], [bank.b])
                    mm(bank.t[:, :], onesb.t[0:1, :], bv_bf.t[0:1, half * 512:(half + 1) * 512], False, True,
                       [onesb.b, bv_bf.b], [bank.b])
                    cp("act", vtok.t[:, c, half * 512:(half + 1) * 512], bank.t[:, :], [bank.b], [vtok.b])

        def gla_local(f, P, rP, kin32, d1, dirn, nch, n):
            pos = 0 if dirn == 0 else 127
            cp("dve", fap(d1.t, pos, [[128, nch]]), fap(f.t, pos, [[128, nch]]), [f.b], [d1.b])
            if dirn == 0:
                o_ap, f_ap, d_ap = P.t[:, 0:n], f.t[:, 0:n], d1.t[:, 0:n]
            else:
                o_ap, f_ap, d_ap = (fap(P.t, n - 1, [[-1, n]]), fap(f.t, n - 1, [[-1, n]]), fap(d1.t, n - 1, [[-1, n]]))
            k.op("dve", lambda e: e.tensor_tensor_scan(out=o_ap, data0=f_ap, data1=d_ap, initial=1.0,
                                                       op0=ALU.mult, op1=ALU.max), [f.b, d1.b], [P.b])
            k.op("dve", lambda e: e.reciprocal(out=rP.t[:, 0:n], in_=P.t[:, 0:n]), [P.b], [rP.b])
            ts("pool", f.t[:, 0:n], f.t[:, 0:n], -1.0, 1.0, ALU.mult, ALU.add, [f.b], [f.b])
            tt("pool", kin32.t[:, 0:n], f.t[:, 0:n], rP.t[:, 0:n], ALU.mult, [f.b, rP.b], [kin32.b])

        def plast_bc(P, dirn, nch):
            return fap(P.t, 127 if dirn == 0 else 0, [[128, nch], [0, 128]])

        def plast(P, dirn, nch):
            return fap(P.t, 127 if dirn == 0 else 0, [[128, nch]])

        def f_evac(ftile, n, idx, cbidx, eng2):
            def ev(bank):
                act(ftile.t[:, 0:n], bank.t[:, 0:n], AF.Tanh, [bank.b, hbinT.b], [ftile.b],
                    scale=0.5, bias=hbinT.t[:, cbidx:cbidx + 1])
                ts(eng2, ftile.t[:, 0:n], ftile.t[:, 0:n], fc1.t[:, idx:idx + 1], fc0.t[:, idx:idx + 1],
                   ALU.mult, ALU.add, [ftile.b, fc1.b, fc0.b], [ftile.b])
            return ev

        def mixer_bufs(name, nsub):
            return {kk: [k.buf(f"{name}_{kk}{i}") for i in range(nsub)] for kk in ("xc", "xb", "A", "B0", "B1")} | {"raw": [k.buf(f"{name}_raw{i}") for i in range(4)]}

        def mixer_b(j, raw, xc, xcb, A, B, Tn, W, sub, tmp, diag, mb, h0f, h0b, fin, wr_bf, wi_bf, G):
            rows = Tn // W
            nr = sub // W
            nsub = Tn // sub

            def pm(t, s0):
                return fap(t, s0 // W, [[1, nr], [rows, W]])

            def rv(t, s0):
                return fap(t, s0, [[W, nr], [1, W]])

            for i in range(5):
                ts("pool", diag.t[:, i, :], ident32.t[:, :], convw.t[:, j * 5 + i:j * 5 + i + 1], None, ALU.mult, None,
                   [ident32.b, convw.b], [diag.b])
            csz = max(Tn // 4, 1)

            def rawb(lo_, hi_):
                return mb["raw"][max(lo_, 0) // csz:min((min(hi_, Tn) - 1) // csz, 3) + 1]

            def front(si):
                s0 = si * sub
                rb = rawb(s0 - 2 * W, s0 + sub + 2 * W)
                bank = pj.next()
                order = [2, 1, 3]
                for n_, i in enumerate(order):
                    off = (i - 2) * W
                    lo, hi = max(s0, -off), min(s0 + sub, Tn - off)
                    mm(bank.t[:, lo - s0:hi - s0], diag.t[:, i, :], raw.t[:, lo + off:hi + off], n_ == 0, n_ == 2,
                       [diag.b] + rb, [bank.b])
                act(xc.t[:, s0:s0 + sub], bank.t[:, 0:sub], AF.Identity, [bank.b, convb.b], [mb["xc"][si]],
                    bias=convb.t[:, j:j + 1])
                for i in (0, 4):
                    off = (i - 2) * W
                    lo, hi = max(s0, -off), min(s0 + sub, Tn - off)
                    stt(xc.t[:, lo:hi], raw.t[:, lo + off:hi + off], convw.t[:, j * 5 + i:j * 5 + i + 1], xc.t[:, lo:hi],
                        ALU.mult, ALU.add, rb + [convw.b, mb["xc"][si]], [mb["xc"][si]])
                cp("pool", xcb.t[:, s0:s0 + sub], xc.t[:, s0:s0 + sub], [mb["xc"][si]], [mb["xb"][si]])

            for dirn in range(2):
                Bd = B if dirn == 0 else raw
                BS = mb["B0"] if dirn == 0 else mb["B1"]
                AS = mb["A"]
                gi = dirn * 8 + j
                def igate(si):
                    s0 = si * sub
                    pi = ps[4 + si % 2]
                    mm(pi.t[:, 0:sub], wi_bf.t[:, gi * 128:(gi + 1) * 128], xcb.t[:, s0:s0 + sub], True, True,
                       [wi_bf.b, mb["xb"][si]], [pi.b])
                    act(pm(Bd.t, s0), rv(pi.t, 0), AF.Tanh, [pi.b, hbi.b], [BS[si]] + (mb["raw"] if dirn == 1 else []),
                        scale=0.5, bias=hbi.t[:, gi:gi + 1])

                if dirn == 1:
                    for si in range(nsub):
                        igate(si)
                for g0 in range(0, nsub, G):
                    grp = list(range(g0, min(nsub, g0 + G)))
                    tms = {}
                    if dirn == 0:
                        for si in grp:
                            front(si)
                    for si in grp:
                        s0 = si * sub
                        sl = slice(s0, s0 + sub)
                        pr = ps[2 + si % 2]
                        mm(pr.t[:, 0:sub], wr_bf.t[:, gi * 128:(gi + 1) * 128], xcb.t[:, sl], True, True,
                           [wr_bf.b, mb["xb"][si]], [pr.b])
                        act(pm(A.t, s0), rv(pr.t, 0), AF.Tanh, [pr.b, hbr.b], [AS[si]], scale=0.5, bias=hbr.t[:, gi:gi + 1])
                        if dirn == 0:
                            igate(si)
                    for si in grp:
                        s0 = si * sub
                        t_m = tmp.next()
                        tms[si] = t_m
                        act(pm(A.t, s0), pm(A.t, s0), AF.Exp, [AS[si], hcA.b], [AS[si]], scale=hcA.t[:, gi:gi + 1],
                            bias=hcA.t[:, gi:gi + 1])
                        stt(rv(t_m.t, 0), pm(A.t, s0), 1.0, pm(A.t, s0), ALU.mult, ALU.mult, [AS[si]], [t_m.b])
                    for si in grp:
                        t_m = tms[si]
                        act(rv(t_m.t, 0), rv(t_m.t, 0), AF.Sqrt, [t_m.b], [t_m.b], scale=-0.25, bias=0.25)
                    for si in grp:
                        s0 = si * sub
                        t_m = tms[si]
                        stt(pm(Bd.t, s0), pm(Bd.t, s0), 1.0, rv(xc.t, s0), ALU.add, ALU.mult, [BS[si], mb["xc"][si]], [BS[si]])
                        tt("pool" if si % 3 else "dve", pm(Bd.t, s0), pm(Bd.t, s0), rv(t_m.t, 0), ALU.mult,
                           [BS[si], t_m.b], [BS[si]])
                if dirn == 0:
                    a_ap, b_ap = A.t[:, 0:Tn], Bd.t[:, 0:Tn]
                    init = h0f
                else:
                    a_ap, b_ap = fap(A.t, Tn - 1, [[-1, Tn]]), fap(Bd.t, Tn - 1, [[-1, Tn]])
                    init = h0b
                k.op("dve", lambda e: e.tensor_tensor_scan(out=b_ap, data0=a_ap, data1=b_ap, initial=init,
                                                           op0=ALU.mult, op1=ALU.add),
                     AS + BS + [hcf.b, hcb.b], BS)
                fin(dirn, Bd, BS)

        with ExitStack() as st:
            xring = ring(st, "xt", [128, D], F32, 6)
            uT_r = ring(st, "uT", [128, 8, NT], BF16, 2)
            wring = ring(st, "w", [128, 8, 128], BF16, 6)
            wv = sb(st, "wv", [128, 8, 8, 128], BF16)
            vtok = sb(st, "vtok", [128, NCH, D], BF16)
            qraw_r = ring(st, "qraw", [128, NT], F32, 2)
            ff_r = ring(st, "ff", [128, NT], F32, 2)
            fb_r = ring(st, "fb", [128, NT], F32, 2)
            P_r = ring(st, "P", [128, NT], F32, 4)
            rP_r = ring(st, "rP", [128, NT], F32, 2)
            kin_r = ring(st, "kin", [128, NT], F32, 2)
            kinv_r = ring(st, "kinv", [128, NT], BF16, 4)
            qdec_r = ring(st, "qdec", [128, NT], BF16, 4)
            kend_r = ring(st, "kend", [128, NT], BF16, 4)
            kendT_r = ring(st, "kendT", [128, 2 * NCH, 128], BF16, 2)
            d1_r = [ring(st, "d1f", [128, NT], F32, 2), ring(st, "d1b", [128, NT], F32, 2)]
            t1_r = ring(st, "t1", [128, NT], F32, 2)
            t2_r = ring(st, "t2", [128, NT], F32, 2)
            scs_r = ring(st, "scs", [128, NT], BF16, 2)
            decf_r = ring(st, "decf", [128, NCH], F32, 2)
            rpl_r = ring(st, "rpl", [128, NCH], F32, 4)
            sst_r = ring(st, "sst", [128, NCH, 128], BF16, 2)
            s32_r = ring(st, "s32", [128, NCH, 128], F32, 2)
            ost_r = ring(st, "ost", [128, NT], F32, 2)
            ubst_r = ring(st, "ubst", [128, NCH, 128], F32, 2)
            z5st_r = ring(st, "z5st", [128, NT], F32, 2)
            craw = sb(st, "craw", [128, 8, CTXL], F32)
            cxc = sb(st, "cxc", [128, CTXL], F32)
            cxcb = sb(st, "cxcb", [128, CTXL], BF16)
            cA_ = sb(st, "cA_", [128, CTXL], F32)
            cB_ = sb(st, "cB_", [128, CTXL], F32)
            crawj = sb(st, "crawj", [128, CTXL], F32)
            ctmp = ring(st, "ctmp", [128, CTXL], F32, 3)

            k.dma("sp", wv.t[:], WB[24:32].rearrange("cb p kc c -> p cb kc c"), writes=[wv.b])
            maskf4 = sb(st, "maskf4", [128, 4, 128], F32)
            maskb4 = sb(st, "maskb4", [128, 4, 128], F32)
            k.dma("sp", maskf4.t[:], bass.AP(tensor=maskf_d.tensor, offset=maskf_d.offset,
                                              ap=[[128, 128], [0, 4], [1, 128]]), writes=[maskf4.b])
            k.dma("sp", maskb4.t[:], bass.AP(tensor=maskb_d.tensor, offset=maskb_d.offset,
                                              ap=[[128, 128], [0, 4], [1, 128]]), writes=[maskb4.b])
            Sf = [sb(st, f"Sf{h}", [128, 128], F32) for h in range(8)]
            for h in range(8):
                k.op("pool", lambda e: e.memset(Sf[h].t[:], 0.0), writes=[Sf[h].b])
            wr_bf = sb(st, "wr_bf", [128, 2048], BF16)
            wi_bf = sb(st, "wi_bf", [128, 2048], BF16)
            k.dma("pool", wr_bf.t[:], wr_d[:, :], writes=[wr_bf.b])
            k.dma("pool", wi_bf.t[:], wi_d[:, :], writes=[wi_bf.b])
            for rg in d1_r:
                for tl in rg.tiles:
                    k.op("pool", lambda e: e.memset(tl.t[:], 0.0), writes=[tl.b])

            def startA(h, uT, n, want_q, dirs=(0, 1)):
                hd = {"h": h}
                parts = []
                if want_q:
                    qraw = qraw_r.next()
                    hd["qraw"] = qraw
                    parts.append(lambda: proj_fm(h, uT, n, wring, lambda bank: act(
                        qraw.t[:, 0:n], bank.t[:, 0:n], AF.Silu, [bank.b, binT.b], [qraw.b], bias=binT.t[:, h:h + 1])))
                hd["f"] = {}
                if 0 in dirs:
                    ff = ff_r.next()
                    hd["f"][0] = ff
                    parts.append(lambda: proj_fm(8 + h, uT, n, wring, f_evac(ff, n, h, 8 + h, "pool")))
                if 1 in dirs:
                    fbt = fb_r.next()
                    hd["f"][1] = fbt
                    parts.append(lambda: proj_fm(16 + h, uT, n, wring, f_evac(fbt, n, 8 + h, 16 + h, "pool")))
                hd["parts"] = parts
                return hd

            def stageB(hd, n, nch, want_out, dirs=(0, 1)):
                tl = {}

                def scan(out_t, f_t, d_t, fwd):
                    if fwd:
                        o_ap, f_ap, d_ap = out_t[:, 0:n], f_t[:, 0:n], d_t[:, 0:n]
                    else:
                        o_ap, f_ap, d_ap = (fap(out_t, n - 1, [[-1, n]]), fap(f_t, n - 1, [[-1, n]]), fap(d_t, n - 1, [[-1, n]]))
                    return lambda e: e.tensor_tensor_scan(out=o_ap, data0=f_ap, data1=d_ap, initial=1.0,
                                                          op0=ALU.mult, op1=ALU.max)

                for dirn in dirs:
                    f = hd["f"][dirn]
                    P, Q, k32 = P_r.next(), rP_r.next(), kin_r.next()
                    d1, d1o = d1_r[dirn].next(), d1_r[1 - dirn].next()
                    tl[dirn] = (f, P, Q, k32, d1, d1o)
                    hd[("P", dirn)] = P
                    pos = 0 if dirn == 0 else 127
                    cp("pool", fap(d1.t, pos, [[128, nch]]), fap(f.t, pos, [[128, nch]]), [f.b], [d1.b])
                    cp("pool", fap(d1o.t, 127 - pos, [[128, nch]]), fap(f.t, 127 - pos, [[128, nch]]), [f.b], [d1o.b])
                for dirn in dirs:
                    f, P, Q, k32, d1, d1o = tl[dirn]
                    k.op("dve", scan(P.t, f.t, d1.t, dirn == 0), [f.b, d1.b], [P.b])
                    k.op("dve", scan(Q.t, f.t, d1o.t, dirn == 1), [f.b, d1o.b], [Q.b])
                    ts("pool", f.t[:, 0:n], f.t[:, 0:n], -1.0, 1.0, ALU.mult, ALU.add, [f.b], [f.b])
                for dirn in dirs:
                    f, P, Q, k32, d1, d1o = tl[dirn]
                    rPl = rpl_r.next()
                    k.op("dve", lambda e: e.reciprocal(out=rPl.t[:, 0:nch], in_=plast(P, dirn, nch)), [P.b], [rPl.b])
                    tl[dirn] = tl[dirn] + (rPl,)
                    if want_out:
                        qdec = qdec_r.next()
                        qraw = hd["qraw"]
                        stt(qdec.t[:, 0:n], qraw.t[:, 0:n], QSCALE, P.t[:, 0:n], ALU.mult, ALU.mult,
                            [qraw.b, P.b], [qdec.b])
                        hd[("qdec", dirn)] = qdec
                for dirn in dirs:
                    f, P, Q, k32, d1, d1o, rPl = tl[dirn]
                    if dirn == 0:
                        tt("pool", k32.t[:, 0:n - 1], f.t[:, 0:n - 1], Q.t[:, 1:n], ALU.mult, [f.b, Q.b], [k32.b])
                        cp("pool", fap(k32.t, 127, [[128, nch]]), fap(f.t, 127, [[128, nch]]), [f.b], [k32.b])
                    else:
                        tt("pool", k32.t[:, 1:n], f.t[:, 1:n], Q.t[:, 0:n - 1], ALU.mult, [f.b, Q.b], [k32.b])
                        cp("pool", fap(k32.t, 0, [[128, nch]]), fap(f.t, 0, [[128, nch]]), [f.b], [k32.b])
                    kend = kend_r.next()
                    cp("act", kend.t[:, 0:n], k32.t[:, 0:n], [k32.b], [kend.b])
                    hd[("kend", dirn)] = kend
                    if want_out:
                        kinv = kinv_r.next()
                        tt("dve", fap(kinv.t, 0, [[128, nch], [1, 128]]), fap(k32.t, 0, [[128, nch], [1, 128]]),
                           fap(rPl.t, 0, [[1, nch], [0, 128]]), ALU.mult, [k32.b, rPl.b], [kinv.b])
                        hd[("kinv", dirn)] = kinv

            def pe1(hd, nch, want_out, dirs=(0, 1)):
                if want_out:
                    for dirn, bank in ((0, ps[2]), (1, ps[3])):
                        kinv, qdec = hd[("kinv", dirn)], hd[("qdec", dirn)]
                        for c in range(nch):
                            sl = slice(c * 128, (c + 1) * 128)
                            mm(bank.t[:, sl], kinv.t[:, sl], qdec.t[:, sl], True, True, [kinv.b, qdec.b], [bank.b])
                kendT = kendT_r.next()
                hd["kendT"] = kendT
                for dirn in dirs:
                    kend = hd[("kend", dirn)]
                    for c in range(nch):
                        tr(trb.t[:, (dirn * nch + c) * 128:(dirn * nch + c + 1) * 128], kend.t[:, c * 128:(c + 1) * 128],
                           identb.t[:], [kend.b, identb.b], [trb.b])
                lo, hi = min(dirs) * nch * 128, (max(dirs) + 1) * nch * 128
                cp("act", fap(kendT.t, lo, [[1, hi - lo]]), trb.t[:, lo:hi], [trb.b], [kendT.b])

            def pe2(hd, nch, dirs=(0, 1)):
                h, kendT = hd["h"], hd["kendT"]
                for dirn in dirs:
                    bank = ps[4 + dirn]
                    for c in range(nch):
                        mm(bank.t[:, c * 128:(c + 1) * 128], kendT.t[:, dirn * nch + c, :],
                           vtok.t[:, c, h * 128:(h + 1) * 128], True, True, [kendT.b, vtok.b], [bank.b])

            cx = [xring.next() for _ in range(2)]
            for i in range(2):
                k.dma("sp", cx[i].t[:], ctx_d[i * 128:(i + 1) * 128, :], writes=[cx[i].b])
            uT = uT_r.next()
            make_uT(cx, uT, modc.t, scp1c.t, [modc.b, scp1c.b])
            v_proj(uT, 2, wv, vtok)
            for h in range(8):
                hd = startA(h, uT, CTXL, False)
                for p in hd["parts"]:
                    p()
                stageB(hd, CTXL, 2, False)
                pe1(hd, 2, False)
                pe2(hd, 2)
                Pf, Pb = hd[("P", 0)], hd[("P", 1)]
                for c in range(2):
                    stt(Sf[h].t[:], Sf[h].t[:], fap(Pf.t, c * 128 + 127, [[1, 1]]), ps[4].t[:, c * 128:(c + 1) * 128],
                        ALU.mult, ALU.add, [Sf[h].b, Pf.b, ps[4].b], [Sf[h].b])
                for c in (1, 0):
                    stt(Sb[h].t[:], Sb[h].t[:], fap(Pb.t, c * 128, [[1, 1]]), ps[5].t[:, c * 128:(c + 1) * 128],
                        ALU.mult, ALU.add, [Sb[h].b, Pb.b, ps[5].b], [Sb[h].b])
            for j in range(8):
                proj_fm(40 + j, uT, CTXL, wring,
                        lambda bank: act(craw.t[:, j, :], bank.t[:, 0:CTXL], AF.Identity, [bank.b, binT.b], [craw.b],
                                         bias=binT.t[:, 40 + j:41 + j]))
            cmb = mixer_bufs("cmb", 1)
            cdiag = sb(st, "cdiag", [128, 5, 128], F32)
            for j in range(8):
                cp("pool", crawj.t[:], craw.t[:, j, :], [craw.b], cmb["raw"] + cmb["B1"])

                def fin_ctx(dirn, Bd, BS):
                    if dirn == 0:
                        cp("dve", hcf.t[:, j:j + 1], Bd.t[:, CTXL - 1:CTXL], BS, [hcf.b])
                    else:
                        cp("dve", hcb.t[:, j:j + 1], Bd.t[:, 0:1], BS, [hcb.b])
                mixer_b(j, crawj, cxc, cxcb, cA_, cB_, CTXL, 1, CTXL, ctmp, cdiag, cmb, 0.0, 0.0, fin_ctx, wr_bf, wi_bf, 1)

            def load_uT(stile):
                t0 = stile * NT
                xs = [xring.next() for _ in range(NCH)]
                for i in range(NCH):
                    k.dma("sp", xs[i].t[:], x_d[t0 + i * 128:t0 + (i + 1) * 128, :], writes=[xs[i].b])
                u = uT_r.next()
                make_uT(xs, u, modx.t, scp1x.t, [modx.b, scp1x.b])
                return u

            uT = load_uT(0)
            for stile in range(NST):
                t0 = stile * NT
                uT_next = None
                if stile >= NST_OWN:
                    hd = startA(0, uT, NT, False, (1,))
                    hd["parts"][0]()
                    v_proj(uT, NCH, wv, vtok)
                    for h in range(8):
                        nxt = startA(h + 1, uT, NT, False, (1,)) if h < 7 else None
                        stageB(hd, NT, NCH, False, (1,))
                        cp("dve", dec_b.t[:, h, stile * NCH:(stile + 1) * NCH], plast(hd[("P", 1)], 1, NCH),
                           [hd[("P", 1)].b], [dec_b.b])
                        if nxt:
                            nxt["parts"][0]()
                        pe1(hd, NCH, False, (1,))
                        z5 = z5st_r.next()
                        proj_fm(40 + h, uT, NT, wring,
                                lambda bank: act(z5.t[:, :], bank.t[:, :], AF.Identity, [bank.b, binT.b], [z5.b],
                                                 bias=binT.t[:, 40 + h:41 + h]))
                        k.dma("act", XT[h, :, t0:t0 + NT], z5.t[:], reads=[z5.b])
                        pe2(hd, NCH, (1,))
                        ubst = ubst_r.next()
                        cp("act", fap(ubst.t, 0, [[1, NT]]), ps[5].t[:, :], [ps[5].b], [ubst.b])
                        k.dma("act", UB[h, :, stile * NCH:(stile + 1) * NCH, :], ubst.t[:], reads=[ubst.b])
                        if h == 5 and stile + 1 < NST:
                            uT_next = load_uT(stile + 1)
                        hd = nxt
                    uT = uT_next
                    continue
                hd = startA(0, uT, NT, True)
                for p in hd["parts"]:
                    p()
                v_proj(uT, NCH, wv, vtok)
                for h in range(8):
                    nxt = startA(h + 1, uT, NT, True) if h < 7 else None
                    stageB(hd, NT, NCH, True)
                    Pf, Pb = hd[("P", 0)], hd[("P", 1)]
                    decf = decf_r.next()
                    cp("dve", decf.t[:, :], plast(Pf, 0, NCH), [Pf.b], [decf.b])
                    cp("dve", dec_b.t[:, h, stile * NCH:(stile + 1) * NCH], plast(Pb, 1, NCH), [Pb.b], [dec_b.b])
                    if nxt:
                        nxt["parts"][0]()
                        nxt["parts"][1]()
                    pe1(hd, NCH, True)
                    t1, t2, scs = t1_r.next(), t2_r.next(), scs_r.next()
                    tt("dve", t1.t[:, :], ps[2].t[:, :], fap(maskf4.t, 0, [[1, 512]]), ALU.mult, [ps[2].b, maskf4.b], [t1.b])
                    tt("dve", t2.t[:, :], ps[3].t[:, :], fap(maskb4.t, 0, [[1, 512]]), ALU.mult, [ps[3].b, maskb4.b], [t2.b])
                    tt("dve", scs.t[:, :], t1.t[:, :], t2.t[:, :], ALU.add, [t1.b, t2.b], [scs.b])
                    if nxt:
                        nxt["parts"][2]()
                    pe2(hd, NCH)
                    ubst = ubst_r.next()
                    cp("act", fap(ubst.t, 0, [[1, NT]]), ps[5].t[:, :], [ps[5].b], [ubst.b])
                    k.dma("act", UB[h, :, stile * NCH:(stile + 1) * NCH, :], ubst.t[:], reads=[ubst.b])
                    sst, s32 = sst_r.next(), s32_r.next()
                    cp("pool", sst.t[:, 0, :], Sf[h].t[:], [Sf[h].b], [sst.b])
                    for c in range(NCH):
                        src = Sf[h].t[:] if c == 0 else s32.t[:, c - 1, :]
                        dst = Sf[h].t[:] if c == NCH - 1 else s32.t[:, c, :]
                        stt(dst, src, decf.t[:, c:c + 1], ps[4].t[:, c * 128:(c + 1) * 128], ALU.mult, ALU.add,
                            [Sf[h].b, s32.b, decf.b, ps[4].b], [Sf[h].b] if c == NCH - 1 else [s32.b])
                    cp("pool", fap(sst.t, 128, [[1, (NCH - 1) * 128]]), fap(s32.t, 0, [[1, (NCH - 1) * 128]]), [s32.b], [sst.b])
                    z5 = z5st_r.next()
                    proj_fm(40 + h, uT, NT, wring,
                            lambda bank: act(z5.t[:, :], bank.t[:, :], AF.Identity, [bank.b, binT.b], [z5.b],
                                             bias=binT.t[:, 40 + h:41 + h]))
                    k.dma("act", XT[h, :, t0:t0 + NT], z5.t[:], reads=[z5.b])
                    if h == 5 and stile + 1 < NST:
                        uT_next = load_uT(stile + 1)
                    qdf = hd[("qdec", 0)]
                    for c in range(NCH):
                        sl = slice(c * 128, (c + 1) * 128)
                        mm(ps[6].t[:, sl], vtok.t[:, c, h * 128:(h + 1) * 128], scs.t[:, sl], True, False,
                           [vtok.b, scs.b], [ps[6].b])
                        mm(ps[6].t[:, sl], sst.t[:, c, :], qdf.t[:, sl], False, True, [sst.b, qdf.b], [ps[6].b])
                    ost = ost_r.next()
                    cp("act", ost.t[:, :], ps[6].t[:, :], [ps[6].b], [ost.b])
                    k.dma("act", OP[h, :, t0:t0 + NT], ost.t[:], reads=[ost.b])
                    qdb = hd[("qdec", 1)]
                    k.dma("sp", QDB[h, :, t0:t0 + NT], qdb.t[:], reads=[qdb.b])
                    hd = nxt
                uT = uT_next
            k.barrier()

        with ExitStack() as st:
            raw = sb(st, "raw", [128, T], F32)
            xc = sb(st, "xc", [128, T], F32)
            xcb = sb(st, "xcb", [128, T], BF16)
            A = sb(st, "A", [128, T], F32)
            B = sb(st, "B", [128, T], F32)
            tmp = ring(st, "mtmp", [128, 512], F32, 10)
            diag = sb(st, "diag", [128, 5, 128], F32)
            wr_bf = sb(st, "wr_bf", [128, 2048], BF16)
            wi_bf = sb(st, "wi_bf", [128, 2048], BF16)
            k.dma("pool", wr_bf.t[:], wr_d[:, :], writes=[wr_bf.b])
            k.dma("pool", wi_bf.t[:], wi_d[:, :], writes=[wi_bf.b])
            xmb = mixer_bufs("xmb", T // 512)
            for j in range(8):
                for q4 in range(4):
                    k.dma("sp" if q4 % 2 == 0 else "act", raw.t[:, q4 * 2048:(q4 + 1) * 2048], XT[j, :, q4 * 2048:(q4 + 1) * 2048],
                          writes=[xmb["raw"][q4]] + xmb["B1"])

                def fin_x(dirn, Bd, BS):
                    if dirn == 1:
                        for q4 in range(2):
                            r0 = q4 * (ROWS // 4)
                            cm = [[1, ROWS // 4], [ROWS, GW]]
                            tt("dve" if q4 == 0 else "pool", fap(xc.t, r0 * GW, [[GW, ROWS // 4], [1, GW]]), fap(B.t, r0, cm),
                               fap(Bd.t, r0, cm), ALU.add, xmb["B0"] + xmb["B1"], xmb["xc"][q4 * 4:(q4 + 1) * 4])
                        for q4 in range(2):
                            k.dma("sp", HS[j, :, q4 * 2048:(q4 + 1) * 2048], xc.t[:, q4 * 2048:(q4 + 1) * 2048],
                                  reads=xmb["xc"][q4 * 4:(q4 + 1) * 4], key=xc.b)
                mixer_b(j, raw, xc, xcb, A, B, T, GW, 512, tmp, diag, xmb, hcf.t[:, j:j + 1], hcb.t[:, j:j + 1], fin_x,
                        wr_bf, wi_bf, 8)
            k.barrier()

        with ExitStack() as st:
            xring = ring(st, "xt2", [128, D], F32, 8)
            uT = sb(st, "uT2", [128, 8, NT], BF16)
            wring = ring(st, "w2", [128, 8, 128], BF16, 4)
            pa_bf = sb(st, "pa_bf", [128, 8, D], BF16)
            pb_bf = sb(st, "pb_bf", [128, 8, D], BF16)
            wo_bf = sb(st, "wo_bf", [128, 8, D], BF16)
            lng = sb(st, "lng", [128, D], F32)
            lnb = sb(st, "lnb", [128, D], F32)
            op_r = ring(st, "opl", [128, NT], F32, 2)
            qdb_r = ring(st, "qdbl", [128, NT], BF16, 2)
            ub_r = ring(st, "ubl", [128, NCH, 128], F32, 2)
            sbs_r = ring(st, "sbs", [128, NCH, 128], BF16, 2)
            s32_r = ring(st, "s32b", [128, NCH, 128], F32, 1)
            o_r = ring(st, "o", [128, NT], F32, 2)
            sq_r = ring(st, "sq", [128, NT], F32, 2)
            rs_r = ring(st, "rs", [128, NT], F32, 2)
            g4_r = ring(st, "g4", [128, NT], F32, 2)
            oaT = sb(st, "oaT", [128, 8, NT], BF16)
            obT = sb(st, "obT", [128, 8, NT], BF16)
            yT = sb(st, "yT", [128, 8, NT], BF16)
            h_r = ring(st, "hl", [128, NT], F32, 2)
            g6_r = ring(st, "g6", [128, NT], F32, 2)
            s7_r = ring(st, "s7", [128, NT], F32, 2)
            s8_r = ring(st, "s8", [128, NT], F32, 2)
            ta_r = ring(st, "ta", [128, NT], F32, 2)
            tb_r = ring(st, "tb", [128, NT], F32, 2)
            r_r = ring(st, "r", [128, D], F32, 2)
            st6_r = ring(st, "st6", [128, 12], F32, 2)
            mv_r = ring(st, "mv", [128, 4], F32, 2)
            rms_eps_t = sb(st, "rms_eps", [128, 1], F32)
            ln_eps_t = sb(st, "ln_eps", [128, 1], F32)
            k.op("dve", lambda e: e.memset(rms_eps_t.t[:], RMS_EPS), writes=[rms_eps_t.b])
            k.op("dve", lambda e: e.memset(ln_eps_t.t[:], LN_EPS), writes=[ln_eps_t.b])
            k.dma("sp", pa_bf.t[:], PAB[:, :, :], writes=[pa_bf.b])
            k.dma("sp", pb_bf.t[:], PBB[:, :, :], writes=[pb_bf.b])
            k.dma("sp", wo_bf.t[:], WOB[:, :, :], writes=[wo_bf.b])
            k.dma("sp", lng.t[:], lng_d[:, :], writes=[lng.b])
            k.dma("sp", lnb.t[:], lnb_d[:, :], writes=[lnb.b])
            def start_tile2(stile_):
                t0_ = stile_ * NT
                xs_ = [xring.next() for _ in range(NCH)]
                for i in range(NCH):
                    k.dma("sp", xs_[i].t[:], x_d[t0_ + i * 128:t0_ + (i + 1) * 128, :], writes=[xs_[i].b])
                make_uT(xs_, uT, modx.t, scp1x.t, [modx.b, scp1x.b])
                return xs_

            xs_pref = None
            for stile in range(NST - 1, -1, -1):
                t0 = stile * NT
                if stile >= NST_OWN:
                    for h in range(8):
                        ubl = ub_r.next()
                        k.dma("sp", ubl.t[:], UB[h, :, stile * NCH:(stile + 1) * NCH, :], writes=[ubl.b])
                        for c in range(NCH - 1, -1, -1):
                            stt(Sb[h].t[:], Sb[h].t[:], dec_b.t[:, h, stile * NCH + c:stile * NCH + c + 1], ubl.t[:, c, :],
                                ALU.mult, ALU.add, [Sb[h].b, dec_b.b, ubl.b], [Sb[h].b])
                    continue
                if xs_pref is None:
                    xs_pref = start_tile2(stile)
                xs = xs_pref
                xs_pref = None
                def loads_rec(h):
                    opl, qdbl, ubl = op_r.next(), qdb_r.next(), ub_r.next()
                    k.dma("sp", opl.t[:], OP[h, :, t0:t0 + NT], writes=[opl.b])
                    k.dma("sp", qdbl.t[:], QDB[h, :, t0:t0 + NT], writes=[qdbl.b])
                    k.dma("sp", ubl.t[:], UB[h, :, stile * NCH:(stile + 1) * NCH, :], writes=[ubl.b])
                    sbs, s32 = sbs_r.next(), s32_r.next()
                    cp("pool", sbs.t[:, NCH - 1, :], Sb[h].t[:], [Sb[h].b], [sbs.b])
                    for c in range(NCH - 1, -1, -1):
                        src = Sb[h].t[:] if c == NCH - 1 else s32.t[:, c, :]
                        dst = Sb[h].t[:] if c == 0 else s32.t[:, c - 1, :]
                        stt(dst, src, dec_b.t[:, h, stile * NCH + c:stile * NCH + c + 1], ubl.t[:, c, :], ALU.mult, ALU.add,
                            [Sb[h].b, s32.b, dec_b.b, ubl.b], [Sb[h].b] if c == 0 else [s32.b])
                    cp("pool", fap(sbs.t, 0, [[1, (NCH - 1) * 128]]), fap(s32.t, 0, [[1, (NCH - 1) * 128]]), [s32.b], [sbs.b])
                    return opl, qdbl, sbs

                cur = loads_rec(0)
                for h in range(8):
                    opl, qdbl, sbs = cur
                    for c in range(NCH):
                        sl = slice(c * 128, (c + 1) * 128)
                        mm(ps[2].t[:, sl], sbs.t[:, c, :], qdbl.t[:, sl], True, True, [sbs.b, qdbl.b], [ps[2].b])
                    o = o_r.next()
                    tt("dve", o.t[:, :], ps[2].t[:, :], opl.t[:, :], ALU.add, [ps[2].b, opl.b], [o.b])
                    sq = sq_r.next()
                    act(sq.t[:, :], o.t[:, :], AF.Square, [o.b], [sq.b])
                    if h < 7:
                        cur = loads_rec(h + 1)
                    g4 = g4_r.next()
                    proj_fm(32 + h, uT, NT, wring, lambda bank: act(g4.t[:, :], bank.t[:, :], AF.Silu, [bank.b, binT.b],
                                                                     [g4.b], bias=binT.t[:, 32 + h:33 + h]))
                    hl, g6 = h_r.next(), g6_r.next()
                    k.dma("sp", hl.t[:], HS[h, :, t0:t0 + NT], writes=[hl.b])
                    proj_fm(48 + h, uT, NT, wring, lambda bank: act(g6.t[:, :], bank.t[:, :], AF.Silu, [bank.b, binT.b],
                                                                     [g6.b], bias=binT.t[:, 48 + h:49 + h]))
                    tt("pool", obT.t[:, h, :], hl.t[:, :], g6.t[:, :], ALU.mult, [hl.b, g6.b], [obT.b])
                    mm(ps[3].t[:, :], ones32.t[:, :], sq.t[:, :], True, True, [ones32.b, sq.b], [ps[3].b])
                    rs = rs_r.next()
                    act(rs.t[:, :], ps[3].t[:, :], AF.Ln, [ps[3].b], [rs.b], scale=1.0 / 128.0, bias=rms_eps_t.t[:, 0:1])
                    act(rs.t[:, :], rs.t[:, :], AF.Exp, [rs.b], [rs.b], scale=-0.5)
                    stt(o.t[:, :], o.t[:, :], nag.t[:, 0:1], rs.t[:, :], ALU.mult, ALU.mult, [o.b, nag.b, rs.b], [o.b])
                    tt("pool", oaT.t[:, h, :], o.t[:, :], g4.t[:, :], ALU.mult, [o.b, g4.b], [oaT.b])
                for j in range(8):
                    s7, s8 = s7_r.next(), s8_r.next()
                    proj_fm(56 + j, uT, NT, wring, lambda bank: act(s7.t[:, :], bank.t[:, :], AF.Tanh, [bank.b, hbinT.b],
                                                                     [s7.b], scale=0.5, bias=hbinT.t[:, 56 + j:57 + j]))
                    proj_fm(64 + j, uT, NT, wring, lambda bank: act(s8.t[:, :], bank.t[:, :], AF.Tanh, [bank.b, hbinT.b],
                                                                     [s8.b], scale=0.5, bias=hbinT.t[:, 64 + j:65 + j]))
                    ta, tb = ta_r.next(), tb_r.next()
                    for kc in range(8):
                        mm(ps[4].t[:, :], pa_bf.t[:, kc, j * 128:(j + 1) * 128], oaT.t[:, kc, :], kc == 0, kc == 7,
                           [pa_bf.b, oaT.b], [ps[4].b])
                    stt(ta.t[:, :], s7.t[:, :], 1.0, ps[4].t[:, :], ALU.add, ALU.mult, [ps[4].b, s7.b], [ta.b])
                    for kc in range(8):
                        mm(ps[5].t[:, :], pb_bf.t[:, kc, j * 128:(j + 1) * 128], obT.t[:, kc, :], kc == 0, kc == 7,
                           [pb_bf.b, obT.b], [ps[5].b])
                    stt(tb.t[:, :], s8.t[:, :], 1.0, ps[5].t[:, :], ALU.add, ALU.mult, [ps[5].b, s8.b], [tb.b])
                    tt("pool", yT.t[:, j, :], ta.t[:, :], tb.t[:, :], ALU.add, [ta.b, tb.b], [yT.b])
                if stile > 0:
                    xs_pref = start_tile2(stile - 1)
                for i in range(NCH):
                    r = r_r.next()
                    for half in range(2):
                        bank = ps[6] if half == 0 else ps[3]
                        hs = slice(half * 512, (half + 1) * 512)
                        for kc in range(8):
                            mm(bank.t[:, :], yT.t[:, kc, i * 128:(i + 1) * 128], wo_bf.t[:, kc, hs], kc == 0, kc == 7,
                               [yT.b, wo_bf.b], [bank.b])
                        tt("dve", r.t[:, hs], bank.t[:, :], gt_bc.t[:, hs], ALU.mult, [bank.b, gt_bc.b], [r.b])
                    stt(r.t[:, :], xs[i].t[:, :], ALPHA, r.t[:, :], ALU.mult, ALU.add, [xs[i].b, r.b], [r.b])
                    st6, mv = st6_r.next(), mv_r.next()
                    k.op("dve", lambda e: e.bn_stats(out=st6.t[:, 0:6], in_=r.t[:, 0:512]), [r.b], [st6.b])
                    k.op("dve", lambda e: e.bn_stats(out=st6.t[:, 6:12], in_=r.t[:, 512:1024]), [r.b], [st6.b])
                    k.op("dve", lambda e: e.bn_aggr(out=mv.t[:, 0:2], in_=st6.t[:, 0:12]), [st6.b], [mv.b])
                    act(mv.t[:, 2:3], mv.t[:, 1:2], AF.Ln, [mv.b], [mv.b], scale=1.0, bias=ln_eps_t.t[:, 0:1])
                    act(mv.t[:, 2:3], mv.t[:, 2:3], AF.Exp, [mv.b], [mv.b], scale=-0.5)
                    stt(mv.t[:, 3:4], mv.t[:, 0:1], -1.0, mv.t[:, 2:3], ALU.mult, ALU.mult, [mv.b], [mv.b])
                    act(r.t[:, :], r.t[:, :], AF.Identity, [r.b, mv.b], [r.b], scale=mv.t[:, 2:3], bias=mv.t[:, 3:4])
                    tt("pool", r.t[:, :], r.t[:, :], lng.t[:, :], ALU.mult, [r.b, lng.b], [r.b])
                    tt("pool", r.t[:, :], r.t[:, :], lnb.t[:, :], ALU.add, [r.b, lnb.b], [r.b])
                    k.dma("pool", out_d[t0 + i * 128:t0 + (i + 1) * 128, :], r.t[:, :], reads=[r.b])
            k.barrier()
        if debug:
            print("instr counts", {n: e.n for n, e in k.engs.items()}, "waits", k.nwaits)
    return nc


def _fm(v, n):
    return np.ascontiguousarray(np.asarray(v, np.float32).reshape(n, 128).T)


_CACHE = {}


def _shared_inputs(w_mod, b_mod, w_in, b_in, lb_logits, norm_a_g, conv_w, conv_b, w_r, b_r, w_i, b_i, lam,
                   p_a, p_b, w_out, ln_g, ln_b, flip):
    f = np.float32
    w_in0 = np.asarray(w_in, f)[0]
    w4 = w_in0.reshape(8, 128, 72, 128).transpose(2, 1, 0, 3)
    b_in0 = np.asarray(b_in, f)[0]
    binT = _fm(b_in0, 72)
    lbl = np.asarray(lb_logits, f).reshape(2, 2, 8, 128)
    wr = np.asarray(w_r, f)[0]
    wi = np.asarray(w_i, f)[0]
    br_ = np.asarray(b_r, f)[0]
    bi_ = np.asarray(b_i, f)[0]
    lam_ = np.asarray(lam, f)[0]
    cw = np.asarray(conv_w, f)[0]
    z = np.zeros_like(cw[0])
    if flip:
        order = list(range(0, 8)) + list(range(16, 24)) + list(range(8, 16)) + list(range(24, 72))
        w4 = w4[order]
        binT = binT[:, order]
        lbl = lbl[:, ::-1]
        wr, wi, br_, bi_, lam_ = wr[::-1], wi[::-1], br_[::-1], bi_[::-1], lam_[::-1]
        taps = np.stack([cw[3], cw[2], cw[1], cw[0], z], axis=0)
    else:
        taps = np.stack([z, cw[0], cw[1], cw[2], cw[3]], axis=0)
    return {
        "w_mod": np.ascontiguousarray(np.asarray(w_mod, f)[0]),
        "bmodT": _fm(np.asarray(b_mod, f)[0], 24),
        "bmod_row": np.ascontiguousarray(np.asarray(b_mod, f)[0][None, :]),
        "w4": np.ascontiguousarray(w4),
        "binT": np.ascontiguousarray(binT),
        "bv_row": np.ascontiguousarray(b_in0[None, 3072:4096]),
        "lbl": np.ascontiguousarray(lbl.transpose(3, 0, 1, 2).reshape(128, 32)),
        "nag": np.ascontiguousarray(np.asarray(norm_a_g, f)[0].reshape(128, 1)),
        "convw": np.ascontiguousarray(taps.reshape(5, 8, 128).transpose(2, 1, 0).reshape(128, 40)),
        "convb": _fm(np.asarray(conv_b, f)[0], 8),
        "wr": np.ascontiguousarray(wr.transpose(2, 0, 1, 3).reshape(128, 2048)),
        "wi": np.ascontiguousarray(wi.transpose(2, 0, 1, 3).reshape(128, 2048)),
        "br": _fm(np.ascontiguousarray(br_).reshape(-1), 16),
        "bi": _fm(np.ascontiguousarray(bi_).reshape(-1), 16),
        "lam": _fm(np.ascontiguousarray(lam_).reshape(-1), 16),
        "pa": np.ascontiguousarray(np.asarray(p_a, f)[0].reshape(8, 128, D).transpose(1, 0, 2)),
        "pb": np.ascontiguousarray(np.asarray(p_b, f)[0].reshape(8, 128, D).transpose(1, 0, 2)),
        "wo": np.ascontiguousarray(np.asarray(w_out, f)[0].reshape(8, 128, D).transpose(1, 0, 2)),
        "lng": np.ascontiguousarray(np.broadcast_to(np.asarray(ln_g, f)[0][None, :], (128, D))),
        "lnb": np.ascontiguousarray(np.broadcast_to(np.asarray(ln_b, f)[0][None, :], (128, D))),
        "ident": np.eye(128, dtype=f),
        "maskf": np.triu(np.ones((128, 128), f)),
        "maskb": np.tril(np.ones((128, 128), f)),
    }


def kernel(x, c, ctx, c_ctx, w_mod, b_mod, w_in, b_in, lb_logits, norm_a_g, conv_w, conv_b,
           w_r, b_r, w_i, b_i, lam, p_a, p_b, w_out, ln_g, ln_b):
    f = np.float32
    x = np.asarray(x, f); ctx = np.asarray(ctx, f); c = np.asarray(c, f); c_ctx = np.asarray(c_ctx, f)
    params = (w_mod, b_mod, w_in, b_in, lb_logits, norm_a_g, conv_w, conv_b, w_r, b_r, w_i, b_i, lam,
              p_a, p_b, w_out, ln_g, ln_b)
    shared = [_shared_inputs(*params, flip=False), _shared_inputs(*params, flip=True)]
    in_maps = []
    for core in range(8):
        b, half = core // 2, core % 2
        m = dict(shared[half])
        if half == 0:
            m["x"] = np.ascontiguousarray(x[b])
            m["ctx"] = np.ascontiguousarray(ctx[b])
        else:
            m["x"] = np.ascontiguousarray(x[b][::-1])
            m["ctx"] = np.ascontiguousarray(ctx[b][::-1])
        m["cvec"] = np.ascontiguousarray(np.concatenate([_fm(c[b], 8), _fm(c_ctx, 8)], axis=1))
        in_maps.append(m)
    debug = bool(os.environ.get("MK_DEBUG"))
    key = ("nc", debug)
    if key not in _CACHE:
        _CACHE[key] = build_program(debug)
    nc = _CACHE[key]
    res = run_bass_kernel_spmd(nc, in_maps, core_ids=list(range(8)))
    if debug:
        _CACHE["last"] = res
    out = np.empty((4, T, D), f)
    for b in range(4):
        out[b, :T_OWN] = np.asarray(res.results[2 * b]["out"], f)
        out[b, T_OWN:] = np.asarray(res.results[2 * b + 1]["out"], f)[::-1]
    return out
```
